# Optimizing a Trainium2 kernel written in Bass

```python
import math
import jax, jax.numpy as jnp
from jax import lax
import numpy as np

D_MODEL = 2048
BATCH = 4
SEQ = 2048
DEPTH = 4

DA_HEADS = 4
DA_HD = 64
DA_W = DA_HEADS * 2 * DA_HD
SW_HEADS = 8
SW_KV = 2
SW_HD = 64
WINDOW = 128
SW_QW = SW_HEADS * SW_HD
SW_KW = SW_KV * SW_HD
MB_HEADS = 8
MB_HD = 64
MB_W = MB_HEADS * MB_HD
MB_BLOCK = 256
MB_TOPK = 3
MB_CHUNK = 32
Q_BLOCK = 128
N_BUCKETS = 32
MAX_DIST = 128
N_ATT_HEADS = DA_HEADS + SW_HEADS + MB_HEADS
N_BRANCH = 3
IN_W = 3 * DA_W + SW_QW + 2 * SW_KW + 3 * MB_W + N_BRANCH * D_MODEL
D_FF = 5632
CONV_W = 3
EPS = 1e-6

kernel_name = "hybrid_gated_diff_swa_moba_convffn"


def rmsnorm(x, g):
    xf = x.astype(jnp.float32)
    y = xf * lax.rsqrt(jnp.mean(xf * xf, axis=-1, keepdims=True) + EPS)
    return (y * g.astype(jnp.float32)).astype(x.dtype)


def rel_bucket(dist):
    n = jnp.maximum(dist, 0)
    max_exact = N_BUCKETS // 2
    nf = jnp.maximum(n, 1).astype(jnp.float32)
    large = max_exact + (jnp.log(nf / max_exact) / math.log(MAX_DIST / max_exact)
                         * (N_BUCKETS - max_exact)).astype(jnp.int32)
    large = jnp.minimum(large, N_BUCKETS - 1)
    return jnp.where(n < max_exact, n, large)


def diff_attention(q, k, v, lam, lam_init, subln_g, bias_t):
    B, S = q.shape[0], q.shape[1]
    nb = S // Q_BLOCK
    scale = DA_HD ** -0.5
    qb = jnp.moveaxis(q.reshape(B, nb, Q_BLOCK, DA_HEADS, 2, DA_HD), 1, 0)
    kpos = jnp.arange(S)

    def block(args):
        qi, i = args
        qpos = i * Q_BLOCK + jnp.arange(Q_BLOCK)
        dist = qpos[:, None] - kpos[None, :]
        bias = jnp.take(bias_t, rel_bucket(dist), axis=1)
        logits = jnp.einsum('bqhmd,bkhmd->bhmqk', qi, k).astype(jnp.float32) * scale + bias[None, :, None]
        logits = jnp.where(dist >= 0, logits, -jnp.inf)
        p = jax.nn.softmax(logits, axis=-1)
        attn = p[:, :, 0] - lam * p[:, :, 1]
        return jnp.einsum('bhqk,bkhe->bqhe', attn.astype(v.dtype), v)

    o = lax.map(block, (qb, jnp.arange(nb)))
    o = jnp.moveaxis(o, 0, 1).reshape(B, S, DA_HEADS, 2 * DA_HD)
    o = rmsnorm(o, subln_g) * (1.0 - lam_init)
    return o.reshape(B, S, DA_W)


def sliding_window_attention(q, k, v, sinks, bias_t):
    B, S = q.shape[0], q.shape[1]
    nb = S // Q_BLOCK
    G = SW_HEADS // SW_KV
    scale = SW_HD ** -0.5
    qb = q.reshape(B, nb, Q_BLOCK, SW_KV, G, SW_HD)
    kb = k.reshape(B, nb, Q_BLOCK, SW_KV, SW_HD)
    vb = v.reshape(B, nb, Q_BLOCK, SW_KV, SW_HD)
    shift = lambda t: jnp.concatenate([jnp.zeros_like(t[:, :1]), t[:, :-1]], axis=1)
    kc = jnp.concatenate([shift(kb), kb], axis=2)
    vc = jnp.concatenate([shift(vb), vb], axis=2)
    blk = jnp.arange(nb)[:, None] * Q_BLOCK
    qpos = blk + jnp.arange(Q_BLOCK)[None]
    kpos = blk - Q_BLOCK + jnp.arange(2 * Q_BLOCK)[None]
    dist = qpos[:, :, None] - kpos[:, None, :]
    valid = (dist >= 0) & (dist < WINDOW) & (kpos[:, None, :] >= 0)
    bias = jnp.take(bias_t, rel_bucket(dist), axis=1)
    bias = jnp.moveaxis(bias.reshape(SW_KV, G, nb, Q_BLOCK, 2 * Q_BLOCK), 2, 0)
    logits = jnp.einsum('bnqkgd,bnskd->bnkgqs', qb, kc).astype(jnp.float32) * scale + bias[None]
    logits = jnp.where(valid[None, :, None, None], logits, -jnp.inf)
    sink = jnp.broadcast_to(sinks.astype(jnp.float32).reshape(1, 1, SW_KV, G, 1, 1),
                            logits.shape[:-1] + (1,))
    p = jax.nn.softmax(jnp.concatenate([logits, sink], axis=-1), axis=-1)[..., :-1]
    o = jnp.einsum('bnkgqs,bnskd->bnqkgd', p.astype(vc.dtype), vc)
    return o.reshape(B, S, SW_QW)


def moba_attention(q, k, v, bias_t):
    B, S = q.shape[0], q.shape[1]
    nblk = -(-S // MB_BLOCK)
    Sp = nblk * MB_BLOCK
    padw = ((0, 0), (0, Sp - S), (0, 0), (0, 0))
    q, k, v = [jnp.pad(t, padw).transpose(0, 2, 1, 3) for t in (q, k, v)]
    scale = MB_HD ** -0.5
    kblk = k.reshape(B, MB_HEADS, nblk, MB_BLOCK, MB_HD)
    vblk = v.reshape(B, MB_HEADS, nblk, MB_BLOCK, MB_HD)
    kmean = jnp.mean(kblk.astype(jnp.float32), axis=3)
    own = jnp.arange(Sp) // MB_BLOCK
    gate = jnp.einsum('bhtd,bhnd->bhtn', q.astype(jnp.float32), kmean)
    past = jnp.arange(nblk)[None, :] < own[:, None]
    gate = jnp.where(past, gate, -jnp.inf)
    topk = min(MB_TOPK, nblk)
    _, idx = lax.top_k(gate, topk)

    nch = Sp // MB_CHUNK
    per_blk = MB_BLOCK // MB_CHUNK
    qc = q.reshape(B, MB_HEADS, nch, MB_CHUNK, MB_HD).transpose(2, 0, 1, 3, 4)
    idxc = idx.reshape(B, MB_HEADS, nch, MB_CHUNK, topk).transpose(2, 0, 1, 3, 4)
    bi = jnp.arange(B)[:, None, None, None]
    hi = jnp.arange(MB_HEADS)[None, :, None, None]
    hi5 = hi[..., None]
    s_off = jnp.arange(MB_BLOCK)

    def chunk(args):
        qi, ii, c = args
        qpos = c * MB_CHUNK + jnp.arange(MB_CHUNK)
        ob = c // per_blk
        ks = kblk[bi, hi, ii]
        vs = vblk[bi, hi, ii]
        dist_s = qpos[None, None, :, None, None] - (ii[..., None] * MB_BLOCK + s_off)
        ls = (jnp.einsum('bhcd,bhcjsd->bhcjs', qi, ks).astype(jnp.float32) * scale
              + bias_t[hi5, rel_bucket(dist_s)])
        ls = jnp.where((jnp.arange(topk) < ob)[:, None], ls, -jnp.inf)
        ls = ls.reshape(B, MB_HEADS, MB_CHUNK, topk * MB_BLOCK)
        ko = lax.dynamic_slice_in_dim(k, ob * MB_BLOCK, MB_BLOCK, axis=2)
        vo = lax.dynamic_slice_in_dim(v, ob * MB_BLOCK, MB_BLOCK, axis=2)
        dist_o = qpos[:, None] - (ob * MB_BLOCK + s_off)[None, :]
        lo = (jnp.einsum('bhcd,bhsd->bhcs', qi, ko).astype(jnp.float32) * scale
              + jnp.take(bias_t, rel_bucket(dist_o), axis=1)[None])
        lo = jnp.where(dist_o >= 0, lo, -jnp.inf)
        p = jax.nn.softmax(jnp.concatenate([ls, lo], axis=-1), axis=-1).astype(vs.dtype)
        ps = p[..., :topk * MB_BLOCK].reshape(B, MB_HEADS, MB_CHUNK, topk, MB_BLOCK)
        po = p[..., topk * MB_BLOCK:]
        return (jnp.einsum('bhcjs,bhcjsd->bhcd', ps, vs)
                + jnp.einsum('bhcs,bhsd->bhcd', po, vo))

    o = lax.map(chunk, (qc, idxc, jnp.arange(nch)))
    o = o.transpose(1, 0, 3, 2, 4).reshape(B, Sp, MB_W)
    return o[:, :S]


def conv_ffn(h, w_up, conv_w, conv_b, w_down):
    S = h.shape[1]
    u = h @ w_up
    up = jnp.pad(u, ((0, 0), (CONV_W - 1, 0), (0, 0)))
    c = conv_b
    for i in range(CONV_W):
        c = c + up[:, i:i + S] * conv_w[i]
    gate, val = jnp.split(c, 2, axis=-1)
    return (jax.nn.gelu(gate, approximate=True) * val) @ w_down


def setup_inputs(seed: int = 0) -> dict:
    key = jax.random.key(seed)
    ks = jax.random.split(key, 24)
    f32 = jnp.float32
    nrm = lambda k, shape, s: jax.random.normal(k, shape, f32) * s
    gain = lambda k: 1.0 + nrm(k, (DEPTH, D_MODEL), 0.02)
    return {
        "x": nrm(ks[0], (BATCH, SEQ, D_MODEL), 1.0),
        "rel_bias_table": nrm(ks[1], (N_BUCKETS, N_ATT_HEADS), 0.5),
        "w_in": nrm(ks[2], (DEPTH, D_MODEL, IN_W), D_MODEL ** -0.5),
        "b_gate": nrm(ks[3], (DEPTH, N_BRANCH * D_MODEL), 0.1),
        "lam_q1": nrm(ks[4], (DEPTH, DA_HD), 0.1),
        "lam_k1": nrm(ks[5], (DEPTH, DA_HD), 0.1),
        "lam_q2": nrm(ks[6], (DEPTH, DA_HD), 0.1),
        "lam_k2": nrm(ks[7], (DEPTH, DA_HD), 0.1),
        "diff_subln_g": 1.0 + nrm(ks[8], (DEPTH, 2 * DA_HD), 0.02),
        "sinks": nrm(ks[9], (DEPTH, SW_HEADS), 0.5),
        "w_oa": nrm(ks[10], (DEPTH, DA_W, D_MODEL), DA_W ** -0.5),
        "w_ob": nrm(ks[11], (DEPTH, SW_QW, D_MODEL), SW_QW ** -0.5),
        "w_oc": nrm(ks[12], (DEPTH, MB_W, D_MODEL), MB_W ** -0.5),
        "w_out": nrm(ks[13], (DEPTH, D_MODEL, D_MODEL), D_MODEL ** -0.5),
        "pre_mix_g": gain(ks[14]),
        "post_mix_g": gain(ks[15]),
        "pre_ffn_g": gain(ks[16]),
        "post_ffn_g": gain(ks[17]),
        "w_up": nrm(ks[18], (DEPTH, D_MODEL, 2 * D_FF), D_MODEL ** -0.5),
        "conv_w": nrm(ks[19], (DEPTH, CONV_W, 2 * D_FF), CONV_W ** -0.5),
        "conv_b": nrm(ks[20], (DEPTH, 2 * D_FF), 0.02),
        "w_down": nrm(ks[21], (DEPTH, D_FF, D_MODEL), D_FF ** -0.5),
    }


def reference(x, rel_bias_table, w_in, b_gate, lam_q1, lam_k1, lam_q2, lam_k2, diff_subln_g,
              sinks, w_oa, w_ob, w_oc, w_out, pre_mix_g, post_mix_g, pre_ffn_g, post_ffn_g,
              w_up, conv_w, conv_b, w_down):
    B, S = x.shape[0], x.shape[1]
    tab_t = rel_bias_table.T
    tab_a = tab_t[:DA_HEADS]
    tab_b = tab_t[DA_HEADS:DA_HEADS + SW_HEADS]
    tab_c = tab_t[DA_HEADS + SW_HEADS:]
    cuts = list(np.cumsum([DA_W, DA_W, DA_W, SW_QW, SW_KW, SW_KW, MB_W, MB_W, MB_W]))
    for l in range(DEPTH):
        h = rmsnorm(x, pre_mix_g[l])
        proj = h @ w_in[l]
        qa, ka, va, qb, kb, vb, qc, kc, vc, g = jnp.split(proj, cuts, axis=-1)
        gates = jax.nn.sigmoid((g + b_gate[l]).astype(jnp.float32)).astype(x.dtype)
        ga, gb, gc = jnp.split(gates, N_BRANCH, axis=-1)

        lam_init = 0.8 - 0.6 * math.exp(-0.3 * l)
        lam = (jnp.exp(jnp.sum(lam_q1[l].astype(jnp.float32) * lam_k1[l].astype(jnp.float32)))
               - jnp.exp(jnp.sum(lam_q2[l].astype(jnp.float32) * lam_k2[l].astype(jnp.float32)))
               + lam_init)
        ya = diff_attention(qa.reshape(B, S, DA_HEADS, 2, DA_HD), ka.reshape(B, S, DA_HEADS, 2, DA_HD),
                            va.reshape(B, S, DA_HEADS, 2 * DA_HD), lam, lam_init, diff_subln_g[l], tab_a)
        yb = sliding_window_attention(qb.reshape(B, S, SW_HEADS, SW_HD), kb.reshape(B, S, SW_KV, SW_HD),
                                      vb.reshape(B, S, SW_KV, SW_HD), sinks[l], tab_b)
        yc = moba_attention(qc.reshape(B, S, MB_HEADS, MB_HD), kc.reshape(B, S, MB_HEADS, MB_HD),
                            vc.reshape(B, S, MB_HEADS, MB_HD), tab_c)
        mix = ga * (ya @ w_oa[l]) + gb * (yb @ w_ob[l]) + gc * (yc @ w_oc[l])
        x = x + rmsnorm(mix @ w_out[l], post_mix_g[l])
        h = rmsnorm(x, pre_ffn_g[l])
        x = x + rmsnorm(conv_ffn(h, w_up[l], conv_w[l], conv_b[l], w_down[l]), post_ffn_g[l])
    return x
```

```python
import math
import numpy as np
import ml_dtypes
import concourse.bass as bass
import concourse.mybir as mybir
from concourse.bass_utils import run_bass_kernel_spmd

F32 = mybir.dt.float32
BF16 = mybir.dt.bfloat16
AF = mybir.ActivationFunctionType
ALU = mybir.AluOpType
AX = mybir.AxisListType

D = 2048
S = 2048
DEPTH = 4
INW = 9984
DFF = 5632
NT = 4
TW = 512
NQB = 16
EPS = 1e-6
NEG = -30000.0
PC_PREMIX, PC_POSTMIX, PC_PREFFN, PC_POSTFFN, PC_BGATE, PC_CW, PC_CB = 0, 16, 32, 48, 64, 112, 376
PC = 464


class Sched:
    ENGS = ("pe", "act", "dve", "pool", "sp")

    def __init__(self, nc):
        self.nc = nc
        self.eng = {"pe": nc.tensor, "act": nc.scalar, "dve": nc.vector,
                    "pool": nc.gpsimd, "sp": nc.sync}
        self.sem, self.cnt = {}, {}
        self.waited = {e: {} for e in self.ENGS}
        self.last_w, self.reads = {}, {}
        self._stack = []
        self.cur = {}
        self.gen = {}
        for e in self.ENGS:
            self._rot("E_" + e)

    def _mk(self, name):
        cm = self.nc.semaphore(name)
        h = cm.__enter__()
        self._stack.append(cm)
        self.sem[name] = h
        self.cnt[name] = 0

    LIMIT = 1500

    def _rot(self, key):
        g = self.gen.get(key, -1) + 1
        self.gen[key] = g
        name = "%s#%d" % (key, g)
        self._mk(name)
        self.cur[key] = name
        return name

    def close(self):
        for cm in reversed(self._stack):
            cm.__exit__(None, None, None)

    def _deps(self, reads, writes):
        ev = []
        for k in reads:
            if k in self.last_w:
                ev.append(self.last_w[k])
        for k in writes:
            if k in self.last_w:
                ev.append(self.last_w[k])
            ev.extend(self.reads.get(k, ()))
        return ev

    def _emit_waits(self, e, evs):
        need = {}
        for (s, v, src) in evs:
            if src == "pe" and e == "pe":
                continue
            if v > need.get(s, 0):
                need[s] = v
        for s, v in need.items():
            if self.waited[e].get(s, 0) >= v:
                continue
            self.eng[e].wait_ge(self.sem[s], v)
            self.waited[e][s] = v

    def _record(self, event, reads, writes):
        for k in reads:
            self.reads.setdefault(k, []).append(event)
        for k in writes:
            self.last_w[k] = event
            self.reads[k] = []

    def op(self, e, fn, reads=(), writes=()):
        self._emit_waits(e, self._deps(reads, writes))
        ins = fn(self.eng[e])
        s = self.cur["E_" + e]
        if self.cnt[s] + 1 > self.LIMIT:
            s = self._rot("E_" + e)
        self.cnt[s] += 1
        ins.then_inc(self.sem[s], 1)
        self._record((s, self.cnt[s], e), reads, writes)

    def dma(self, e, fn, semkey, reads=(), writes=()):
        key = "D_" + semkey
        if key not in self.cur:
            self._rot(key)
        self._emit_waits(e, self._deps(reads, writes))
        insl = fn(self.eng[e])
        if not isinstance(insl, (list, tuple)):
            insl = [insl]
        s = self.cur[key]
        if self.cnt[s] + 16 * len(insl) > self.LIMIT:
            s = self._rot(key)
        for ins in insl:
            ins.then_inc(self.sem[s], 16)
            self.cnt[s] += 16
        self._record((s, self.cnt[s], "dma"), reads, writes)

    def barrier(self):
        evs = [(s, c, "x") for s, c in self.cnt.items() if c > 0]
        for e in self.ENGS:
            self._emit_waits(e, evs)


def rel_bucket_np(n):
    n = np.maximum(n, 0)
    nf = np.maximum(n, 1).astype(np.float32)
    large = 16 + (np.log(nf / np.float32(16)) / np.float32(math.log(8.0)) * np.float32(16)).astype(np.int32)
    large = np.minimum(large, 31)
    return np.where(n < 16, n, large)


def host_consts():
    k = np.arange(128)[:, None]
    q = np.arange(128)[None, :]
    E = np.zeros((128, 32, 2, 128), np.float32)
    for dl in range(2):
        b = rel_bucket_np(q - k + 128 * dl)
        for bb in range(32):
            E[:, bb, dl, :] = (b == bb)
    mdiag = np.where(q >= k, 0.0, NEG).astype(np.float32)
    mnear = np.where(q < k, 0.0, NEG).astype(np.float32)
    masks = np.stack([mdiag, mnear], 1)
    ident = np.eye(128, dtype=np.float32)
    ones = np.ones((128, 128), np.float32)
    j = np.arange(16)[:, None]
    n = np.arange(8)[None, :]
    valid = (n < (j // 2)).astype(np.float32)
    own = (n == (j // 2)).astype(np.float32)
    def bc(a):
        return np.ascontiguousarray(np.broadcast_to(a[None, :, :], (128, 16, 8))).astype(np.float32)
    selc = np.stack([bc(valid), bc(own), bc((1.0 - valid) * -1e9)], 1)
    return dict(cE=E, cmask=masks, cident=ident.astype(ml_dtypes.bfloat16), cones=ones, cident32=ident, csel=selc)


class _Stop(Exception):
    pass


def build(n_layers=DEPTH, stop_after=None, dbg=False):
    nc = bass.Bass("TRN2", target_bir_lowering=False)
    dt_in = lambda name, shape, dt=F32: nc.dram_tensor(name, list(shape), dt, kind="ExternalInput")
    xT_in = dt_in("xT", [D, S])
    table = dt_in("table", [32, 20])
    w_in = dt_in("w_in", [n_layers, D, INW])
    w_o = [dt_in(n_, [n_layers, 512, D]) for n_ in ("w_oa", "w_ob", "w_oc")]
    w_out = dt_in("w_out", [n_layers, D, D])
    w_up = dt_in("w_up", [n_layers, D, 2 * DFF])
    w_down = dt_in("w_down", [n_layers, DFF, D])
    pcols = dt_in("pcols", [128, DEPTH, PC])
    lamv = dt_in("lamv", [DEPTH, 4, 64])
    subg = dt_in("subg", [DEPTH, 128])
    sinks = dt_in("sinks", [DEPTH, 8])
    cE = dt_in("cE", [128, 32, 2, 128])
    cmask = dt_in("cmask", [128, 2, 128])
    cident = dt_in("cident", [128, 128], BF16)
    cident32 = dt_in("cident32", [128, 128])
    cones = dt_in("cones", [128, 128])
    csel = dt_in("csel", [128, 3, 16, 8])
    out = nc.dram_tensor("out", [D, S], F32, kind="ExternalOutput")
    kd = "ExternalOutput" if dbg else "Internal"
    qkT = nc.dram_tensor("qkT", [2688, S], BF16, kind=kd)
    qc32 = nc.dram_tensor("qc32", [512, S], F32, kind=kd)
    vS = nc.dram_tensor("vS", [S, 1152], BF16, kind=kd)
    gT = nc.dram_tensor("gT", [6144, S], BF16, kind=kd)
    yT = nc.dram_tensor("yT", [1536, S], BF16, kind=kd)
    QK_ROW = {"qa": 0, "ka": 512, "qb": 1024, "kb": 1536, "qc": 1664, "kc": 2176}

    S_ = Sched(nc)
    import contextlib
    es = contextlib.ExitStack()
    sb = lambda name, shape, dt=F32: es.enter_context(nc.sbuf_tensor(name, list(shape), dt))
    PB = [es.enter_context(nc.psum_tensor("pb%d" % i, [128, 512], F32)) for i in range(8)]
    pcol = sb("pcol", [128, 1, PC])
    ident = sb("ident", [128, 128], BF16)
    ident32 = sb("ident32", [128, 128])
    ones32 = sb("ones32", [128, 128])
    biasT = sb("biasT", [128, 20, 2, 128])
    cfar = sb("cfar", [128, 20])
    selc = sb("selc", [128, 3, 16, 8])
    WS = [sb("ws%d" % i, [128, 8192], BF16) for i in range(2)]
    wsi = [0]
    xt = sb("xt", [128, 16, TW])
    rstd = sb("rstd", [128, TW])
    sqt = [sb("sqt%d" % i, [128, TW]) for i in range(2)]
    stg = [sb("stg%d" % i, [128, 4, TW], BF16) for i in range(2)]
    big = sb("big", [128, 27648], BF16)
    big2 = sb("big2", [128, 16384], BF16)
    hT = big2[:, 0:8192].rearrange("p (c t) -> p c t", c=16)
    halo = sb("halo", [128, 88, 2])
    ubuf = [sb("ubuf%d" % i, [128, TW + 2]) for i in range(2)]
    ctmp = [sb("ctmp%d" % i, [128, TW]) for i in range(3)]
    small = sb("small", [128, 64])
    lvt_t = sb("lvt_t", [128, 256])
    gsub_t = sb("gsub_t", [128, 128])
    kmean_t = sb("kmean_t", [128, 32])
    khl_t = sb("khl_t", [128, 64], BF16)

    def ld(dst, src, key, eng="sp", reads=()):
        S_.dma(eng, lambda e: e.dma_start(out=dst, in_=src), key, reads=reads, writes=[key])

    ld(ident[:], cident.ap(), "ident")
    ld(ident32[:], cident32.ap(), "ident32")
    ld(ones32[:], cones.ap(), "ones32")
    ld(selc[:], csel.ap(), "selc")
    ld(cfar[:], bass.AP(table, 31 * 20, [[0, 128], [1, 20]]), "cfar")
    S_.op("pool", lambda e: e.memset(halo[:], 0.0), writes=["halo"])

    Et = big[:].bitcast(F32)[:, 0:8192].rearrange("p (b d q) -> p b d q", b=32, d=2)
    tabbc = big[:].bitcast(F32)[:, 8192:8832]
    ld(Et, cE.ap(), "Et")
    ld(tabbc, bass.AP(table, 0, [[0, 128], [1, 640]]), "tabbc")
    ld(biasT[:, :, 0, :], bass.AP(cmask, 0, [[256, 128], [0, 20], [1, 128]]), "biasT")
    S_.op("pool", lambda e: e.memset(biasT[:, :, 1, :], 0.0), reads=[], writes=["biasT1"])
    ld(biasT[:, 4:12, 1, :], bass.AP(cmask, 128, [[256, 128], [0, 8], [1, 128]]), "biasT1", reads=["biasT1"])
    for h in range(20):
        for b in range(32):
            def f(e, h=h, b=b):
                return e.scalar_tensor_tensor(out=biasT[:, h, :, :], in0=Et[:, b, :, :],
                                              scalar=tabbc[:, b * 20 + h:b * 20 + h + 1],
                                              in1=biasT[:, h, :, :], op0=ALU.mult, op1=ALU.add)
            S_.op("dve", f, reads=["Et", "tabbc", "biasT", "biasT1", "bT%d" % h], writes=["bT%d" % h])
    S_.barrier()
    bias_keys = ["bT%d" % h for h in range(20)] + ["cfar"]

    def wslot():
        i = wsi[0]
        wsi[0] = (i + 1) % 2
        return WS[i], "ws%d" % i

    pbi = [0]

    def next_pb(lo=0, hi=4):
        i = lo + pbi[0] % (hi - lo)
        pbi[0] += 1
        return PB[i], "pb%d" % i

    def load_w(src_ap, kc, ncols):
        wt, key = wslot()
        view = wt[:, 0:kc * ncols].rearrange("p (k n) -> p k n", k=kc)
        S_.dma("pool", lambda e: e.dma_start(out=view, in_=src_ap.rearrange("(k p) n -> p k n", p=128)),
               key, writes=[key])
        return view, key

    def rms_stats(src_chunk, src_keys, nchunks=16):
        for c in range(nchunks):
            sq, sk = sqt[c % 2], "sqt%d" % (c % 2)
            S_.op("act", lambda e, c=c, sq=sq: e.activation(out=sq[:], in_=src_chunk(c), func=AF.Square),
                  reads=src_keys, writes=[sk])
            S_.op("pe", lambda e, c=c, sq=sq: e.matmul(PB[4][:], lhsT=ones32[:], rhs=sq[:], start=(c == 0),
                                                     stop=(c == nchunks - 1)),
                  reads=[sk, "ones32"], writes=["pb4"])
        S_.op("dve", lambda e: e.tensor_scalar(out=rstd[:], in0=PB[4][:], scalar1=1.0 / D, scalar2=EPS,
                                               op0=ALU.mult, op1=ALU.add), reads=["pb4"], writes=["rstd"])
        S_.op("act", lambda e: e.activation(out=rstd[:], in_=rstd[:], func=AF.Sqrt), reads=["rstd"], writes=["rstd"])
        S_.op("dve", lambda e: e.reciprocal(out=rstd[:], in_=rstd[:]), reads=["rstd"], writes=["rstd"])

    def make_h(l, gcol0):
        rms_stats(lambda c: xt[:, c, :], ["xt"])
        for c in range(16):
            S_.op("dve", lambda e, c=c: e.scalar_tensor_tensor(
                out=hT[:, c, :], in0=xt[:, c, :], scalar=pcol[:, 0, gcol0 + c:gcol0 + c + 1], in1=rstd[:],
                op0=ALU.mult, op1=ALU.mult), reads=["xt", "rstd", "pcol"], writes=["hT"])

    def gemm_fm(wview, wkey, kc, nch, act_chunk, act_keys, consumer):
        for j in range(nch):
            ps, pk = next_pb()

            def mm(e, j=j, ps=ps):
                last = None
                for k in range(kc):
                    last = e.matmul(ps[:], lhsT=wview[:, k, j * 128:(j + 1) * 128], rhs=act_chunk(k),
                                    start=(k == 0), stop=(k == kc - 1))
                return last
            S_.op("pe", mm, reads=[wkey] + act_keys, writes=[pk])
            consumer(j, ps, pk)

    oT = big2[:].bitcast(F32).rearrange("p (c t) -> p c t", c=16)

    def post_norm_residual(l, gcol0):
        rms_stats(lambda c: oT[:, c, :], ["oT"])
        for c in range(16):
            S_.op("dve", lambda e, c=c: e.scalar_tensor_tensor(
                out=oT[:, c, :], in0=oT[:, c, :], scalar=pcol[:, 0, gcol0 + c:gcol0 + c + 1], in1=rstd[:],
                op0=ALU.mult, op1=ALU.mult), reads=["oT", "rstd"], writes=["oT"])
            S_.op("dve", lambda e, c=c: e.tensor_tensor(out=xt[:, c, :], in0=xt[:, c, :], in1=oT[:, c, :],
                                                        op=ALU.add), reads=["oT", "xt"], writes=["xt"])

    def tokkey(name, tt):
        return "%s:%d" % (name, tt)

    def stop(tag):
        if stop_after == tag:
            raise _Stop()
    try:
      for l in range(n_layers):
        x_src = xT_in if l == 0 else out
        stop("setup")
        ld(pcol[:], pcols.ap()[:, l:l + 1, :], "pcol")
        import os
        for tt in list(range(NT)) * int(os.environ.get("REPS1", "1")):
            t0 = tt * TW
            ld(xt[:], x_src.ap()[:, t0:t0 + TW].rearrange("(c p) t -> p c t", p=128), "xt",
               reads=[tokkey("x", tt)])
            make_h(l, PC_PREMIX)
            stop("norm")
            groups = [("qa", 0, 512, "fm"), ("ka", 512, 512, "fm"), ("va", 1024, 512, "tm"),
                      ("qb", 1536, 512, "fm"), ("kb", 2048, 128, "fm"), ("vb", 2176, 128, "tm"),
                      ("qc", 2304, 512, "fm"), ("kc", 2816, 512, "fm"), ("vc", 3328, 512, "tm")]
            groups += [("g%d" % i, 3840 + i * 512, 512, "gate") for i in range(12)]
            vcol = {"va": 0, "vb": 512, "vc": 640}
            for gi, (nm, c0, ncol, kind) in enumerate(groups):
                wv, wk = load_w(w_in.ap()[l, :, c0:c0 + ncol], 16, ncol)
                st, sk = stg[gi % 2], "stg%d" % (gi % 2)
                nch = ncol // 128
                if kind in ("fm", "gate"):
                    def cons(j, ps, pk, nm=nm, kind=kind, st=st, sk=sk, gi=gi):
                        if kind == "gate":
                            bc_ = PC_BGATE + (gi - 9) * 4 + j
                            S_.op("act", lambda e: e.activation(out=st[:, j, :], in_=ps[:], func=AF.Sigmoid,
                                                                bias=pcol[:, 0, bc_:bc_ + 1], scale=1.0),
                                  reads=[pk, "pcol"], writes=[sk])
                        else:
                            S_.op("act", lambda e: e.activation(out=st[:, j, :], in_=ps[:], func=AF.Identity),
                                  reads=[pk], writes=[sk])
                            if nm == "qc":
                                S_.op("act", lambda e: e.activation(out=ctmp[2][:], in_=ps[:], func=AF.Identity),
                                      reads=[pk], writes=["ctmp2"])
                                d2 = qc32.ap()[j * 128:(j + 1) * 128, t0:t0 + TW]
                                S_.dma("sp", lambda e, d2=d2: e.dma_start(out=d2, in_=ctmp[2][:]), "qc32w",
                                       reads=["ctmp2"], writes=[tokkey("qc32", tt)])
                    gemm_fm(wv, wk, 16, nch, lambda k: hT[:, k, :], ["hT"], cons)
                    if kind == "gate":
                        r0 = (gi - 9) * 512
                        dst = gT.ap()[r0:r0 + 512, t0:t0 + TW].rearrange("(j p) t -> p j t", p=128)
                        S_.dma("sp", lambda e, dst=dst, st=st: e.dma_start(out=dst, in_=st[:]), "stw%d" % (gi % 2),
                               reads=[sk], writes=[tokkey("gT", tt)])
                    else:
                        r0 = QK_ROW[nm]
                        dst = qkT.ap()[r0:r0 + ncol, t0:t0 + TW].rearrange("(j p) t -> p j t", p=128)
                        S_.dma("sp", lambda e, dst=dst, st=st, nch=nch: e.dma_start(out=dst, in_=st[:, 0:nch, :]),
                               "stw%d" % (gi % 2), reads=[sk], writes=[tokkey("qkT", tt)])
                else:
                    stv = st[:].rearrange("p j t -> p (j t)")[:, 0:4 * ncol].rearrange("p (i n) -> p i n", i=4)
                    for i in range(4):
                        ps, pk = next_pb()

                        def mm(e, i=i, ps=ps, wv=wv, ncol=ncol):
                            last = None
                            for k in range(16):
                                last = e.matmul(ps[:, 0:ncol], lhsT=hT[:, k, i * 128:(i + 1) * 128], rhs=wv[:, k, :],
                                                start=(k == 0), stop=(k == 15))
                            return last
                        S_.op("pe", mm, reads=[wk, "hT"], writes=[pk])
                        S_.op("act", lambda e, i=i, ps=ps, ncol=ncol, stv=stv: e.activation(
                            out=stv[:, i, :], in_=ps[:, 0:ncol], func=AF.Identity), reads=[pk], writes=[sk])
                    v0 = vcol[nm]
                    dst = vS.ap()[t0:t0 + TW, v0:v0 + ncol].rearrange("(i p) n -> p i n", p=128)
                    S_.dma("sp", lambda e, dst=dst, stv=stv: e.dma_start(out=dst, in_=stv), "stw%d" % (gi % 2),
                           reads=[sk], writes=[tokkey("vS", tt)])
                stop("g%d" % gi)

        stop("S1")
        S_.barrier()
        allk = lambda nm: [tokkey(nm, tt) for tt in range(NT)]
        bigv = big[:]
        qT = bigv[:, 0:8192].rearrange("p (c t) -> p c t", c=4)
        kT = bigv[:, 8192:16384].rearrange("p (c t) -> p c t", c=4)
        vA = bigv[:, 16384:16384 + 16 * 4 * 129].rearrange("p (b h e) -> p b h e", b=16, h=4)
        vC = bigv[:, 16384:16384 + 16 * 8 * 65].rearrange("p (b h e) -> p b h e", b=16, h=8)
        q32 = xt[:].rearrange("p c t -> p (c t)").rearrange("p (c t) -> p c t", c=4)
        ytok = bigv[:, 24704:24704 + 2048].rearrange("p (j n) -> p j n", j=4)
        PT = [stg[0][:].rearrange("p j t -> p (j t)")[:, i * 512:(i + 1) * 512] for i in range(4)]
        PTK = ["PT%d" % i for i in range(4)]
        pti = [0]
        tmp32 = ctmp[0]
        accs = [(PB[2], PB[3], "pb2", "pb3"), (PB[4], PB[5], "pb4", "pb5")]
        ysT = stg[1][:].rearrange("p j t -> p (j t)")

        def attn_tile(hh, kt_ap, q_ap_fn, kt, jlist, pv_fn, scale_bias=True, swa=False):
            groups_ = []
            far = [j for j in jlist if j - kt >= 2]
            if far:
                groups_.append(("far", far))
            if kt + 1 in jlist:
                groups_.append(("near", [kt + 1]))
            if kt in jlist:
                groups_.append(("diag", [kt]))
            for kind, js in groups_:
                n = len(js) * 128
                sps, spk = (PB[0], "pb0") if pti[0] % 2 == 0 else (PB[1], "pb1")
                pt, ptk = PT[pti[0] % 4], PTK[pti[0] % 4]
                pti[0] += 1
                S_.op("pe", lambda e, js=js, n=n, sps=sps: e.matmul(sps[:, 0:n], lhsT=kt_ap, rhs=q_ap_fn(js[0] * 128, n),
                                                                  start=True, stop=True),
                      reads=["attn_in"], writes=[spk])
                if kind == "far":
                    S_.op("act", lambda e, n=n, sps=sps, pt=pt: e.activation(out=pt[:, 0:n], in_=sps[:, 0:n], func=AF.Exp,
                                                                            bias=cfar[:, hh:hh + 1], scale=0.125),
                          reads=[spk, "cfar"], writes=[ptk])
                else:
                    dl = 1 if kind == "near" else 0
                    S_.op("dve", lambda e, sps=sps, dl=dl: e.scalar_tensor_tensor(
                        out=tmp32[:, 0:128], in0=sps[:, 0:128], scalar=0.125, in1=biasT[:, hh, dl, :],
                        op0=ALU.mult, op1=ALU.add), reads=[spk] + bias_keys, writes=["tmp32"])
                    S_.op("act", lambda e, pt=pt: e.activation(out=pt[:, 0:128], in_=tmp32[:, 0:128], func=AF.Exp),
                          reads=["tmp32"], writes=[ptk])
                for ji, j in enumerate(js):
                    pv_fn(j, pt[:, ji * 128:(ji + 1) * 128], ptk)

        lam_init = 0.8 - 0.6 * math.exp(-0.3 * l)
        lamt = small[:, 0:8]
        lvt = lvt_t[:].rearrange("p (a b) -> p a b", a=4)
        ld(lvt, bass.AP(lamv, l * 256, [[0, 128], [64, 4], [1, 64]]), "lvt")
        gsub = gsub_t[:]
        ld(gsub, bass.AP(subg, l * 128, [[0, 128], [1, 128]]), "gsub")
        esink = small[:, 8:16]
        ld(esink, bass.AP(sinks, l * 8, [[0, 128], [1, 8]]), "esink")
        S_.op("dve", lambda e: e.tensor_tensor(out=lvt[:, 0, :], in0=lvt[:, 0, :], in1=lvt[:, 1, :], op=ALU.mult),
              reads=["lvt"], writes=["lvt"])
        S_.op("dve", lambda e: e.tensor_tensor(out=lvt[:, 2, :], in0=lvt[:, 2, :], in1=lvt[:, 3, :], op=ALU.mult),
              reads=["lvt"], writes=["lvt"])
        S_.op("dve", lambda e: e.reduce_sum(out=lamt[:, 0:1], in_=lvt[:, 0, :], axis=AX.X), reads=["lvt"], writes=["lamt"])
        S_.op("dve", lambda e: e.reduce_sum(out=lamt[:, 1:2], in_=lvt[:, 2, :], axis=AX.X), reads=["lvt"], writes=["lamt"])
        S_.op("act", lambda e: e.activation(out=lamt[:, 2:4], in_=lamt[:, 0:2], func=AF.Exp), reads=["lamt"], writes=["lamt"])
        S_.op("dve", lambda e: e.scalar_tensor_tensor(out=lamt[:, 4:5], in0=lamt[:, 3:4], scalar=-lam_init, in1=lamt[:, 2:3],
                                                      op0=ALU.add, op1=ALU.subtract), reads=["lamt"], writes=["lamt"])
        S_.op("dve", lambda e: e.tensor_scalar(out=gsub, in0=gsub, scalar1=(1.0 - lam_init), scalar2=None, op0=ALU.mult),
              reads=["gsub"], writes=["gsub"])
        S_.op("act", lambda e: e.activation(out=esink, in_=esink, func=AF.Exp), reads=["esink"], writes=["esink"])

        def flush_y(jc, ncols_used, row0):
            nchk = ncols_used // 128
            for c in range(nchk):
                tp = PB[6][:].bitcast(BF16)[:, 0:512]
                for jj in range(4):
                    S_.op("pe", lambda e, c=c, jj=jj: e.transpose(tp[:, jj * 128:(jj + 1) * 128],
                                                                 ytok[:, jj, c * 128:(c + 1) * 128], ident[:]),
                          reads=["ytok", "ident"], writes=["pb6"])
                S_.op("act", lambda e, c=c: e.activation(out=ysT[:, c * 512:(c + 1) * 512], in_=tp, func=AF.Identity),
                      reads=["pb6"], writes=["ysT"])
            dst = yT.ap()[row0:row0 + ncols_used, jc * 512:(jc + 1) * 512].rearrange("(c p) t -> p c t", p=128)
            S_.dma("sp", lambda e: e.dma_start(out=dst, in_=ysT[:, 0:nchk * 512].rearrange("p (c t) -> p c t", c=nchk)),
                   "yTw", reads=["ysT"], writes=["yT"])

        ld(qT, qkT.ap()[0:512, :].rearrange("(c p) t -> p c t", p=128), "attn_in", reads=allk("qkT"))
        ld(kT, qkT.ap()[512:1024, :].rearrange("(c p) t -> p c t", p=128), "attn_in", reads=allk("qkT"))
        S_.op("pool", lambda e: e.memset(vA[:, :, :, 128:129], 1.0), writes=["attn_in"])
        for h_ in range(4):
            ld(vA[:, :, h_, 0:128], vS.ap()[:, h_ * 128:(h_ + 1) * 128].rearrange("(b p) e -> p b e", p=128), "attn_in",
               reads=allk("vS"))
        omt = ctmp[2][:].rearrange("p (j e) -> p j e", j=4)
        om1 = ctmp[1][:].rearrange("p (j e) -> p j e", j=4)
        ai = 0
        for jc in range(4):
            jlist = list(range(4 * jc, 4 * jc + 4))
            for h in range(4):
                for m in range(2):
                    a0, a1, k0, k1 = accs[ai % 2]
                    ai += 1
                    nkt = 4 * jc + 4

                    touched = {}
                    tot = {k0: 8 * jc + 3, k1: 8 * jc + 7}

                    def pv(j, ptap, ptk, h=h, a0=a0, a1=a1, k0=k0, k1=k1, touched=touched, tot=tot, ktc=[None]):
                        jj = j - 4 * jc
                        bank, bk = (a0, k0) if jj < 2 else (a1, k1)
                        o0 = (jj % 2) * 256
                        kt_ = ktc[0]
                        st_ = bk not in touched
                        touched[bk] = touched.get(bk, 0) + 1
                        sp_ = touched[bk] == tot[bk]
                        S_.op("pe", lambda e: e.matmul(bank[:, o0:o0 + 129], lhsT=ptap, rhs=vA[:, kt_, h, :],
                                                      start=st_, stop=sp_),
                              reads=[ptk, "attn_in"], writes=[bk])
                    for kt in range(nkt):
                        pv.__defaults__[-1][0] = kt
                        attn_tile(h, kT[m * 64:(m + 1) * 64, h, kt * 128:(kt + 1) * 128],
                                  lambda q0, n, m=m, h=h: qT[m * 64:(m + 1) * 64, h, q0:q0 + n], kt, jlist, pv)
                    dstm = omt if m == 0 else om1
                    for jj in range(4):
                        bank, bk = (a0, k0) if jj < 2 else (a1, k1)
                        o0 = (jj % 2) * 256
                        S_.op("dve", lambda e, bank=bank, o0=o0, jj=jj: e.reciprocal(out=small[:, 16 + jj:17 + jj],
                                                                                    in_=bank[:, o0 + 128:o0 + 129]),
                              reads=[bk], writes=["rden"])
                        S_.op("dve", lambda e, bank=bank, o0=o0, jj=jj, dstm=dstm: e.tensor_scalar(
                            out=dstm[:, jj, :], in0=bank[:, o0:o0 + 128], scalar1=small[:, 16 + jj:17 + jj], scalar2=None,
                            op0=ALU.mult), reads=[bk, "rden"], writes=["om%d" % m])
                S_.op("dve", lambda e: e.scalar_tensor_tensor(out=omt, in0=om1, scalar=lamt[:, 4:5], in1=omt,
                                                              op0=ALU.mult, op1=ALU.add),
                      reads=["om0", "om1", "lamt"], writes=["om0"])
                S_.op("dve", lambda e: e.tensor_tensor(out=om1, in0=omt, in1=omt, op=ALU.mult), reads=["om0"], writes=["om1"])
                S_.op("dve", lambda e: e.reduce_sum(out=small[:, 20:24], in_=om1, axis=AX.X), reads=["om1"], writes=["ssq"])
                S_.op("dve", lambda e: e.tensor_scalar(out=small[:, 20:24], in0=small[:, 20:24], scalar1=1.0 / 128, scalar2=EPS,
                                                       op0=ALU.mult, op1=ALU.add), reads=["ssq"], writes=["ssq"])
                S_.op("act", lambda e: e.activation(out=small[:, 20:24], in_=small[:, 20:24], func=AF.Sqrt), reads=["ssq"], writes=["ssq"])
                S_.op("dve", lambda e: e.reciprocal(out=small[:, 20:24], in_=small[:, 20:24]), reads=["ssq"], writes=["ssq"])
                for jj in range(4):
                    S_.op("dve", lambda e, jj=jj, h=h: e.scalar_tensor_tensor(
                        out=ytok[:, jj, h * 128:(h + 1) * 128], in0=omt[:, jj, :], scalar=small[:, 20 + jj:21 + jj],
                        in1=gsub, op0=ALU.mult, op1=ALU.mult), reads=["om0", "ssq", "gsub"], writes=["ytok"])
            flush_y(jc, 512, 0)

        stop("A")
        kB = kT[:, 0:2, :]
        ld(qT, qkT.ap()[1024:1536, :].rearrange("(c p) t -> p c t", p=128), "attn_in", reads=allk("qkT") + ["yT"])
        for g in range(2):
            for half in range(2):
                ld(kB[half * 64:(half + 1) * 64, g, :], qkT.ap()[1536 + g * 64:1536 + (g + 1) * 64, :], "attn_in")
        vB = vC[:, :, 0:2, :]
        S_.op("pool", lambda e: e.memset(vB[:, :, :, 64:65], 1.0), writes=["attn_in"])
        for h_ in range(2):
            ld(vB[:, :, h_, 0:64], vS.ap()[:, 512 + h_ * 64:512 + (h_ + 1) * 64].rearrange("(b p) e -> p b e", p=128), "attn_in")
        for jc in range(4):
            jlist = list(range(4 * jc, 4 * jc + 4))
            for h in range(8):
                a0, a1, k0, k1 = accs[ai % 2]
                ai += 1
                g = h // 4
                pb_ = (h % 2) * 64

                touchedb = {}
                totb = 7 if jc == 0 else 8

                def pvb(j, ptap, ptk, g=g, a0=a0, k0=k0, touchedb=touchedb, totb=totb, ktc=[None]):
                    jj = j - 4 * jc
                    kt_ = ktc[0]
                    first = k0 not in touchedb
                    touchedb[k0] = touchedb.get(k0, 0) + 1
                    lastb = touchedb[k0] == totb
                    S_.op("pe", lambda e: e.matmul(a0[:, jj * 128:jj * 128 + 65], lhsT=ptap, rhs=vB[:, kt_, g, :],
                                                  start=first, stop=lastb),
                          reads=[ptk, "attn_in"], writes=[k0])
                for kt in range(max(0, 4 * jc - 1), 4 * jc + 4):
                    pvb.__defaults__[-1][0] = kt
                    js = [j for j in jlist if j in (kt, kt + 1)]
                    attn_tile(4 + h, kB[pb_:pb_ + 64, g, kt * 128:(kt + 1) * 128],
                              lambda q0, n, h=h, pb_=pb_: qT[pb_:pb_ + 64, h // 2, q0:q0 + n], kt, js, pvb)
                for jj in range(4):
                    S_.op("dve", lambda e, jj=jj, h=h, a0=a0: e.tensor_scalar(
                        out=small[:, 16 + jj:17 + jj], in0=a0[:, jj * 128 + 64:jj * 128 + 65], scalar1=esink[:, h:h + 1],
                        scalar2=None, op0=ALU.add), reads=[k0, "esink"], writes=["rden"])
                    S_.op("dve", lambda e, jj=jj: e.reciprocal(out=small[:, 16 + jj:17 + jj], in_=small[:, 16 + jj:17 + jj]),
                          reads=["rden"], writes=["rden"])
                    S_.op("dve", lambda e, jj=jj, h=h, a0=a0: e.tensor_scalar(
                        out=ytok[:, jj, h * 64:(h + 1) * 64], in0=a0[:, jj * 128:jj * 128 + 64],
                        scalar1=small[:, 16 + jj:17 + jj], scalar2=None, op0=ALU.mult),
                        reads=[k0, "rden"], writes=["ytok"])
            flush_y(jc, 512, 512)

        stop("B")
        ld(qT, qkT.ap()[1664:2176, :].rearrange("(c p) t -> p c t", p=128), "attn_in", reads=allk("qkT") + ["yT"])
        ld(kT, qkT.ap()[2176:2688, :].rearrange("(c p) t -> p c t", p=128), "attn_in")
        ld(q32, qc32.ap().rearrange("(c p) t -> p c t", p=128), "attn_in", reads=allk("qc32"))
        S_.op("pool", lambda e: e.memset(vC[:, :, :, 64:65], 1.0), writes=["attn_in"])
        for h_ in range(8):
            ld(vC[:, :, h_, 0:64], vS.ap()[:, 640 + h_ * 64:640 + (h_ + 1) * 64].rearrange("(b p) e -> p b e", p=128), "attn_in")
        kmean = kmean_t[:].rearrange("p (c n) -> p c n", c=4)
        for c_ in range(4):
            S_.op("dve", lambda e, c_=c_: e.reduce_sum(out=kmean_t[:, c_ * 8:(c_ + 1) * 8],
                                                      in_=kT[:, c_, :].rearrange("p (n s) -> p n s", s=256), axis=AX.X),
                  reads=["attn_in"], writes=["kmean"])
        khi = khl_t[:, 0:32].rearrange("p (c n) -> p c n", c=4)
        klo = khl_t[:, 32:64].rearrange("p (c n) -> p c n", c=4)
        qlo = big2[:, 4096:4096 + 8192].rearrange("p (c t) -> p c t", c=4)
        S_.op("act", lambda e: e.activation(out=khl_t[:, 0:32], in_=kmean_t[:], func=AF.Identity), reads=["kmean"], writes=["khl"])
        S_.op("dve", lambda e: e.tensor_tensor(out=khl_t[:, 32:64], in0=kmean_t[:], in1=khl_t[:, 0:32], op=ALU.subtract),
              reads=["kmean", "khl"], writes=["khl"])
        for c_ in range(4):
            S_.op("dve", lambda e, c_=c_: e.tensor_tensor(out=qlo[:, c_, :], in0=q32[:, c_, :], in1=qT[:, c_, :], op=ALU.subtract),
                  reads=["attn_in"], writes=["qlo"])
        def selbc(i, jc):
            a = selc[:, i, 4 * jc:4 * jc + 4, :]
            return bass.AP(a.tensor, a.offset, [list(a.ap[0]), [8, 4], [0, 8], [1, 8]])
        stop("C0")
        selt = sqt[0][:, 0:256].rearrange("p (j h n) -> p j h n", j=4, h=8)
        gte = sqt[1][:, 0:256].rearrange("p (j h n) -> p j h n", j=4, h=8)
        cmp_ = big2[:].bitcast(F32)[:, 0:2048]
        accC = ctmp[2][:].rearrange("p (j e) -> p j e", j=4)
        for jc in range(4):
            jlist = list(range(4 * jc, 4 * jc + 4))
            gcnt = {0: 0, 1: 0}
            for jj in range(4):
                j = 4 * jc + jj
                for h in range(8):
                    pb_ = (h % 2) * 64
                    def gmm(e, jj=jj, j=j, h=h, pb_=pb_):
                        qs = slice(j * 128, (j + 1) * 128)
                        par = h % 2
                        bank = PB[7] if par == 0 else PB[6]
                        c0_ = ((h // 2) * 4 + jj) * 8
                        gc_ = bank[:, c0_:c0_ + 8]
                        first = gcnt[par] == 0
                        gcnt[par] += 3
                        e.matmul(gc_, lhsT=qT[pb_:pb_ + 64, h // 2, qs], rhs=khi[pb_:pb_ + 64, h // 2, :],
                                 start=first, stop=False)
                        e.matmul(gc_, lhsT=qT[pb_:pb_ + 64, h // 2, qs], rhs=klo[pb_:pb_ + 64, h // 2, :],
                                 start=False, stop=False)
                        return e.matmul(gc_, lhsT=qlo[pb_:pb_ + 64, h // 2, qs], rhs=khi[pb_:pb_ + 64, h // 2, :],
                                        start=False, stop=(gcnt[par] == 48))
                    S_.op("pe", gmm, reads=["attn_in", "khl", "qlo"], writes=["pb7", "pb6"])
            stop("C1")
            stop("C1_%d" % jc)
            g2 = sqt[1][:, 0:256]
            s2 = sqt[0][:, 0:256]
            gp = PB[7][:, 0:256]
            P0 = list(g2.ap[0])
            Ps = list(s2.ap[0])

            def mk(i, jc=jc):
                a = selc[:, i, 4 * jc:4 * jc + 4, :]
                return bass.AP(a.tensor, a.offset, [list(a.ap[0]), [1, 8], [0, 8], [8, 4]])
            for par in range(2):
                gpb = (PB[7] if par == 0 else PB[6])[:, 0:128]
                in_ps = bass.AP(gpb.tensor, gpb.offset, [list(gpb.ap[0]), [1, 8], [32, 4], [8, 4]])
                out_g = bass.AP(g2.tensor, g2.offset + par * 4, [P0, [32, 8], [8, 4], [1, 4]])
                a_ = selc[:, 2, 4 * jc:4 * jc + 4, :]
                m2 = bass.AP(a_.tensor, a_.offset, [list(a_.ap[0]), [1, 8], [0, 4], [8, 4]])
                S_.op("dve", lambda e, in_ps=in_ps, out_g=out_g, m2=m2: e.tensor_tensor(out=out_g, in0=in_ps, in1=m2, op=ALU.add),
                      reads=["pb7", "pb6", "selc"], writes=["gte"])
            in0 = bass.AP(g2.tensor, g2.offset, [P0, [0, 8], [32, 8], [1, 32]])
            in1 = bass.AP(g2.tensor, g2.offset, [P0, [32, 8], [0, 8], [1, 32]])
            cmpv = cmp_.rearrange("p (n m a) -> p n m a", n=8, m=8)
            S_.op("dve", lambda e, in0=in0, in1=in1: e.tensor_tensor(out=cmpv, in0=in0, in1=in1, op=ALU.subtract),
                  reads=["gte"], writes=["cmp"])
            S_.op("dve", lambda e: e.tensor_scalar(out=cmp_, in0=cmp_, scalar1=1e20, scalar2=1.0, op0=ALU.mult, op1=ALU.min),
                  reads=["cmp"], writes=["cmp"])
            S_.op("dve", lambda e: e.tensor_scalar(out=cmp_, in0=cmp_, scalar1=0.0, scalar2=None, op0=ALU.max),
                  reads=["cmp"], writes=["cmp"])
            cin = bass.AP(cmp_.tensor, cmp_.offset, [list(cmp_.ap[0]), [256, 8], [1, 32], [32, 8]])
            s2na = bass.AP(s2.tensor, s2.offset, [Ps, [32, 8], [1, 32]])
            S_.op("dve", lambda e, cin=cin, s2na=s2na: e.reduce_sum(out=s2na, in_=cin, axis=AX.X), reads=["cmp"], writes=["selt"])
            S_.op("dve", lambda e: e.tensor_scalar(out=s2, in0=s2, scalar1=-1.0, scalar2=2.5, op0=ALU.mult, op1=ALU.add),
                  reads=["selt"], writes=["selt"])
            S_.op("dve", lambda e: e.tensor_scalar(out=s2, in0=s2, scalar1=1e20, scalar2=1.0, op0=ALU.mult, op1=ALU.min),
                  reads=["selt"], writes=["selt"])
            S_.op("dve", lambda e: e.tensor_scalar(out=s2, in0=s2, scalar1=0.0, scalar2=None, op0=ALU.max),
                  reads=["selt"], writes=["selt"])
            s2v = bass.AP(s2.tensor, s2.offset, [Ps, [32, 8], [4, 8], [1, 4]])
            S_.op("dve", lambda e, s2v=s2v, m0=mk(0): e.tensor_tensor(out=s2v, in0=s2v, in1=m0, op=ALU.mult),
                  reads=["selt", "selc"], writes=["selt"])
            S_.op("dve", lambda e, s2v=s2v, m1=mk(1): e.tensor_tensor(out=s2v, in0=s2v, in1=m1, op=ALU.add),
                  reads=["selt", "selc"], writes=["selt"])
            stop("C2")
            stop("C2_%d" % jc)
            for h in range(8):
                pb_ = (h % 2) * 64
                for nb in range(2 * jc + 2):
                    a0, a1, k0, k1 = accs[ai % 2]
                    ai += 1
                    js_n = [j for j in jlist if j // 2 >= nb]

                    touchedc = {}
                    totc = sum((1 if j == 2 * nb else 2) for j in js_n)

                    def pvc(j, ptap, ptk, h=h, a0=a0, k0=k0, nb=nb, touchedc=touchedc, totc=totc, ktc=[None]):
                        jj = j - 4 * jc
                        kt_ = ktc[0]
                        first = k0 not in touchedc
                        touchedc[k0] = touchedc.get(k0, 0) + 1
                        last = touchedc[k0] == totc
                        S_.op("pe", lambda e: e.matmul(a0[:, jj * 128:jj * 128 + 65], lhsT=ptap, rhs=vC[:, kt_, h, :],
                                                      start=first, stop=last),
                              reads=[ptk, "attn_in"], writes=[k0])
                    for kt in (2 * nb, 2 * nb + 1):
                        pvc.__defaults__[-1][0] = kt
                        js = [j for j in js_n if j >= kt]
                        if not js:
                            continue
                        attn_tile(12 + h, kT[pb_:pb_ + 64, h // 2, kt * 128:(kt + 1) * 128],
                                  lambda q0, n, h=h, pb_=pb_: qT[pb_:pb_ + 64, h // 2, q0:q0 + n], kt, js, pvc)
                    for j in js_n:
                        jj = j - 4 * jc
                        if nb == 0:
                            S_.op("dve", lambda e, jj=jj, h=h, a0=a0, nb=nb: e.tensor_scalar(
                                out=accC[:, jj, 0:65], in0=a0[:, jj * 128:jj * 128 + 65], scalar1=sqt[0][:, nb * 32 + h * 4 + jj:nb * 32 + h * 4 + jj + 1],
                                scalar2=None, op0=ALU.mult), reads=[k0, "selt"], writes=["accC"])
                        else:
                            S_.op("dve", lambda e, jj=jj, h=h, a0=a0, nb=nb: e.scalar_tensor_tensor(
                                out=accC[:, jj, 0:65], in0=a0[:, jj * 128:jj * 128 + 65], scalar=sqt[0][:, nb * 32 + h * 4 + jj:nb * 32 + h * 4 + jj + 1],
                                in1=accC[:, jj, 0:65], op0=ALU.mult, op1=ALU.add), reads=[k0, "selt", "accC"], writes=["accC"])
                    stop("C3")
                for jj in range(4):
                    S_.op("dve", lambda e, jj=jj: e.reciprocal(out=small[:, 16 + jj:17 + jj], in_=accC[:, jj, 64:65]),
                          reads=["accC"], writes=["rden"])
                    S_.op("dve", lambda e, jj=jj, h=h: e.tensor_scalar(
                        out=ytok[:, jj, h * 64:(h + 1) * 64], in0=accC[:, jj, 0:64], scalar1=small[:, 16 + jj:17 + jj],
                        scalar2=None, op0=ALU.mult), reads=["accC", "rden"], writes=["ytok"])
                stop("C4")
                stop("C4_%d_%d" % (jc, h))
            flush_y(jc, 512, 1024)
            stop("C5")

        stop("C")
        S_.barrier()
        yTt = bigv[:, 0:6144].rearrange("p (c t) -> p c t", c=12)
        gts = bigv[:, 6144:12288].rearrange("p (b c t) -> p b c t", b=3, c=4)
        mixT = bigv[:, 12288:20480].rearrange("p (c t) -> p c t", c=16)
        actT = bigv[:, 0:44 * TW].rearrange("p (c t) -> p c t", c=44)
        for tt in range(NT):
            t0 = tt * TW
            ld(xt[:], x_src.ap()[:, t0:t0 + TW].rearrange("(c p) t -> p c t", p=128), "xt", reads=[tokkey("x", tt)])
            ld(yTt, yT.ap()[:, t0:t0 + TW].rearrange("(c p) t -> p c t", p=128), "yTt", reads=["yT", "attn_in", "ytok"])
            for cg in range(4):
                for br in range(3):
                    r0 = br * 2048 + cg * 512
                    ld(gts[:, br, :, :], gT.ap()[r0:r0 + 512, t0:t0 + TW].rearrange("(c p) t -> p c t", p=128), "gts",
                       reads=allk("gT") + ["attn_in"])
                wt, wk = wslot()
                wv = wt[:, 0:6144].rearrange("p (b k n) -> p b k n", b=3, k=4)
                for br in range(3):
                    S_.dma("pool", lambda e, br=br, wv=wv: e.dma_start(
                        out=wv[:, br, :, :], in_=w_o[br].ap()[l, :, cg * 512:(cg + 1) * 512].rearrange("(k p) n -> p k n", p=128)),
                        wk, writes=[wk])
                for ch in range(4):
                    pss = []
                    for br in range(3):
                        ps, pk = next_pb()

                        def mm(e, br=br, ch=ch, ps=ps, wv=wv):
                            last = None
                            for k in range(4):
                                last = e.matmul(ps[:], lhsT=wv[:, br, k, ch * 128:(ch + 1) * 128], rhs=yTt[:, br * 4 + k, :],
                                                start=(k == 0), stop=(k == 3))
                            return last
                        S_.op("pe", mm, reads=[wk, "yTt"], writes=[pk])
                        pss.append((ps, pk))
                    t_a, t_b = ctmp[0], ctmp[1]
                    S_.op("dve", lambda e, ch=ch, pss=pss: e.tensor_tensor(out=t_a[:], in0=pss[0][0][:], in1=gts[:, 0, ch, :], op=ALU.mult),
                          reads=[pss[0][1], "gts"], writes=["t_a"])
                    S_.op("dve", lambda e, ch=ch, pss=pss: e.tensor_tensor(out=t_b[:], in0=pss[1][0][:], in1=gts[:, 1, ch, :], op=ALU.mult),
                          reads=[pss[1][1], "gts"], writes=["t_b"])
                    S_.op("dve", lambda e: e.tensor_tensor(out=t_a[:], in0=t_a[:], in1=t_b[:], op=ALU.add),
                          reads=["t_a", "t_b"], writes=["t_a"])
                    S_.op("dve", lambda e, ch=ch, pss=pss: e.tensor_tensor(out=t_b[:], in0=pss[2][0][:], in1=gts[:, 2, ch, :], op=ALU.mult),
                          reads=[pss[2][1], "gts", "t_b"], writes=["t_b"])
                    S_.op("dve", lambda e, ch=ch: e.tensor_tensor(out=mixT[:, cg * 4 + ch, :], in0=t_a[:], in1=t_b[:], op=ALU.add),
                          reads=["t_a", "t_b"], writes=["mixT"])
            for cg in range(4):
                wv, wk = load_w(w_out.ap()[l, :, cg * 512:(cg + 1) * 512], 16, 512)

                def cons(j, ps, pk, cg=cg):
                    S_.op("act", lambda e: e.activation(out=oT[:, cg * 4 + j, :], in_=ps[:], func=AF.Identity),
                          reads=[pk], writes=["oT"])
                gemm_fm(wv, wk, 16, 4, lambda k: mixT[:, k, :], ["mixT"], cons)
            post_norm_residual(l, PC_POSTMIX)
            make_h(l, PC_PREFFN)
            for i4 in range(11):
                wvg, wkg = load_w(w_up.ap()[l, :, i4 * 512:(i4 + 1) * 512], 16, 512)
                wvv, wkv = load_w(w_up.ap()[l, :, DFF + i4 * 512:DFF + (i4 + 1) * 512], 16, 512)
                for j in range(4):
                    fi = i4 * 4 + j
                    cres = []
                    for which, (wv, wk) in enumerate(((wvg, wkg), (wvv, wkv))):
                        fch = fi + 44 * which
                        ps, pk = next_pb()

                        def mm(e, j=j, ps=ps, wv=wv):
                            last = None
                            for k in range(16):
                                last = e.matmul(ps[:], lhsT=wv[:, k, j * 128:(j + 1) * 128], rhs=hT[:, k, :],
                                                start=(k == 0), stop=(k == 15))
                            return last
                        S_.op("pe", mm, reads=[wk, "hT"], writes=[pk])
                        ub, uk = ubuf[which], "ubuf%d" % which
                        S_.op("act", lambda e, ub=ub, fch=fch: e.activation(out=ub[:, 0:2], in_=halo[:, fch, :], func=AF.Identity),
                              reads=["halo"], writes=[uk])
                        S_.op("act", lambda e, ub=ub, ps=ps: e.activation(out=ub[:, 2:TW + 2], in_=ps[:], func=AF.Identity),
                              reads=[pk], writes=[uk])
                        S_.op("act", lambda e, ub=ub, fch=fch: e.activation(out=halo[:, fch, :], in_=ub[:, TW:TW + 2], func=AF.Identity),
                              reads=[uk], writes=["halo"])
                        ct, ck = ctmp[which], "ctmp%d" % which
                        eng = "dve"
                        cw = PC_CW
                        S_.op(eng, lambda e, ub=ub, ct=ct, fch=fch: e.tensor_scalar(
                            out=ct[:], in0=ub[:, 2:TW + 2], scalar1=pcol[:, 0, cw + 176 + fch:cw + 177 + fch],
                            scalar2=pcol[:, 0, PC_CB + fch:PC_CB + fch + 1], op0=ALU.mult, op1=ALU.add),
                            reads=[uk, "pcol"], writes=[ck])
                        S_.op(eng, lambda e, ub=ub, ct=ct, fch=fch: e.scalar_tensor_tensor(
                            out=ct[:], in0=ub[:, 1:TW + 1], scalar=pcol[:, 0, cw + 88 + fch:cw + 89 + fch], in1=ct[:],
                            op0=ALU.mult, op1=ALU.add), reads=[uk, ck], writes=[ck])
                        S_.op(eng, lambda e, ub=ub, ct=ct, fch=fch: e.scalar_tensor_tensor(
                            out=ct[:], in0=ub[:, 0:TW], scalar=pcol[:, 0, cw + fch:cw + fch + 1], in1=ct[:],
                            op0=ALU.mult, op1=ALU.add), reads=[uk, ck], writes=[ck])
                        cres.append((ct, ck))
                    S_.op("act", lambda e, cres=cres: e.activation(out=cres[0][0][:], in_=cres[0][0][:], func=AF.Gelu_apprx_tanh),
                          reads=[cres[0][1]], writes=[cres[0][1]])
                    S_.op("dve", lambda e, cres=cres, fi=fi: e.tensor_tensor(out=actT[:, fi, :], in0=cres[0][0][:], in1=cres[1][0][:],
                                                                            op=ALU.mult),
                          reads=[cres[0][1], cres[1][1]], writes=["actT"])
            for cg in range(16):
                wv, wk = load_w(w_down.ap()[l, :, cg * 128:(cg + 1) * 128], 44, 128)

                def cons(j, ps, pk, cg=cg):
                    S_.op("act", lambda e: e.activation(out=oT[:, cg, :], in_=ps[:], func=AF.Identity),
                          reads=[pk], writes=["oT"])
                gemm_fm(wv, wk, 44, 1, lambda k: actT[:, k, :], ["actT"], cons)
            post_norm_residual(l, PC_POSTFFN)
            S_.dma("sp", lambda e, t0=t0: e.dma_start(out=out.ap()[:, t0:t0 + TW].rearrange("(c p) t -> p c t", p=128), in_=xt[:]),
                   "outw", reads=["xt"], writes=[tokkey("x", tt)])
        S_.barrier()
        S_.op("pool", lambda e: e.memset(halo[:], 0.0), reads=["halo"], writes=["halo"])
    except _Stop:
        pass
    S_.barrier()
    es.close()
    S_.close()
    return nc


_CACHE = {}


def host_inputs(inp, n_layers=DEPTH):
    f = lambda a: np.ascontiguousarray(np.asarray(a, dtype=np.float32))
    pc = np.zeros((DEPTH, 128, PC), np.float32)
    col = lambda v, n: f(v).reshape(DEPTH, n, 128).transpose(0, 2, 1)
    pc[:, :, PC_PREMIX:PC_PREMIX + 16] = col(inp["pre_mix_g"], 16)
    pc[:, :, PC_POSTMIX:PC_POSTMIX + 16] = col(inp["post_mix_g"], 16)
    pc[:, :, PC_PREFFN:PC_PREFFN + 16] = col(inp["pre_ffn_g"], 16)
    pc[:, :, PC_POSTFFN:PC_POSTFFN + 16] = col(inp["post_ffn_g"], 16)
    pc[:, :, PC_BGATE:PC_BGATE + 48] = col(inp["b_gate"], 48)
    cw = f(inp["conv_w"]).reshape(DEPTH, 3, 88, 128).transpose(0, 3, 1, 2).reshape(DEPTH, 128, 264)
    pc[:, :, PC_CW:PC_CW + 264] = cw
    pc[:, :, PC_CB:PC_CB + 88] = col(inp["conv_b"], 88)
    shared = dict(
        table=f(inp["rel_bias_table"]), w_in=f(inp["w_in"][:n_layers]), w_oa=f(inp["w_oa"][:n_layers]), w_ob=f(inp["w_ob"][:n_layers]),
        w_oc=f(inp["w_oc"][:n_layers]), w_out=f(inp["w_out"][:n_layers]), w_up=f(inp["w_up"][:n_layers]), w_down=f(inp["w_down"][:n_layers]),
        pcols=np.ascontiguousarray(pc.transpose(1, 0, 2)),
        lamv=np.ascontiguousarray(np.stack([f(inp["lam_q1"]), f(inp["lam_k1"]), f(inp["lam_q2"]), f(inp["lam_k2"])], 1)),
        subg=f(inp["diff_subln_g"]), sinks=f(inp["sinks"]))
    shared.update(host_consts())
    x = f(inp["x"])
    return [dict(shared, xT=np.ascontiguousarray(x[b].T)) for b in range(4)]


def kernel(**inputs):
    if "nc" not in _CACHE:
        _CACHE["nc"] = build()
    in_maps = host_inputs(inputs)
    res = run_bass_kernel_spmd(_CACHE["nc"], in_maps, core_ids=list(range(4)))
    return np.stack([np.ascontiguousarray(res.results[b]["out"].T) for b in range(4)], 0).astype(np.float32)
```

```python
import math
import numpy as np
import ml_dtypes
import concourse.bass as bass
import concourse.mybir as mybir
from concourse.bass_utils import run_bass_kernel_spmd

F32 = mybir.dt.float32
BF16 = mybir.dt.bfloat16
AF = mybir.ActivationFunctionType
ALU = mybir.AluOpType
AX = mybir.AxisListType

D = 2048
S = 2048
DEPTH = 4
INW = 9984
DFF = 5632
NT = 4
TW = 512
NQB = 16
EPS = 1e-6
NEG = -30000.0
PC_PREMIX, PC_POSTMIX, PC_PREFFN, PC_POSTFFN, PC_BGATE, PC_CW, PC_CB = 0, 16, 32, 48, 64, 112, 376
PC = 464


class Sched:
    ENGS = ("pe", "act", "dve", "pool", "sp")

    def __init__(self, nc):
        self.nc = nc
        self.eng = {"pe": nc.tensor, "act": nc.scalar, "dve": nc.vector,
                    "pool": nc.gpsimd, "sp": nc.sync}
        self.sem, self.cnt = {}, {}
        self.waited = {e: {} for e in self.ENGS}
        self.last_w, self.reads = {}, {}
        self._stack = []
        self.cur = {}
        self.gen = {}
        for e in self.ENGS:
            self._rot("E_" + e)

    def _mk(self, name):
        cm = self.nc.semaphore(name)
        h = cm.__enter__()
        self._stack.append(cm)
        self.sem[name] = h
        self.cnt[name] = 0

    LIMIT = 1500

    def _rot(self, key):
        g = self.gen.get(key, -1) + 1
        self.gen[key] = g
        name = "%s#%d" % (key, g)
        self._mk(name)
        self.cur[key] = name
        return name

    def close(self):
        for cm in reversed(self._stack):
            cm.__exit__(None, None, None)

    def _deps(self, reads, writes):
        ev = []
        for k in reads:
            if k in self.last_w:
                ev.append(self.last_w[k])
        for k in writes:
            if k in self.last_w:
                ev.append(self.last_w[k])
            ev.extend(self.reads.get(k, ()))
        return ev

    def _emit_waits(self, e, evs):
        need = {}
        for (s, v, src) in evs:
            if src == "pe" and e == "pe":
                continue
            if v > need.get(s, 0):
                need[s] = v
        for s, v in need.items():
            if self.waited[e].get(s, 0) >= v:
                continue
            self.eng[e].wait_ge(self.sem[s], v)
            self.waited[e][s] = v

    def _record(self, event, reads, writes):
        for k in reads:
            self.reads.setdefault(k, []).append(event)
        for k in writes:
            self.last_w[k] = event
            self.reads[k] = []

    def op(self, e, fn, reads=(), writes=()):
        self._emit_waits(e, self._deps(reads, writes))
        ins = fn(self.eng[e])
        s = self.cur["E_" + e]
        if self.cnt[s] + 1 > self.LIMIT:
            s = self._rot("E_" + e)
        self.cnt[s] += 1
        ins.then_inc(self.sem[s], 1)
        self._record((s, self.cnt[s], e), reads, writes)

    def dma(self, e, fn, semkey, reads=(), writes=()):
        key = "D_" + semkey
        if key not in self.cur:
            self._rot(key)
        self._emit_waits(e, self._deps(reads, writes))
        insl = fn(self.eng[e])
        if not isinstance(insl, (list, tuple)):
            insl = [insl]
        s = self.cur[key]
        if self.cnt[s] + 16 * len(insl) > self.LIMIT:
            s = self._rot(key)
        for ins in insl:
            ins.then_inc(self.sem[s], 16)
            self.cnt[s] += 16
        self._record((s, self.cnt[s], "dma"), reads, writes)

    def barrier(self):
        evs = [(s, c, "x") for s, c in self.cnt.items() if c > 0]
        for e in self.ENGS:
            self._emit_waits(e, evs)


def rel_bucket_np(n):
    n = np.maximum(n, 0)
    nf = np.maximum(n, 1).astype(np.float32)
    large = 16 + (np.log(nf / np.float32(16)) / np.float32(math.log(8.0)) * np.float32(16)).astype(np.int32)
    large = np.minimum(large, 31)
    return np.where(n < 16, n, large)


def host_consts():
    k = np.arange(128)[:, None]
    q = np.arange(128)[None, :]
    E = np.zeros((128, 32, 2, 128), np.float32)
    for dl in range(2):
        b = rel_bucket_np(q - k + 128 * dl)
        for bb in range(32):
            E[:, bb, dl, :] = (b == bb)
    mdiag = np.where(q >= k, 0.0, NEG).astype(np.float32)
    mnear = np.where(q < k, 0.0, NEG).astype(np.float32)
    masks = np.stack([mdiag, mnear], 1)
    ident = np.eye(128, dtype=np.float32)
    ones = np.ones((128, 128), np.float32)
    j = np.arange(16)[:, None]
    n = np.arange(8)[None, :]
    valid = (n < (j // 2)).astype(np.float32)
    own = (n == (j // 2)).astype(np.float32)
    def bc(a):
        return np.ascontiguousarray(np.broadcast_to(a[None, :, :], (128, 16, 8))).astype(np.float32)
    selc = np.stack([bc(valid), bc(own), bc((1.0 - valid) * -1e9)], 1)
    return dict(cE=E, cmask=masks, cident=ident.astype(ml_dtypes.bfloat16), cones=ones, cident32=ident, csel=selc)


class _Stop(Exception):
    pass


def build(n_layers=DEPTH, stop_after=None, dbg=False):
    nc = bass.Bass("TRN2", target_bir_lowering=False)
    dt_in = lambda name, shape, dt=F32: nc.dram_tensor(name, list(shape), dt, kind="ExternalInput")
    xT_in = dt_in("xT", [D, S])
    table = dt_in("table", [32, 20])
    w_in = dt_in("w_in", [n_layers, D, INW])
    w_o = [dt_in(n_, [n_layers, 512, D]) for n_ in ("w_oa", "w_ob", "w_oc")]
    w_out = dt_in("w_out", [n_layers, D, D])
    w_up = dt_in("w_up", [n_layers, D, 2 * DFF])
    w_down = dt_in("w_down", [n_layers, DFF, D])
    pcols = dt_in("pcols", [128, DEPTH, PC])
    lamv = dt_in("lamv", [DEPTH, 4, 64])
    subg = dt_in("subg", [DEPTH, 128])
    sinks = dt_in("sinks", [DEPTH, 8])
    cE = dt_in("cE", [128, 32, 2, 128])
    cmask = dt_in("cmask", [128, 2, 128])
    cident = dt_in("cident", [128, 128], BF16)
    cident32 = dt_in("cident32", [128, 128])
    cones = dt_in("cones", [128, 128])
    csel = dt_in("csel", [128, 3, 16, 8])
    out = nc.dram_tensor("out", [D, S], F32, kind="ExternalOutput")
    kd = "ExternalOutput" if dbg else "Internal"
    qkT = nc.dram_tensor("qkT", [2688, S], BF16, kind=kd)
    qc32 = nc.dram_tensor("qc32", [512, S], F32, kind=kd)
    vS = nc.dram_tensor("vS", [S, 1152], BF16, kind=kd)
    gT = nc.dram_tensor("gT", [6144, S], BF16, kind=kd)
    yT = nc.dram_tensor("yT", [1536, S], BF16, kind=kd)
    QK_ROW = {"qa": 0, "ka": 512, "qb": 1024, "kb": 1536, "qc": 1664, "kc": 2176}

    S_ = Sched(nc)
    import contextlib
    es = contextlib.ExitStack()
    sb = lambda name, shape, dt=F32: es.enter_context(nc.sbuf_tensor(name, list(shape), dt))
    PB = [es.enter_context(nc.psum_tensor("pb%d" % i, [128, 512], F32)) for i in range(8)]
    pcol = sb("pcol", [128, 1, PC])
    ident = sb("ident", [128, 128], BF16)
    ident32 = sb("ident32", [128, 128])
    ones32 = sb("ones32", [128, 128])
    biasT = sb("biasT", [128, 20, 2, 128])
    cfar = sb("cfar", [128, 20])
    selc = sb("selc", [128, 3, 16, 8])
    WS = [sb("ws%d" % i, [128, 8192], BF16) for i in range(2)]
    wsi = [0]
    xt = sb("xt", [128, 16, TW])
    rstd = sb("rstd", [128, TW])
    sqt = [sb("sqt%d" % i, [128, TW]) for i in range(2)]
    stg = [sb("stg%d" % i, [128, 4, TW], BF16) for i in range(2)]
    big = sb("big", [128, 27648], BF16)
    big2 = sb("big2", [128, 16384], BF16)
    hT = big2[:, 0:8192].rearrange("p (c t) -> p c t", c=16)
    halo = sb("halo", [128, 88, 2])
    ubuf = [sb("ubuf%d" % i, [128, TW + 2]) for i in range(2)]
    ctmp = [sb("ctmp%d" % i, [128, TW]) for i in range(3)]
    small = sb("small", [128, 64])
    lvt_t = sb("lvt_t", [128, 256])
    gsub_t = sb("gsub_t", [128, 128])
    kmean_t = sb("kmean_t", [128, 32])
    khl_t = sb("khl_t", [128, 64], BF16)

    def ld(dst, src, key, eng="sp", reads=()):
        S_.dma(eng, lambda e: e.dma_start(out=dst, in_=src), key, reads=reads, writes=[key])

    ld(ident[:], cident.ap(), "ident")
    ld(ident32[:], cident32.ap(), "ident32")
    ld(ones32[:], cones.ap(), "ones32")
    ld(selc[:], csel.ap(), "selc")
    ld(cfar[:], bass.AP(table, 31 * 20, [[0, 128], [1, 20]]), "cfar")
    S_.op("pool", lambda e: e.memset(halo[:], 0.0), writes=["halo"])

    Et = big[:].bitcast(F32)[:, 0:8192].rearrange("p (b d q) -> p b d q", b=32, d=2)
    tabbc = big[:].bitcast(F32)[:, 8192:8832]
    ld(Et, cE.ap(), "Et")
    ld(tabbc, bass.AP(table, 0, [[0, 128], [1, 640]]), "tabbc")
    ld(biasT[:, :, 0, :], bass.AP(cmask, 0, [[256, 128], [0, 20], [1, 128]]), "biasT")
    S_.op("pool", lambda e: e.memset(biasT[:, :, 1, :], 0.0), reads=[], writes=["biasT1"])
    ld(biasT[:, 4:12, 1, :], bass.AP(cmask, 128, [[256, 128], [0, 8], [1, 128]]), "biasT1", reads=["biasT1"])
    for h in range(20):
        for b in range(32):
            def f(e, h=h, b=b):
                return e.scalar_tensor_tensor(out=biasT[:, h, :, :], in0=Et[:, b, :, :],
                                              scalar=tabbc[:, b * 20 + h:b * 20 + h + 1],
                                              in1=biasT[:, h, :, :], op0=ALU.mult, op1=ALU.add)
            S_.op("dve", f, reads=["Et", "tabbc", "biasT", "biasT1", "bT%d" % h], writes=["bT%d" % h])
    S_.barrier()
    bias_keys = ["bT%d" % h for h in range(20)] + ["cfar"]

    def wslot():
        i = wsi[0]
        wsi[0] = (i + 1) % 2
        return WS[i], ["wq%d" % (2 * i), "wq%d" % (2 * i + 1)]

    wqi = [0]

    def load_ws(src_ap, kc, ncols):
        q = wqi[0]
        wqi[0] = (q + 1) % 4
        view = WS[q // 2][:, (q % 2) * 4096:(q % 2) * 4096 + kc * ncols].rearrange("p (k n) -> p k n", k=kc)
        S_.dma("pool", lambda e: e.dma_start(out=view, in_=src_ap.rearrange("(k p) n -> p k n", p=128)),
               "wq%d" % q, writes=["wq%d" % q])
        return view, ["wq%d" % q]

    pbi = [0]

    def next_pb(lo=0, hi=4):
        i = lo + pbi[0] % (hi - lo)
        pbi[0] += 1
        return PB[i], "pb%d" % i

    def load_w(src_ap, kc, ncols):
        wt, key = wslot()
        view = wt[:, 0:kc * ncols].rearrange("p (k n) -> p k n", k=kc)
        S_.dma("pool", lambda e: e.dma_start(out=view, in_=src_ap.rearrange("(k p) n -> p k n", p=128)),
               "ws%d" % (int(key[0][2:]) // 2), writes=key)
        return view, key

    def rms_stats(src_chunk, src_keys, nchunks=16):
        for c in range(nchunks):
            sq, sk = sqt[c % 2], "sqt%d" % (c % 2)
            S_.op("act", lambda e, c=c, sq=sq: e.activation(out=sq[:], in_=src_chunk(c), func=AF.Square),
                  reads=src_keys, writes=[sk])
            S_.op("pe", lambda e, c=c, sq=sq: e.matmul(PB[4][:], lhsT=ones32[:], rhs=sq[:], start=(c == 0),
                                                     stop=(c == nchunks - 1)),
                  reads=[sk, "ones32"], writes=["pb4"])
        S_.op("dve", lambda e: e.tensor_scalar(out=rstd[:], in0=PB[4][:], scalar1=1.0 / D, scalar2=EPS,
                                               op0=ALU.mult, op1=ALU.add), reads=["pb4"], writes=["rstd"])
        S_.op("act", lambda e: e.activation(out=rstd[:], in_=rstd[:], func=AF.Sqrt), reads=["rstd"], writes=["rstd"])
        S_.op("dve", lambda e: e.reciprocal(out=rstd[:], in_=rstd[:]), reads=["rstd"], writes=["rstd"])

    def make_h(l, gcol0, dst=None):
        if dst is None:
            dst = lambda c: hT[:, c, :]
        rms_stats(lambda c: xt[:, c, :], ["xt"])
        for c in range(16):
            S_.op("dve", lambda e, c=c: e.scalar_tensor_tensor(
                out=dst(c), in0=xt[:, c, :], scalar=pcol[:, 0, gcol0 + c:gcol0 + c + 1], in1=rstd[:],
                op0=ALU.mult, op1=ALU.mult), reads=["xt", "rstd", "pcol"], writes=["hT"])

    def gemm_fm(wview, wkey, kc, nch, act_chunk, act_keys, consumer):
        for j in range(nch):
            ps, pk = next_pb()

            def mm(e, j=j, ps=ps):
                last = None
                for k in range(kc):
                    last = e.matmul(ps[:], lhsT=wview[:, k, j * 128:(j + 1) * 128], rhs=act_chunk(k),
                                    start=(k == 0), stop=(k == kc - 1))
                return last
            S_.op("pe", mm, reads=wkey + act_keys, writes=[pk])
            consumer(j, ps, pk)

    oT = big2[:].bitcast(F32).rearrange("p (c t) -> p c t", c=16)

    def post_norm_residual(l, gcol0):
        rms_stats(lambda c: oT[:, c, :], ["oT"])
        for c in range(16):
            S_.op("dve", lambda e, c=c: e.scalar_tensor_tensor(
                out=oT[:, c, :], in0=oT[:, c, :], scalar=pcol[:, 0, gcol0 + c:gcol0 + c + 1], in1=rstd[:],
                op0=ALU.mult, op1=ALU.mult), reads=["oT", "rstd"], writes=["oT"])
            S_.op("dve", lambda e, c=c: e.tensor_tensor(out=xt[:, c, :], in0=xt[:, c, :], in1=oT[:, c, :],
                                                        op=ALU.add), reads=["oT", "xt"], writes=["xt"])

    def tokkey(name, tt):
        return "%s:%d" % (name, tt)

    def stop(tag):
        if stop_after == tag:
            raise _Stop()
    try:
      for l in range(n_layers):
        x_src = xT_in if l == 0 else out
        stop("setup")
        ld(pcol[:], pcols.ap()[:, l:l + 1, :], "pcol")
        hT2 = big2[:].rearrange("p (c t) -> p c t", c=16)
        groups = [("qa", 0, 512, "fm"), ("ka", 512, 512, "fm"), ("va", 1024, 512, "tm"),
                  ("qb", 1536, 512, "fm"), ("kb", 2048, 128, "fm"), ("vb", 2176, 128, "tm"),
                  ("qc", 2304, 512, "fm"), ("kc", 2816, 512, "fm"), ("vc", 3328, 512, "tm")]
        groups += [("g%d" % i, 3840 + i * 512, 512, "gate") for i in range(12)]
        vcol = {"va": 0, "vb": 512, "vc": 640}
        for tp in range(NT // 2):
            for half in range(2):
                tt = 2 * tp + half
                ld(xt[:], x_src.ap()[:, tt * TW:(tt + 1) * TW].rearrange("(c p) t -> p c t", p=128), "xt",
                   reads=[tokkey("x", tt)])
                make_h(l, PC_PREMIX, dst=lambda c, half=half: hT2[:, c, half * TW:(half + 1) * TW])
            for gi, (nm, c0, ncol, kind) in enumerate(groups):
                wv, wk = load_w(w_in.ap()[l, :, c0:c0 + ncol], 16, ncol)
                nch = ncol // 128
                for half in range(2):
                    tt = 2 * tp + half
                    t0 = tt * TW
                    hTh = hT2[:, :, half * TW:(half + 1) * TW]
                    sidx = (2 * gi + half) % 2
                    st, sk = stg[sidx], "stg%d" % sidx
                    if kind in ("fm", "gate"):
                        def cons(j, ps, pk, nm=nm, kind=kind, st=st, sk=sk, gi=gi, t0=t0, tt=tt):
                            if kind == "gate":
                                bc_ = PC_BGATE + (gi - 9) * 4 + j
                                S_.op("act", lambda e: e.activation(out=st[:, j, :], in_=ps[:], func=AF.Sigmoid,
                                                                    bias=pcol[:, 0, bc_:bc_ + 1], scale=1.0),
                                      reads=[pk, "pcol"], writes=[sk])
                            else:
                                S_.op("act", lambda e: e.activation(out=st[:, j, :], in_=ps[:], func=AF.Identity),
                                      reads=[pk], writes=[sk])
                                if nm == "qc":
                                    S_.op("act", lambda e: e.activation(out=ctmp[2][:], in_=ps[:], func=AF.Identity),
                                          reads=[pk], writes=["ctmp2"])
                                    d2 = qc32.ap()[j * 128:(j + 1) * 128, t0:t0 + TW]
                                    S_.dma("sp", lambda e, d2=d2: e.dma_start(out=d2, in_=ctmp[2][:]), "qc32w",
                                           reads=["ctmp2"], writes=[tokkey("qc32", tt)])
                        gemm_fm(wv, wk, 16, nch, lambda k, hTh=hTh: hTh[:, k, :], ["hT"], cons)
                        if kind == "gate":
                            r0 = (gi - 9) * 512
                            dst = gT.ap()[r0:r0 + 512, t0:t0 + TW].rearrange("(j p) t -> p j t", p=128)
                            S_.dma("sp", lambda e, dst=dst, st=st: e.dma_start(out=dst, in_=st[:]), "stw%d" % sidx,
                                   reads=[sk], writes=[tokkey("gT", tt)])
                        else:
                            r0 = QK_ROW[nm]
                            dst = qkT.ap()[r0:r0 + ncol, t0:t0 + TW].rearrange("(j p) t -> p j t", p=128)
                            S_.dma("sp", lambda e, dst=dst, st=st, nch=nch: e.dma_start(out=dst, in_=st[:, 0:nch, :]),
                                   "stw%d" % sidx, reads=[sk], writes=[tokkey("qkT", tt)])
                    else:
                        stv = st[:].rearrange("p j t -> p (j t)")[:, 0:4 * ncol].rearrange("p (i n) -> p i n", i=4)
                        for i in range(4):
                            ps, pk = next_pb()

                            def mm(e, i=i, ps=ps, wv=wv, ncol=ncol, hTh=hTh):
                                last = None
                                for k in range(16):
                                    last = e.matmul(ps[:, 0:ncol], lhsT=hTh[:, k, i * 128:(i + 1) * 128], rhs=wv[:, k, :],
                                                    start=(k == 0), stop=(k == 15))
                                return last
                            S_.op("pe", mm, reads=wk + ["hT"], writes=[pk])
                            S_.op("act", lambda e, i=i, ps=ps, ncol=ncol, stv=stv: e.activation(
                                out=stv[:, i, :], in_=ps[:, 0:ncol], func=AF.Identity), reads=[pk], writes=[sk])
                        v0 = vcol[nm]
                        dst = vS.ap()[t0:t0 + TW, v0:v0 + ncol].rearrange("(i p) n -> p i n", p=128)
                        S_.dma("sp", lambda e, dst=dst, stv=stv: e.dma_start(out=dst, in_=stv), "stw%d" % sidx,
                               reads=[sk], writes=[tokkey("vS", tt)])

        stop("S1")
        S_.barrier()
        allk = lambda nm: [tokkey(nm, tt) for tt in range(NT)]
        bigv = big[:]
        qT = bigv[:, 0:8192].rearrange("p (c t) -> p c t", c=4)
        kT = bigv[:, 8192:16384].rearrange("p (c t) -> p c t", c=4)
        vA = bigv[:, 16384:16384 + 16 * 4 * 129].rearrange("p (b h e) -> p b h e", b=16, h=4)
        vC = bigv[:, 16384:16384 + 16 * 8 * 65].rearrange("p (b h e) -> p b h e", b=16, h=8)
        q32 = xt[:].rearrange("p c t -> p (c t)").rearrange("p (c t) -> p c t", c=4)
        ytok = bigv[:, 24704:24704 + 2048].rearrange("p (j n) -> p j n", j=4)
        PT = [stg[0][:].rearrange("p j t -> p (j t)")[:, i * 512:(i + 1) * 512] for i in range(4)]
        PTK = ["PT%d" % i for i in range(4)]
        pti = [0]
        tmp32 = ctmp[0]
        accs = [(PB[2], PB[3], "pb2", "pb3"), (PB[4], PB[5], "pb4", "pb5")]
        ysT = stg[1][:].rearrange("p j t -> p (j t)")

        def attn_tile(hh, kt_ap, q_ap_fn, kt, jlist, pv_fn, scale_bias=True, swa=False):
            groups_ = []
            far = [j for j in jlist if j - kt >= 2]
            if far:
                groups_.append(("far", far))
            if kt + 1 in jlist:
                groups_.append(("near", [kt + 1]))
            if kt in jlist:
                groups_.append(("diag", [kt]))
            for kind, js in groups_:
                n = len(js) * 128
                sps, spk = (PB[0], "pb0") if pti[0] % 2 == 0 else (PB[1], "pb1")
                pt, ptk = PT[pti[0] % 4], PTK[pti[0] % 4]
                pti[0] += 1
                S_.op("pe", lambda e, js=js, n=n, sps=sps: e.matmul(sps[:, 0:n], lhsT=kt_ap, rhs=q_ap_fn(js[0] * 128, n),
                                                                  start=True, stop=True),
                      reads=["attn_in"], writes=[spk])
                if kind == "far":
                    S_.op("act", lambda e, n=n, sps=sps, pt=pt: e.activation(out=pt[:, 0:n], in_=sps[:, 0:n], func=AF.Exp,
                                                                            bias=cfar[:, hh:hh + 1], scale=0.125),
                          reads=[spk, "cfar"], writes=[ptk])
                else:
                    dl = 1 if kind == "near" else 0
                    S_.op("dve", lambda e, sps=sps, dl=dl: e.scalar_tensor_tensor(
                        out=tmp32[:, 0:128], in0=sps[:, 0:128], scalar=0.125, in1=biasT[:, hh, dl, :],
                        op0=ALU.mult, op1=ALU.add), reads=[spk] + bias_keys, writes=["tmp32"])
                    S_.op("act", lambda e, pt=pt: e.activation(out=pt[:, 0:128], in_=tmp32[:, 0:128], func=AF.Exp),
                          reads=["tmp32"], writes=[ptk])
                for ji, j in enumerate(js):
                    pv_fn(j, pt[:, ji * 128:(ji + 1) * 128], ptk)

        lam_init = 0.8 - 0.6 * math.exp(-0.3 * l)
        lamt = small[:, 0:8]
        lvt = lvt_t[:].rearrange("p (a b) -> p a b", a=4)
        ld(lvt, bass.AP(lamv, l * 256, [[0, 128], [64, 4], [1, 64]]), "lvt")
        gsub = gsub_t[:]
        ld(gsub, bass.AP(subg, l * 128, [[0, 128], [1, 128]]), "gsub")
        esink = small[:, 8:16]
        ld(esink, bass.AP(sinks, l * 8, [[0, 128], [1, 8]]), "esink")
        S_.op("dve", lambda e: e.tensor_tensor(out=lvt[:, 0, :], in0=lvt[:, 0, :], in1=lvt[:, 1, :], op=ALU.mult),
              reads=["lvt"], writes=["lvt"])
        S_.op("dve", lambda e: e.tensor_tensor(out=lvt[:, 2, :], in0=lvt[:, 2, :], in1=lvt[:, 3, :], op=ALU.mult),
              reads=["lvt"], writes=["lvt"])
        S_.op("dve", lambda e: e.reduce_sum(out=lamt[:, 0:1], in_=lvt[:, 0, :], axis=AX.X), reads=["lvt"], writes=["lamt"])
        S_.op("dve", lambda e: e.reduce_sum(out=lamt[:, 1:2], in_=lvt[:, 2, :], axis=AX.X), reads=["lvt"], writes=["lamt"])
        S_.op("act", lambda e: e.activation(out=lamt[:, 2:4], in_=lamt[:, 0:2], func=AF.Exp), reads=["lamt"], writes=["lamt"])
        S_.op("dve", lambda e: e.scalar_tensor_tensor(out=lamt[:, 4:5], in0=lamt[:, 3:4], scalar=-lam_init, in1=lamt[:, 2:3],
                                                      op0=ALU.add, op1=ALU.subtract), reads=["lamt"], writes=["lamt"])
        S_.op("dve", lambda e: e.tensor_scalar(out=gsub, in0=gsub, scalar1=(1.0 - lam_init), scalar2=None, op0=ALU.mult),
              reads=["gsub"], writes=["gsub"])
        S_.op("act", lambda e: e.activation(out=esink, in_=esink, func=AF.Exp), reads=["esink"], writes=["esink"])

        def flush_y(jc, ncols_used, row0):
            nchk = ncols_used // 128
            for c in range(nchk):
                tp = PB[6][:].bitcast(BF16)[:, 0:512]
                for jj in range(4):
                    S_.op("pe", lambda e, c=c, jj=jj: e.transpose(tp[:, jj * 128:(jj + 1) * 128],
                                                                 ytok[:, jj, c * 128:(c + 1) * 128], ident[:]),
                          reads=["ytok", "ident"], writes=["pb6"])
                S_.op("act", lambda e, c=c: e.activation(out=ysT[:, c * 512:(c + 1) * 512], in_=tp, func=AF.Identity),
                      reads=["pb6"], writes=["ysT"])
            dst = yT.ap()[row0:row0 + ncols_used, jc * 512:(jc + 1) * 512].rearrange("(c p) t -> p c t", p=128)
            S_.dma("sp", lambda e: e.dma_start(out=dst, in_=ysT[:, 0:nchk * 512].rearrange("p (c t) -> p c t", c=nchk)),
                   "yTw", reads=["ysT"], writes=["yT"])

        ld(qT, qkT.ap()[0:512, :].rearrange("(c p) t -> p c t", p=128), "attn_in", reads=allk("qkT"))
        ld(kT, qkT.ap()[512:1024, :].rearrange("(c p) t -> p c t", p=128), "attn_in", reads=allk("qkT"))
        S_.op("pool", lambda e: e.memset(vA[:, :, :, 128:129], 1.0), writes=["attn_in"])
        for h_ in range(4):
            ld(vA[:, :, h_, 0:128], vS.ap()[:, h_ * 128:(h_ + 1) * 128].rearrange("(b p) e -> p b e", p=128), "attn_in",
               reads=allk("vS"))
        omt = ctmp[2][:].rearrange("p (j e) -> p j e", j=4)
        om1 = ctmp[1][:].rearrange("p (j e) -> p j e", j=4)
        ai = 0
        for jc in range(4):
            jlist = list(range(4 * jc, 4 * jc + 4))
            for h in range(4):
                for m in range(2):
                    a0, a1, k0, k1 = accs[ai % 2]
                    ai += 1
                    nkt = 4 * jc + 4

                    touched = {}
                    tot = {k0: 8 * jc + 3, k1: 8 * jc + 7}

                    def pv(j, ptap, ptk, h=h, a0=a0, a1=a1, k0=k0, k1=k1, touched=touched, tot=tot, ktc=[None]):
                        jj = j - 4 * jc
                        bank, bk = (a0, k0) if jj < 2 else (a1, k1)
                        o0 = (jj % 2) * 256
                        kt_ = ktc[0]
                        st_ = bk not in touched
                        touched[bk] = touched.get(bk, 0) + 1
                        sp_ = touched[bk] == tot[bk]
                        S_.op("pe", lambda e: e.matmul(bank[:, o0:o0 + 129], lhsT=ptap, rhs=vA[:, kt_, h, :],
                                                      start=st_, stop=sp_),
                              reads=[ptk, "attn_in"], writes=[bk])
                    for kt in range(nkt):
                        pv.__defaults__[-1][0] = kt
                        attn_tile(h, kT[m * 64:(m + 1) * 64, h, kt * 128:(kt + 1) * 128],
                                  lambda q0, n, m=m, h=h: qT[m * 64:(m + 1) * 64, h, q0:q0 + n], kt, jlist, pv)
                    dstm = omt if m == 0 else om1
                    for jj in range(4):
                        bank, bk = (a0, k0) if jj < 2 else (a1, k1)
                        o0 = (jj % 2) * 256
                        S_.op("dve", lambda e, bank=bank, o0=o0, jj=jj: e.reciprocal(out=small[:, 16 + jj:17 + jj],
                                                                                    in_=bank[:, o0 + 128:o0 + 129]),
                              reads=[bk], writes=["rden"])
                        S_.op("dve", lambda e, bank=bank, o0=o0, jj=jj, dstm=dstm: e.tensor_scalar(
                            out=dstm[:, jj, :], in0=bank[:, o0:o0 + 128], scalar1=small[:, 16 + jj:17 + jj], scalar2=None,
                            op0=ALU.mult), reads=[bk, "rden"], writes=["om%d" % m])
                S_.op("dve", lambda e: e.scalar_tensor_tensor(out=omt, in0=om1, scalar=lamt[:, 4:5], in1=omt,
                                                              op0=ALU.mult, op1=ALU.add),
                      reads=["om0", "om1", "lamt"], writes=["om0"])
                S_.op("dve", lambda e: e.tensor_tensor(out=om1, in0=omt, in1=omt, op=ALU.mult), reads=["om0"], writes=["om1"])
                S_.op("dve", lambda e: e.reduce_sum(out=small[:, 20:24], in_=om1, axis=AX.X), reads=["om1"], writes=["ssq"])
                S_.op("dve", lambda e: e.tensor_scalar(out=small[:, 20:24], in0=small[:, 20:24], scalar1=1.0 / 128, scalar2=EPS,
                                                       op0=ALU.mult, op1=ALU.add), reads=["ssq"], writes=["ssq"])
                S_.op("act", lambda e: e.activation(out=small[:, 20:24], in_=small[:, 20:24], func=AF.Sqrt), reads=["ssq"], writes=["ssq"])
                S_.op("dve", lambda e: e.reciprocal(out=small[:, 20:24], in_=small[:, 20:24]), reads=["ssq"], writes=["ssq"])
                for jj in range(4):
                    S_.op("dve", lambda e, jj=jj, h=h: e.scalar_tensor_tensor(
                        out=ytok[:, jj, h * 128:(h + 1) * 128], in0=omt[:, jj, :], scalar=small[:, 20 + jj:21 + jj],
                        in1=gsub, op0=ALU.mult, op1=ALU.mult), reads=["om0", "ssq", "gsub"], writes=["ytok"])
            flush_y(jc, 512, 0)

        stop("A")
        kB = kT[:, 0:2, :]
        ld(qT, qkT.ap()[1024:1536, :].rearrange("(c p) t -> p c t", p=128), "attn_in", reads=allk("qkT") + ["yT"])
        for g in range(2):
            for half in range(2):
                ld(kB[half * 64:(half + 1) * 64, g, :], qkT.ap()[1536 + g * 64:1536 + (g + 1) * 64, :], "attn_in")
        vB = vC[:, :, 0:2, :]
        S_.op("pool", lambda e: e.memset(vB[:, :, :, 64:65], 1.0), writes=["attn_in"])
        for h_ in range(2):
            ld(vB[:, :, h_, 0:64], vS.ap()[:, 512 + h_ * 64:512 + (h_ + 1) * 64].rearrange("(b p) e -> p b e", p=128), "attn_in")
        for jc in range(4):
            jlist = list(range(4 * jc, 4 * jc + 4))
            for h in range(8):
                a0, a1, k0, k1 = accs[ai % 2]
                ai += 1
                g = h // 4
                pb_ = (h % 2) * 64

                touchedb = {}
                totb = 7 if jc == 0 else 8

                def pvb(j, ptap, ptk, g=g, a0=a0, k0=k0, touchedb=touchedb, totb=totb, ktc=[None]):
                    jj = j - 4 * jc
                    kt_ = ktc[0]
                    first = k0 not in touchedb
                    touchedb[k0] = touchedb.get(k0, 0) + 1
                    lastb = touchedb[k0] == totb
                    S_.op("pe", lambda e: e.matmul(a0[:, jj * 128:jj * 128 + 65], lhsT=ptap, rhs=vB[:, kt_, g, :],
                                                  start=first, stop=lastb),
                          reads=[ptk, "attn_in"], writes=[k0])
                for kt in range(max(0, 4 * jc - 1), 4 * jc + 4):
                    pvb.__defaults__[-1][0] = kt
                    js = [j for j in jlist if j in (kt, kt + 1)]
                    attn_tile(4 + h, kB[pb_:pb_ + 64, g, kt * 128:(kt + 1) * 128],
                              lambda q0, n, h=h, pb_=pb_: qT[pb_:pb_ + 64, h // 2, q0:q0 + n], kt, js, pvb)
                for jj in range(4):
                    S_.op("dve", lambda e, jj=jj, h=h, a0=a0: e.tensor_scalar(
                        out=small[:, 16 + jj:17 + jj], in0=a0[:, jj * 128 + 64:jj * 128 + 65], scalar1=esink[:, h:h + 1],
                        scalar2=None, op0=ALU.add), reads=[k0, "esink"], writes=["rden"])
                    S_.op("dve", lambda e, jj=jj: e.reciprocal(out=small[:, 16 + jj:17 + jj], in_=small[:, 16 + jj:17 + jj]),
                          reads=["rden"], writes=["rden"])
                    S_.op("dve", lambda e, jj=jj, h=h, a0=a0: e.tensor_scalar(
                        out=ytok[:, jj, h * 64:(h + 1) * 64], in0=a0[:, jj * 128:jj * 128 + 64],
                        scalar1=small[:, 16 + jj:17 + jj], scalar2=None, op0=ALU.mult),
                        reads=[k0, "rden"], writes=["ytok"])
            flush_y(jc, 512, 512)

        stop("B")
        ld(qT, qkT.ap()[1664:2176, :].rearrange("(c p) t -> p c t", p=128), "attn_in", reads=allk("qkT") + ["yT"])
        ld(kT, qkT.ap()[2176:2688, :].rearrange("(c p) t -> p c t", p=128), "attn_in")
        ld(q32, qc32.ap().rearrange("(c p) t -> p c t", p=128), "attn_in", reads=allk("qc32"))
        S_.op("pool", lambda e: e.memset(vC[:, :, :, 64:65], 1.0), writes=["attn_in"])
        for h_ in range(8):
            ld(vC[:, :, h_, 0:64], vS.ap()[:, 640 + h_ * 64:640 + (h_ + 1) * 64].rearrange("(b p) e -> p b e", p=128), "attn_in")
        kmean = kmean_t[:].rearrange("p (c n) -> p c n", c=4)
        for c_ in range(4):
            S_.op("dve", lambda e, c_=c_: e.reduce_sum(out=kmean_t[:, c_ * 8:(c_ + 1) * 8],
                                                      in_=kT[:, c_, :].rearrange("p (n s) -> p n s", s=256), axis=AX.X),
                  reads=["attn_in"], writes=["kmean"])
        khi = khl_t[:, 0:32].rearrange("p (c n) -> p c n", c=4)
        klo = khl_t[:, 32:64].rearrange("p (c n) -> p c n", c=4)
        qlo = big2[:, 4096:4096 + 8192].rearrange("p (c t) -> p c t", c=4)
        S_.op("act", lambda e: e.activation(out=khl_t[:, 0:32], in_=kmean_t[:], func=AF.Identity), reads=["kmean"], writes=["khl"])
        S_.op("dve", lambda e: e.tensor_tensor(out=khl_t[:, 32:64], in0=kmean_t[:], in1=khl_t[:, 0:32], op=ALU.subtract),
              reads=["kmean", "khl"], writes=["khl"])
        for c_ in range(4):
            S_.op("dve", lambda e, c_=c_: e.tensor_tensor(out=qlo[:, c_, :], in0=q32[:, c_, :], in1=qT[:, c_, :], op=ALU.subtract),
                  reads=["attn_in"], writes=["qlo"])
        def selbc(i, jc):
            a = selc[:, i, 4 * jc:4 * jc + 4, :]
            return bass.AP(a.tensor, a.offset, [list(a.ap[0]), [8, 4], [0, 8], [1, 8]])
        stop("C0")
        selt = sqt[0][:, 0:256].rearrange("p (j h n) -> p j h n", j=4, h=8)
        gte = sqt[1][:, 0:256].rearrange("p (j h n) -> p j h n", j=4, h=8)
        cmp_ = big2[:].bitcast(F32)[:, 0:2048]
        accC = ctmp[2][:].rearrange("p (j e) -> p j e", j=4)
        for jc in range(4):
            jlist = list(range(4 * jc, 4 * jc + 4))
            gcnt = {0: 0, 1: 0}
            for jj in range(4):
                j = 4 * jc + jj
                for h in range(8):
                    pb_ = (h % 2) * 64
                    def gmm(e, jj=jj, j=j, h=h, pb_=pb_):
                        qs = slice(j * 128, (j + 1) * 128)
                        par = h % 2
                        bank = PB[7] if par == 0 else PB[6]
                        c0_ = ((h // 2) * 4 + jj) * 8
                        gc_ = bank[:, c0_:c0_ + 8]
                        first = gcnt[par] == 0
                        gcnt[par] += 3
                        e.matmul(gc_, lhsT=qT[pb_:pb_ + 64, h // 2, qs], rhs=khi[pb_:pb_ + 64, h // 2, :],
                                 start=first, stop=False)
                        e.matmul(gc_, lhsT=qT[pb_:pb_ + 64, h // 2, qs], rhs=klo[pb_:pb_ + 64, h // 2, :],
                                 start=False, stop=False)
                        return e.matmul(gc_, lhsT=qlo[pb_:pb_ + 64, h // 2, qs], rhs=khi[pb_:pb_ + 64, h // 2, :],
                                        start=False, stop=(gcnt[par] == 48))
                    S_.op("pe", gmm, reads=["attn_in", "khl", "qlo"], writes=["pb7", "pb6"])
            stop("C1")
            stop("C1_%d" % jc)
            g2 = sqt[1][:, 0:256]
            s2 = sqt[0][:, 0:256]
            gp = PB[7][:, 0:256]
            P0 = list(g2.ap[0])
            Ps = list(s2.ap[0])

            def mk(i, jc=jc):
                a = selc[:, i, 4 * jc:4 * jc + 4, :]
                return bass.AP(a.tensor, a.offset, [list(a.ap[0]), [1, 8], [0, 8], [8, 4]])
            for par in range(2):
                gpb = (PB[7] if par == 0 else PB[6])[:, 0:128]
                in_ps = bass.AP(gpb.tensor, gpb.offset, [list(gpb.ap[0]), [1, 8], [32, 4], [8, 4]])
                out_g = bass.AP(g2.tensor, g2.offset + par * 4, [P0, [32, 8], [8, 4], [1, 4]])
                a_ = selc[:, 2, 4 * jc:4 * jc + 4, :]
                m2 = bass.AP(a_.tensor, a_.offset, [list(a_.ap[0]), [1, 8], [0, 4], [8, 4]])
                S_.op("dve", lambda e, in_ps=in_ps, out_g=out_g, m2=m2: e.tensor_tensor(out=out_g, in0=in_ps, in1=m2, op=ALU.add),
                      reads=["pb7", "pb6", "selc"], writes=["gte"])
            in0 = bass.AP(g2.tensor, g2.offset, [P0, [0, 8], [32, 8], [1, 32]])
            in1 = bass.AP(g2.tensor, g2.offset, [P0, [32, 8], [0, 8], [1, 32]])
            cmpv = cmp_.rearrange("p (n m a) -> p n m a", n=8, m=8)
            S_.op("dve", lambda e, in0=in0, in1=in1: e.tensor_tensor(out=cmpv, in0=in0, in1=in1, op=ALU.subtract),
                  reads=["gte"], writes=["cmp"])
            S_.op("dve", lambda e: e.tensor_scalar(out=cmp_, in0=cmp_, scalar1=1e20, scalar2=1.0, op0=ALU.mult, op1=ALU.min),
                  reads=["cmp"], writes=["cmp"])
            S_.op("dve", lambda e: e.tensor_scalar(out=cmp_, in0=cmp_, scalar1=0.0, scalar2=None, op0=ALU.max),
                  reads=["cmp"], writes=["cmp"])
            cin = bass.AP(cmp_.tensor, cmp_.offset, [list(cmp_.ap[0]), [256, 8], [1, 32], [32, 8]])
            s2na = bass.AP(s2.tensor, s2.offset, [Ps, [32, 8], [1, 32]])
            S_.op("dve", lambda e, cin=cin, s2na=s2na: e.reduce_sum(out=s2na, in_=cin, axis=AX.X), reads=["cmp"], writes=["selt"])
            S_.op("dve", lambda e: e.tensor_scalar(out=s2, in0=s2, scalar1=-1.0, scalar2=2.5, op0=ALU.mult, op1=ALU.add),
                  reads=["selt"], writes=["selt"])
            S_.op("dve", lambda e: e.tensor_scalar(out=s2, in0=s2, scalar1=1e20, scalar2=1.0, op0=ALU.mult, op1=ALU.min),
                  reads=["selt"], writes=["selt"])
            S_.op("dve", lambda e: e.tensor_scalar(out=s2, in0=s2, scalar1=0.0, scalar2=None, op0=ALU.max),
                  reads=["selt"], writes=["selt"])
            s2v = bass.AP(s2.tensor, s2.offset, [Ps, [32, 8], [4, 8], [1, 4]])
            S_.op("dve", lambda e, s2v=s2v, m0=mk(0): e.tensor_tensor(out=s2v, in0=s2v, in1=m0, op=ALU.mult),
                  reads=["selt", "selc"], writes=["selt"])
            S_.op("dve", lambda e, s2v=s2v, m1=mk(1): e.tensor_tensor(out=s2v, in0=s2v, in1=m1, op=ALU.add),
                  reads=["selt", "selc"], writes=["selt"])
            stop("C2")
            stop("C2_%d" % jc)
            for h in range(8):
                pb_ = (h % 2) * 64
                for nb in range(2 * jc + 2):
                    a0, a1, k0, k1 = accs[ai % 2]
                    ai += 1
                    js_n = [j for j in jlist if j // 2 >= nb]

                    touchedc = {}
                    totc = sum((1 if j == 2 * nb else 2) for j in js_n)

                    def pvc(j, ptap, ptk, h=h, a0=a0, k0=k0, nb=nb, touchedc=touchedc, totc=totc, ktc=[None]):
                        jj = j - 4 * jc
                        kt_ = ktc[0]
                        first = k0 not in touchedc
                        touchedc[k0] = touchedc.get(k0, 0) + 1
                        last = touchedc[k0] == totc
                        S_.op("pe", lambda e: e.matmul(a0[:, jj * 128:jj * 128 + 65], lhsT=ptap, rhs=vC[:, kt_, h, :],
                                                      start=first, stop=last),
                              reads=[ptk, "attn_in"], writes=[k0])
                    for kt in (2 * nb, 2 * nb + 1):
                        pvc.__defaults__[-1][0] = kt
                        js = [j for j in js_n if j >= kt]
                        if not js:
                            continue
                        attn_tile(12 + h, kT[pb_:pb_ + 64, h // 2, kt * 128:(kt + 1) * 128],
                                  lambda q0, n, h=h, pb_=pb_: qT[pb_:pb_ + 64, h // 2, q0:q0 + n], kt, js, pvc)
                    for j in js_n:
                        jj = j - 4 * jc
                        if nb == 0:
                            S_.op("dve", lambda e, jj=jj, h=h, a0=a0, nb=nb: e.tensor_scalar(
                                out=accC[:, jj, 0:65], in0=a0[:, jj * 128:jj * 128 + 65], scalar1=sqt[0][:, nb * 32 + h * 4 + jj:nb * 32 + h * 4 + jj + 1],
                                scalar2=None, op0=ALU.mult), reads=[k0, "selt"], writes=["accC"])
                        else:
                            S_.op("dve", lambda e, jj=jj, h=h, a0=a0, nb=nb: e.scalar_tensor_tensor(
                                out=accC[:, jj, 0:65], in0=a0[:, jj * 128:jj * 128 + 65], scalar=sqt[0][:, nb * 32 + h * 4 + jj:nb * 32 + h * 4 + jj + 1],
                                in1=accC[:, jj, 0:65], op0=ALU.mult, op1=ALU.add), reads=[k0, "selt", "accC"], writes=["accC"])
                    stop("C3")
                for jj in range(4):
                    S_.op("dve", lambda e, jj=jj: e.reciprocal(out=small[:, 16 + jj:17 + jj], in_=accC[:, jj, 64:65]),
                          reads=["accC"], writes=["rden"])
                    S_.op("dve", lambda e, jj=jj, h=h: e.tensor_scalar(
                        out=ytok[:, jj, h * 64:(h + 1) * 64], in0=accC[:, jj, 0:64], scalar1=small[:, 16 + jj:17 + jj],
                        scalar2=None, op0=ALU.mult), reads=["accC", "rden"], writes=["ytok"])
                stop("C4")
                stop("C4_%d_%d" % (jc, h))
            flush_y(jc, 512, 1024)
            stop("C5")

        stop("C")
        S_.barrier()
        yTt = bigv[:, 0:6144].rearrange("p (c t) -> p c t", c=12)
        gts = bigv[:, 6144:12288].rearrange("p (b c t) -> p b c t", b=3, c=4)
        mixT = bigv[:, 12288:20480].rearrange("p (c t) -> p c t", c=16)
        actT = bigv[:, 0:44 * TW].rearrange("p (c t) -> p c t", c=44)
        for tt in range(NT):
            t0 = tt * TW
            ld(xt[:], x_src.ap()[:, t0:t0 + TW].rearrange("(c p) t -> p c t", p=128), "xt", reads=[tokkey("x", tt)])
            ld(yTt, yT.ap()[:, t0:t0 + TW].rearrange("(c p) t -> p c t", p=128), "yTt", reads=["yT", "attn_in", "ytok"])
            for cg in range(4):
                for br in range(3):
                    r0 = br * 2048 + cg * 512
                    ld(gts[:, br, :, :], gT.ap()[r0:r0 + 512, t0:t0 + TW].rearrange("(c p) t -> p c t", p=128), "gts",
                       reads=allk("gT") + ["attn_in"])
                wt, wk = wslot()
                wv = wt[:, 0:6144].rearrange("p (b k n) -> p b k n", b=3, k=4)
                for br in range(3):
                    S_.dma("pool", lambda e, br=br, wv=wv: e.dma_start(
                        out=wv[:, br, :, :], in_=w_o[br].ap()[l, :, cg * 512:(cg + 1) * 512].rearrange("(k p) n -> p k n", p=128)),
                        "ws%d" % (int(wk[0][2:]) // 2), writes=wk)
                for ch in range(4):
                    pss = []
                    for br in range(3):
                        ps, pk = next_pb()

                        def mm(e, br=br, ch=ch, ps=ps, wv=wv):
                            last = None
                            for k in range(4):
                                last = e.matmul(ps[:], lhsT=wv[:, br, k, ch * 128:(ch + 1) * 128], rhs=yTt[:, br * 4 + k, :],
                                                start=(k == 0), stop=(k == 3))
                            return last
                        S_.op("pe", mm, reads=wk + ["yTt"], writes=[pk])
                        pss.append((ps, pk))
                    t_a, t_b = ctmp[0], ctmp[1]
                    S_.op("dve", lambda e, ch=ch, pss=pss: e.tensor_tensor(out=t_a[:], in0=pss[0][0][:], in1=gts[:, 0, ch, :], op=ALU.mult),
                          reads=[pss[0][1], "gts"], writes=["t_a"])
                    S_.op("dve", lambda e, ch=ch, pss=pss: e.tensor_tensor(out=t_b[:], in0=pss[1][0][:], in1=gts[:, 1, ch, :], op=ALU.mult),
                          reads=[pss[1][1], "gts"], writes=["t_b"])
                    S_.op("dve", lambda e: e.tensor_tensor(out=t_a[:], in0=t_a[:], in1=t_b[:], op=ALU.add),
                          reads=["t_a", "t_b"], writes=["t_a"])
                    S_.op("dve", lambda e, ch=ch, pss=pss: e.tensor_tensor(out=t_b[:], in0=pss[2][0][:], in1=gts[:, 2, ch, :], op=ALU.mult),
                          reads=[pss[2][1], "gts", "t_b"], writes=["t_b"])
                    S_.op("dve", lambda e, ch=ch: e.tensor_tensor(out=mixT[:, cg * 4 + ch, :], in0=t_a[:], in1=t_b[:], op=ALU.add),
                          reads=["t_a", "t_b"], writes=["mixT"])
            for cg in range(4):
                wv, wk = load_w(w_out.ap()[l, :, cg * 512:(cg + 1) * 512], 16, 512)

                def cons(j, ps, pk, cg=cg):
                    S_.op("act", lambda e: e.activation(out=oT[:, cg * 4 + j, :], in_=ps[:], func=AF.Identity),
                          reads=[pk], writes=["oT"])
                gemm_fm(wv, wk, 16, 4, lambda k: mixT[:, k, :], ["mixT"], cons)
            post_norm_residual(l, PC_POSTMIX)
            make_h(l, PC_PREFFN)
            for i8 in range(22):
                wvg, wkg = load_ws(w_up.ap()[l, :, i8 * 256:(i8 + 1) * 256], 16, 256)
                wvv, wkv = load_ws(w_up.ap()[l, :, DFF + i8 * 256:DFF + (i8 + 1) * 256], 16, 256)
                for j in range(2):
                    fi = i8 * 2 + j
                    cres = []
                    for which, (wv, wk) in enumerate(((wvg, wkg), (wvv, wkv))):
                        fch = fi + 44 * which
                        ps, pk = next_pb()

                        def mm(e, j=j, ps=ps, wv=wv):
                            last = None
                            for k in range(16):
                                last = e.matmul(ps[:], lhsT=wv[:, k, j * 128:(j + 1) * 128], rhs=hT[:, k, :],
                                                start=(k == 0), stop=(k == 15))
                            return last
                        S_.op("pe", mm, reads=wk + ["hT"], writes=[pk])
                        ub, uk = ubuf[which], "ubuf%d" % which
                        S_.op("act", lambda e, ub=ub, fch=fch: e.activation(out=ub[:, 0:2], in_=halo[:, fch, :], func=AF.Identity),
                              reads=["halo"], writes=[uk])
                        S_.op("act", lambda e, ub=ub, ps=ps: e.activation(out=ub[:, 2:TW + 2], in_=ps[:], func=AF.Identity),
                              reads=[pk], writes=[uk])
                        S_.op("act", lambda e, ub=ub, fch=fch: e.activation(out=halo[:, fch, :], in_=ub[:, TW:TW + 2], func=AF.Identity),
                              reads=[uk], writes=["halo"])
                        ct, ck = ctmp[which], "ctmp%d" % which
                        eng = "dve"
                        cw = PC_CW
                        S_.op(eng, lambda e, ub=ub, ct=ct, fch=fch: e.tensor_scalar(
                            out=ct[:], in0=ub[:, 2:TW + 2], scalar1=pcol[:, 0, cw + 176 + fch:cw + 177 + fch],
                            scalar2=pcol[:, 0, PC_CB + fch:PC_CB + fch + 1], op0=ALU.mult, op1=ALU.add),
                            reads=[uk, "pcol"], writes=[ck])
                        S_.op(eng, lambda e, ub=ub, ct=ct, fch=fch: e.scalar_tensor_tensor(
                            out=ct[:], in0=ub[:, 1:TW + 1], scalar=pcol[:, 0, cw + 88 + fch:cw + 89 + fch], in1=ct[:],
                            op0=ALU.mult, op1=ALU.add), reads=[uk, ck], writes=[ck])
                        S_.op(eng, lambda e, ub=ub, ct=ct, fch=fch: e.scalar_tensor_tensor(
                            out=ct[:], in0=ub[:, 0:TW], scalar=pcol[:, 0, cw + fch:cw + fch + 1], in1=ct[:],
                            op0=ALU.mult, op1=ALU.add), reads=[uk, ck], writes=[ck])
                        cres.append((ct, ck))
                    S_.op("act", lambda e, cres=cres: e.activation(out=cres[0][0][:], in_=cres[0][0][:], func=AF.Gelu_apprx_tanh),
                          reads=[cres[0][1]], writes=[cres[0][1]])
                    S_.op("dve", lambda e, cres=cres, fi=fi: e.tensor_tensor(out=actT[:, fi, :], in0=cres[0][0][:], in1=cres[1][0][:],
                                                                            op=ALU.mult),
                          reads=[cres[0][1], cres[1][1]], writes=["actT"])
            for cg in range(8):
                wva, wka = load_w(w_down.ap()[l, 0:2816, cg * 256:(cg + 1) * 256], 22, 256)
                wvb, wkb = load_w(w_down.ap()[l, 2816:5632, cg * 256:(cg + 1) * 256], 22, 256)
                for j in range(2):
                    ps, pk = next_pb()

                    def mm(e, j=j, ps=ps, wva=wva, wvb=wvb):
                        last = None
                        for k in range(44):
                            wv_ = wva if k < 22 else wvb
                            last = e.matmul(ps[:], lhsT=wv_[:, k % 22, j * 128:(j + 1) * 128], rhs=actT[:, k, :],
                                            start=(k == 0), stop=(k == 43))
                        return last
                    S_.op("pe", mm, reads=wka + wkb + ["actT"], writes=[pk])
                    S_.op("act", lambda e, j=j, ps=ps, cg=cg: e.activation(out=oT[:, cg * 2 + j, :], in_=ps[:], func=AF.Identity),
                          reads=[pk], writes=["oT"])
            post_norm_residual(l, PC_POSTFFN)
            S_.dma("sp", lambda e, t0=t0: e.dma_start(out=out.ap()[:, t0:t0 + TW].rearrange("(c p) t -> p c t", p=128), in_=xt[:]),
                   "outw", reads=["xt"], writes=[tokkey("x", tt)])
        S_.barrier()
        S_.op("pool", lambda e: e.memset(halo[:], 0.0), reads=["halo"], writes=["halo"])
    except _Stop:
        pass
    S_.barrier()
    es.close()
    S_.close()
    return nc


_CACHE = {}


def host_inputs(inp, n_layers=DEPTH):
    f = lambda a: np.ascontiguousarray(np.asarray(a, dtype=np.float32))
    pc = np.zeros((DEPTH, 128, PC), np.float32)
    col = lambda v, n: f(v).reshape(DEPTH, n, 128).transpose(0, 2, 1)
    pc[:, :, PC_PREMIX:PC_PREMIX + 16] = col(inp["pre_mix_g"], 16)
    pc[:, :, PC_POSTMIX:PC_POSTMIX + 16] = col(inp["post_mix_g"], 16)
    pc[:, :, PC_PREFFN:PC_PREFFN + 16] = col(inp["pre_ffn_g"], 16)
    pc[:, :, PC_POSTFFN:PC_POSTFFN + 16] = col(inp["post_ffn_g"], 16)
    pc[:, :, PC_BGATE:PC_BGATE + 48] = col(inp["b_gate"], 48)
    cw = f(inp["conv_w"]).reshape(DEPTH, 3, 88, 128).transpose(0, 3, 1, 2).reshape(DEPTH, 128, 264)
    pc[:, :, PC_CW:PC_CW + 264] = cw
    pc[:, :, PC_CB:PC_CB + 88] = col(inp["conv_b"], 88)
    shared = dict(
        table=f(inp["rel_bias_table"]), w_in=f(inp["w_in"][:n_layers]), w_oa=f(inp["w_oa"][:n_layers]), w_ob=f(inp["w_ob"][:n_layers]),
        w_oc=f(inp["w_oc"][:n_layers]), w_out=f(inp["w_out"][:n_layers]), w_up=f(inp["w_up"][:n_layers]), w_down=f(inp["w_down"][:n_layers]),
        pcols=np.ascontiguousarray(pc.transpose(1, 0, 2)),
        lamv=np.ascontiguousarray(np.stack([f(inp["lam_q1"]), f(inp["lam_k1"]), f(inp["lam_q2"]), f(inp["lam_k2"])], 1)),
        subg=f(inp["diff_subln_g"]), sinks=f(inp["sinks"]))
    shared.update(host_consts())
    x = f(inp["x"])
    return [dict(shared, xT=np.ascontiguousarray(x[b].T)) for b in range(4)]


def kernel(**inputs):
    if "nc" not in _CACHE:
        _CACHE["nc"] = build()
    in_maps = host_inputs(inputs)
    res = run_bass_kernel_spmd(_CACHE["nc"], in_maps, core_ids=list(range(4)))
    return np.stack([np.ascontiguousarray(res.results[b]["out"].T) for b in range(4)], 0).astype(np.float32)
```

```python
import math
import numpy as np
import ml_dtypes
import concourse.bass as bass
import concourse.mybir as mybir
from concourse.bass_utils import run_bass_kernel_spmd

F32 = mybir.dt.float32
BF16 = mybir.dt.bfloat16
AF = mybir.ActivationFunctionType
ALU = mybir.AluOpType
AX = mybir.AxisListType

D = 2048
S = 2048
DEPTH = 4
INW = 9984
DFF = 5632
NT = 4
TW = 512
NQB = 16
EPS = 1e-6
NEG = -30000.0
PC_PREMIX, PC_POSTMIX, PC_PREFFN, PC_POSTFFN, PC_BGATE, PC_CW, PC_CB = 0, 16, 32, 48, 64, 112, 376
PC = 464


class Sched:
    ENGS = ("pe", "act", "dve", "pool", "sp")

    def __init__(self, nc):
        self.nc = nc
        self.eng = {"pe": nc.tensor, "act": nc.scalar, "dve": nc.vector,
                    "pool": nc.gpsimd, "sp": nc.sync}
        self.sem, self.cnt = {}, {}
        self.waited = {e: {} for e in self.ENGS}
        self.last_w, self.reads = {}, {}
        self._stack = []
        self.cur = {}
        self.gen = {}
        for e in self.ENGS:
            self._rot("E_" + e)

    def _mk(self, name):
        cm = self.nc.semaphore(name)
        h = cm.__enter__()
        self._stack.append(cm)
        self.sem[name] = h
        self.cnt[name] = 0

    LIMIT = 1500

    def _rot(self, key):
        g = self.gen.get(key, -1) + 1
        self.gen[key] = g
        name = "%s#%d" % (key, g)
        self._mk(name)
        self.cur[key] = name
        return name

    def close(self):
        for cm in reversed(self._stack):
            cm.__exit__(None, None, None)

    def _deps(self, reads, writes):
        ev = []
        for k in reads:
            if k in self.last_w:
                ev.append(self.last_w[k])
        for k in writes:
            if k in self.last_w:
                ev.append(self.last_w[k])
            ev.extend(self.reads.get(k, ()))
        return ev

    def _emit_waits(self, e, evs):
        need = {}
        for (s, v, src) in evs:
            if src == "pe" and e == "pe":
                continue
            if v > need.get(s, 0):
                need[s] = v
        for s, v in need.items():
            if self.waited[e].get(s, 0) >= v:
                continue
            self.eng[e].wait_ge(self.sem[s], v)
            self.waited[e][s] = v

    def _record(self, event, reads, writes):
        for k in reads:
            self.reads.setdefault(k, []).append(event)
        for k in writes:
            self.last_w[k] = event
            self.reads[k] = []

    def op(self, e, fn, reads=(), writes=()):
        self._emit_waits(e, self._deps(reads, writes))
        ins = fn(self.eng[e])
        s = self.cur["E_" + e]
        if self.cnt[s] + 1 > self.LIMIT:
            s = self._rot("E_" + e)
        self.cnt[s] += 1
        ins.then_inc(self.sem[s], 1)
        self._record((s, self.cnt[s], e), reads, writes)

    def dma(self, e, fn, semkey, reads=(), writes=()):
        key = "D_" + semkey
        if key not in self.cur:
            self._rot(key)
        self._emit_waits(e, self._deps(reads, writes))
        insl = fn(self.eng[e])
        if not isinstance(insl, (list, tuple)):
            insl = [insl]
        s = self.cur[key]
        if self.cnt[s] + 16 * len(insl) > self.LIMIT:
            s = self._rot(key)
        for ins in insl:
            ins.then_inc(self.sem[s], 16)
            self.cnt[s] += 16
        self._record((s, self.cnt[s], "dma"), reads, writes)

    def barrier(self):
        evs = [(s, c, "x") for s, c in self.cnt.items() if c > 0]
        for e in self.ENGS:
            self._emit_waits(e, evs)


def rel_bucket_np(n):
    n = np.maximum(n, 0)
    nf = np.maximum(n, 1).astype(np.float32)
    large = 16 + (np.log(nf / np.float32(16)) / np.float32(math.log(8.0)) * np.float32(16)).astype(np.int32)
    large = np.minimum(large, 31)
    return np.where(n < 16, n, large)


def host_consts():
    k = np.arange(128)[:, None]
    q = np.arange(128)[None, :]
    E = np.zeros((128, 32, 2, 128), np.float32)
    for dl in range(2):
        b = rel_bucket_np(q - k + 128 * dl)
        for bb in range(32):
            E[:, bb, dl, :] = (b == bb)
    mdiag = np.where(q >= k, 0.0, NEG).astype(np.float32)
    mnear = np.where(q < k, 0.0, NEG).astype(np.float32)
    masks = np.stack([mdiag, mnear], 1)
    ident = np.eye(128, dtype=np.float32)
    ones = np.ones((128, 128), np.float32)
    j = np.arange(16)[:, None]
    n = np.arange(8)[None, :]
    valid = (n < (j // 2)).astype(np.float32)
    own = (n == (j // 2)).astype(np.float32)
    def bc(a):
        return np.ascontiguousarray(np.broadcast_to(a[None, :, :], (128, 16, 8))).astype(np.float32)
    selc = np.stack([bc(valid), bc(own), bc((1.0 - valid) * -1e9)], 1)
    return dict(cE=E, cmask=masks, cident=ident.astype(ml_dtypes.bfloat16), cones=ones, cident32=ident, csel=selc)


class _Stop(Exception):
    pass


def build(n_layers=DEPTH, stop_after=None, dbg=False):
    nc = bass.Bass("TRN2", target_bir_lowering=False)
    dt_in = lambda name, shape, dt=F32: nc.dram_tensor(name, list(shape), dt, kind="ExternalInput")
    xT_in = dt_in("xT", [D, S])
    table = dt_in("table", [32, 20])
    w_in = dt_in("w_in", [n_layers, D, INW])
    w_o = [dt_in(n_, [n_layers, 512, D]) for n_ in ("w_oa", "w_ob", "w_oc")]
    w_out = dt_in("w_out", [n_layers, D, D])
    w_up = dt_in("w_up", [n_layers, D, 2 * DFF])
    w_down = dt_in("w_down", [n_layers, DFF, D])
    pcols = dt_in("pcols", [128, DEPTH, PC])
    lamv = dt_in("lamv", [DEPTH, 4, 64])
    subg = dt_in("subg", [DEPTH, 128])
    sinks = dt_in("sinks", [DEPTH, 8])
    cE = dt_in("cE", [128, 32, 2, 128])
    cmask = dt_in("cmask", [128, 2, 128])
    cident = dt_in("cident", [128, 128], BF16)
    cident32 = dt_in("cident32", [128, 128])
    cones = dt_in("cones", [128, 128])
    csel = dt_in("csel", [128, 3, 16, 8])
    out = nc.dram_tensor("out", [D, S], F32, kind="ExternalOutput")
    kd = "ExternalOutput" if dbg else "Internal"
    qkT = nc.dram_tensor("qkT", [2688, S], BF16, kind=kd)
    qc32 = nc.dram_tensor("qc32", [512, S], F32, kind=kd)
    vS = nc.dram_tensor("vS", [S, 1152], BF16, kind=kd)
    gT = nc.dram_tensor("gT", [6144, S], BF16, kind=kd)
    yT = nc.dram_tensor("yT", [1536, S], BF16, kind=kd)
    QK_ROW = {"qa": 0, "ka": 512, "qb": 1024, "kb": 1536, "qc": 1664, "kc": 2176}

    S_ = Sched(nc)
    import contextlib
    es = contextlib.ExitStack()
    sb = lambda name, shape, dt=F32: es.enter_context(nc.sbuf_tensor(name, list(shape), dt))
    PB = [es.enter_context(nc.psum_tensor("pb%d" % i, [128, 512], F32)) for i in range(8)]
    pcol = sb("pcol", [128, 1, PC])
    ident = sb("ident", [128, 128], BF16)
    ident32 = sb("ident32", [128, 128])
    ones32 = sb("ones32", [128, 128])
    biasT = sb("biasT", [128, 20, 2, 128])
    cfar = sb("cfar", [128, 20])
    selc = sb("selc", [128, 3, 16, 8])
    WS = [sb("ws%d" % i, [128, 8192], BF16) for i in range(2)]
    wsi = [0]
    xt = sb("xt", [128, 16, TW])
    rstd = sb("rstd", [128, TW])
    sqt = [sb("sqt%d" % i, [128, TW]) for i in range(2)]
    stg = [sb("stg%d" % i, [128, 4, TW], BF16) for i in range(2)]
    big = sb("big", [128, 27648], BF16)
    big2 = sb("big2", [128, 16384], BF16)
    hT = big2[:, 0:8192].rearrange("p (c t) -> p c t", c=16)
    halo = sb("halo", [128, 88, 2])
    ubuf = [sb("ubuf%d" % i, [128, TW + 2]) for i in range(2)]
    ctmp = [sb("ctmp%d" % i, [128, TW]) for i in range(3)]
    small = sb("small", [128, 64])
    lvt_t = sb("lvt_t", [128, 256])
    gsub_t = sb("gsub_t", [128, 128])
    kmean_t = sb("kmean_t", [128, 32])
    khl_t = sb("khl_t", [128, 64], BF16)

    def ld(dst, src, key, eng="sp", reads=()):
        S_.dma(eng, lambda e: e.dma_start(out=dst, in_=src), key, reads=reads, writes=[key])

    ld(ident[:], cident.ap(), "ident")
    ld(ident32[:], cident32.ap(), "ident32")
    ld(ones32[:], cones.ap(), "ones32")
    ld(selc[:], csel.ap(), "selc")
    ld(cfar[:], bass.AP(table, 31 * 20, [[0, 128], [1, 20]]), "cfar")
    S_.op("pool", lambda e: e.memset(halo[:], 0.0), writes=["halo"])

    Et = big[:].bitcast(F32)[:, 0:8192].rearrange("p (b d q) -> p b d q", b=32, d=2)
    tabbc = big[:].bitcast(F32)[:, 8192:8832]
    ld(Et, cE.ap(), "Et")
    ld(tabbc, bass.AP(table, 0, [[0, 128], [1, 640]]), "tabbc")
    ld(biasT[:, :, 0, :], bass.AP(cmask, 0, [[256, 128], [0, 20], [1, 128]]), "biasT")
    S_.op("pool", lambda e: e.memset(biasT[:, :, 1, :], 0.0), reads=[], writes=["biasT1"])
    ld(biasT[:, 4:12, 1, :], bass.AP(cmask, 128, [[256, 128], [0, 8], [1, 128]]), "biasT1", reads=["biasT1"])
    for h in range(20):
        for b in range(32):
            def f(e, h=h, b=b):
                return e.scalar_tensor_tensor(out=biasT[:, h, :, :], in0=Et[:, b, :, :],
                                              scalar=tabbc[:, b * 20 + h:b * 20 + h + 1],
                                              in1=biasT[:, h, :, :], op0=ALU.mult, op1=ALU.add)
            S_.op("dve", f, reads=["Et", "tabbc", "biasT", "biasT1", "bT%d" % h], writes=["bT%d" % h])
    for h in range(20):
        S_.op("dve", lambda e, h=h: e.tensor_scalar(out=biasT[:, h, :, :], in0=biasT[:, h, :, :], scalar1=8.0, scalar2=None,
                                                   op0=ALU.mult), reads=["bT%d" % h], writes=["bT%d" % h])
    S_.barrier()
    bias_keys = ["bT%d" % h for h in range(20)] + ["cfar"]

    def wslot():
        i = wsi[0]
        wsi[0] = (i + 1) % 2
        return WS[i], ["wq%d" % (2 * i), "wq%d" % (2 * i + 1)]

    wqi = [0]

    def load_ws(src_ap, kc, ncols):
        q = wqi[0]
        wqi[0] = (q + 1) % 4
        view = WS[q // 2][:, (q % 2) * 4096:(q % 2) * 4096 + kc * ncols].rearrange("p (k n) -> p k n", k=kc)
        S_.dma("pool", lambda e: e.dma_start(out=view, in_=src_ap.rearrange("(k p) n -> p k n", p=128)),
               "wq%d" % q, writes=["wq%d" % q])
        return view, ["wq%d" % q]

    pbi = [0]

    def next_pb(lo=0, hi=4):
        i = lo + pbi[0] % (hi - lo)
        pbi[0] += 1
        return PB[i], "pb%d" % i

    def load_w(src_ap, kc, ncols):
        wt, key = wslot()
        view = wt[:, 0:kc * ncols].rearrange("p (k n) -> p k n", k=kc)
        S_.dma("pool", lambda e: e.dma_start(out=view, in_=src_ap.rearrange("(k p) n -> p k n", p=128)),
               "ws%d" % (int(key[0][2:]) // 2), writes=key)
        return view, key

    def rms_stats(src_chunk, src_keys, nchunks=16):
        for c in range(nchunks):
            sq, sk = sqt[c % 2], "sqt%d" % (c % 2)
            S_.op("act", lambda e, c=c, sq=sq: e.activation(out=sq[:], in_=src_chunk(c), func=AF.Square),
                  reads=src_keys, writes=[sk])
            S_.op("pe", lambda e, c=c, sq=sq: e.matmul(PB[4][:], lhsT=ones32[:], rhs=sq[:], start=(c == 0),
                                                     stop=(c == nchunks - 1)),
                  reads=[sk, "ones32"], writes=["pb4"])
        S_.op("dve", lambda e: e.tensor_scalar(out=rstd[:], in0=PB[4][:], scalar1=1.0 / D, scalar2=EPS,
                                               op0=ALU.mult, op1=ALU.add), reads=["pb4"], writes=["rstd"])
        S_.op("act", lambda e: e.activation(out=rstd[:], in_=rstd[:], func=AF.Sqrt), reads=["rstd"], writes=["rstd"])
        S_.op("dve", lambda e: e.reciprocal(out=rstd[:], in_=rstd[:]), reads=["rstd"], writes=["rstd"])

    def make_h(l, gcol0, dst=None):
        if dst is None:
            dst = lambda c: hT[:, c, :]
        rms_stats(lambda c: xt[:, c, :], ["xt"])
        for c in range(16):
            S_.op("dve", lambda e, c=c: e.scalar_tensor_tensor(
                out=dst(c), in0=xt[:, c, :], scalar=pcol[:, 0, gcol0 + c:gcol0 + c + 1], in1=rstd[:],
                op0=ALU.mult, op1=ALU.mult), reads=["xt", "rstd", "pcol"], writes=["hT"])

    def gemm_fm(wview, wkey, kc, nch, act_chunk, act_keys, consumer):
        for j in range(nch):
            ps, pk = next_pb()

            def mm(e, j=j, ps=ps):
                last = None
                for k in range(kc):
                    last = e.matmul(ps[:], lhsT=wview[:, k, j * 128:(j + 1) * 128], rhs=act_chunk(k),
                                    start=(k == 0), stop=(k == kc - 1))
                return last
            S_.op("pe", mm, reads=wkey + act_keys, writes=[pk])
            consumer(j, ps, pk)

    oT = big2[:].bitcast(F32).rearrange("p (c t) -> p c t", c=16)

    def post_norm_residual(l, gcol0):
        rms_stats(lambda c: oT[:, c, :], ["oT"])
        for c in range(16):
            S_.op("dve", lambda e, c=c: e.scalar_tensor_tensor(
                out=oT[:, c, :], in0=oT[:, c, :], scalar=pcol[:, 0, gcol0 + c:gcol0 + c + 1], in1=rstd[:],
                op0=ALU.mult, op1=ALU.mult), reads=["oT", "rstd"], writes=["oT"])
            S_.op("dve", lambda e, c=c: e.tensor_tensor(out=xt[:, c, :], in0=xt[:, c, :], in1=oT[:, c, :],
                                                        op=ALU.add), reads=["oT", "xt"], writes=["xt"])

    def tokkey(name, tt):
        return "%s:%d" % (name, tt)

    def stop(tag):
        if stop_after == tag:
            raise _Stop()
    try:
      for l in range(n_layers):
        x_src = xT_in if l == 0 else out
        stop("setup")
        ld(pcol[:], pcols.ap()[:, l:l + 1, :], "pcol")
        hT2 = big2[:].rearrange("p (c t) -> p c t", c=16)
        groups = [("qa", 0, 512, "fm"), ("ka", 512, 512, "fm"), ("va", 1024, 512, "tm"),
                  ("qb", 1536, 512, "fm"), ("kb", 2048, 128, "fm"), ("vb", 2176, 128, "tm"),
                  ("qc", 2304, 512, "fm"), ("kc", 2816, 512, "fm"), ("vc", 3328, 512, "tm")]
        groups += [("g%d" % i, 3840 + i * 512, 512, "gate") for i in range(12)]
        vcol = {"va": 0, "vb": 512, "vc": 640}
        for tp in range(NT // 2):
            for half in range(2):
                tt = 2 * tp + half
                ld(xt[:], x_src.ap()[:, tt * TW:(tt + 1) * TW].rearrange("(c p) t -> p c t", p=128), "xt",
                   reads=[tokkey("x", tt)])
                make_h(l, PC_PREMIX, dst=lambda c, half=half: hT2[:, c, half * TW:(half + 1) * TW])
            for gi, (nm, c0, ncol, kind) in enumerate(groups):
                wv, wk = load_w(w_in.ap()[l, :, c0:c0 + ncol], 16, ncol)
                nch = ncol // 128
                for half in range(2):
                    tt = 2 * tp + half
                    t0 = tt * TW
                    hTh = hT2[:, :, half * TW:(half + 1) * TW]
                    sidx = (2 * gi + half) % 2
                    st, sk = stg[sidx], "stg%d" % sidx
                    if kind in ("fm", "gate"):
                        def cons(j, ps, pk, nm=nm, kind=kind, st=st, sk=sk, gi=gi, t0=t0, tt=tt):
                            if kind == "gate":
                                bc_ = PC_BGATE + (gi - 9) * 4 + j
                                S_.op("act", lambda e: e.activation(out=st[:, j, :], in_=ps[:], func=AF.Sigmoid,
                                                                    bias=pcol[:, 0, bc_:bc_ + 1], scale=1.0),
                                      reads=[pk, "pcol"], writes=[sk])
                            else:
                                S_.op("act", lambda e: e.activation(out=st[:, j, :], in_=ps[:], func=AF.Identity),
                                      reads=[pk], writes=[sk])
                                if nm == "qc":
                                    S_.op("act", lambda e: e.activation(out=ctmp[2][:], in_=ps[:], func=AF.Identity),
                                          reads=[pk], writes=["ctmp2"])
                                    d2 = qc32.ap()[j * 128:(j + 1) * 128, t0:t0 + TW]
                                    S_.dma("sp", lambda e, d2=d2: e.dma_start(out=d2, in_=ctmp[2][:]), "qc32w",
                                           reads=["ctmp2"], writes=[tokkey("qc32", tt)])
                        gemm_fm(wv, wk, 16, nch, lambda k, hTh=hTh: hTh[:, k, :], ["hT"], cons)
                        if kind == "gate":
                            r0 = (gi - 9) * 512
                            dst = gT.ap()[r0:r0 + 512, t0:t0 + TW].rearrange("(j p) t -> p j t", p=128)
                            S_.dma("sp", lambda e, dst=dst, st=st: e.dma_start(out=dst, in_=st[:]), "stw%d" % sidx,
                                   reads=[sk], writes=[tokkey("gT", tt)])
                        else:
                            r0 = QK_ROW[nm]
                            dst = qkT.ap()[r0:r0 + ncol, t0:t0 + TW].rearrange("(j p) t -> p j t", p=128)
                            S_.dma("sp", lambda e, dst=dst, st=st, nch=nch: e.dma_start(out=dst, in_=st[:, 0:nch, :]),
                                   "stw%d" % sidx, reads=[sk], writes=[tokkey("qkT", tt)])
                    else:
                        stv = st[:].rearrange("p j t -> p (j t)")[:, 0:4 * ncol].rearrange("p (i n) -> p i n", i=4)
                        for i in range(4):
                            ps, pk = next_pb()

                            def mm(e, i=i, ps=ps, wv=wv, ncol=ncol, hTh=hTh):
                                last = None
                                for k in range(16):
                                    last = e.matmul(ps[:, 0:ncol], lhsT=hTh[:, k, i * 128:(i + 1) * 128], rhs=wv[:, k, :],
                                                    start=(k == 0), stop=(k == 15))
                                return last
                            S_.op("pe", mm, reads=wk + ["hT"], writes=[pk])
                            S_.op("act", lambda e, i=i, ps=ps, ncol=ncol, stv=stv: e.activation(
                                out=stv[:, i, :], in_=ps[:, 0:ncol], func=AF.Identity), reads=[pk], writes=[sk])
                        v0 = vcol[nm]
                        dst = vS.ap()[t0:t0 + TW, v0:v0 + ncol].rearrange("(i p) n -> p i n", p=128)
                        S_.dma("sp", lambda e, dst=dst, stv=stv: e.dma_start(out=dst, in_=stv), "stw%d" % sidx,
                               reads=[sk], writes=[tokkey("vS", tt)])

        stop("S1")
        S_.barrier()
        allk = lambda nm: [tokkey(nm, tt) for tt in range(NT)]
        bigv = big[:]
        qT = bigv[:, 0:8192].rearrange("p (c t) -> p c t", c=4)
        kT = bigv[:, 8192:16384].rearrange("p (c t) -> p c t", c=4)
        vA = bigv[:, 16384:16384 + 16 * 4 * 129].rearrange("p (b h e) -> p b h e", b=16, h=4)
        vC = bigv[:, 16384:16384 + 16 * 8 * 65].rearrange("p (b h e) -> p b h e", b=16, h=8)
        q32 = xt[:].rearrange("p c t -> p (c t)").rearrange("p (c t) -> p c t", c=4)
        ytok = bigv[:, 24704:24704 + 2048].rearrange("p (j n) -> p j n", j=4)
        PT = [stg[0][:].rearrange("p j t -> p (j t)")[:, i * 512:(i + 1) * 512] for i in range(4)]
        PT += [ubuf[i][:].bitcast(BF16)[:, 0:512] for i in range(2)]
        PTK = ["PT%d" % i for i in range(6)]
        SPS = [(PB[0], "pb0"), (PB[1], "pb1"), (PB[6], "pb6"), (PB[7], "pb7")]
        pti = [0]
        tmp32 = ctmp[0]
        accs = [(PB[2], PB[3], "pb2", "pb3"), (PB[4], PB[5], "pb4", "pb5")]
        ysT = stg[1][:].rearrange("p j t -> p (j t)")

        pend = [None]

        def run_pv(p):
            fn, kt_, items = p
            fn.__defaults__[-1][0] = kt_
            for (j, ap_, k_) in items:
                fn(j, ap_, k_)

        def flush_pv():
            if pend[0] is not None:
                run_pv(pend[0])
                pend[0] = None

        def attn_tile(hh, kt_ap, q_ap_fn, kt, jlist, pv_fn, scale_bias=True, swa=False):
            groups_ = []
            far = [j for j in jlist if j - kt >= 2]
            if far:
                groups_.append(("far", far))
            if kt + 1 in jlist:
                groups_.append(("near", [kt + 1]))
            if kt in jlist:
                groups_.append(("diag", [kt]))
            for kind, js in groups_:
                n = len(js) * 128
                sps, spk = SPS[pti[0] % 4]
                pt, ptk = PT[pti[0] % 6], PTK[pti[0] % 6]
                pti[0] += 1
                if kind == "far":
                    S_.op("pe", lambda e, js=js, n=n, sps=sps: e.matmul(sps[:, 0:n], lhsT=kt_ap, rhs=q_ap_fn(js[0] * 128, n),
                                                                      start=True, stop=True),
                          reads=["attn_in"], writes=[spk])
                    S_.op("act", lambda e, n=n, sps=sps, pt=pt: e.activation(out=pt[:, 0:n], in_=sps[:, 0:n], func=AF.Exp,
                                                                            bias=cfar[:, hh:hh + 1], scale=0.125),
                          reads=[spk, "cfar"], writes=[ptk])
                else:
                    dl = 1 if kind == "near" else 0

                    def stb(e, js=js, sps=sps, dl=dl):
                        e.matmul(sps[:, 0:128], lhsT=kt_ap, rhs=q_ap_fn(js[0] * 128, 128), start=True, stop=False)
                        return e.matmul(sps[:, 0:128], lhsT=ident32[:], rhs=biasT[:, hh, dl, :], start=False, stop=True)
                    S_.op("pe", stb, reads=["attn_in", "ident32"] + bias_keys, writes=[spk])
                    S_.op("act", lambda e, sps=sps, pt=pt: e.activation(out=pt[:, 0:128], in_=sps[:, 0:128], func=AF.Exp,
                                                                       scale=0.125),
                          reads=[spk], writes=[ptk])
                prev = pend[0]
                pend[0] = (pv_fn, kt, [(j, pt[:, ji * 128:(ji + 1) * 128], ptk) for ji, j in enumerate(js)])
                if prev is not None:
                    run_pv(prev)

        lam_init = 0.8 - 0.6 * math.exp(-0.3 * l)
        lamt = small[:, 0:8]
        lvt = lvt_t[:].rearrange("p (a b) -> p a b", a=4)
        ld(lvt, bass.AP(lamv, l * 256, [[0, 128], [64, 4], [1, 64]]), "lvt")
        gsub = gsub_t[:]
        ld(gsub, bass.AP(subg, l * 128, [[0, 128], [1, 128]]), "gsub")
        esink = small[:, 8:16]
        ld(esink, bass.AP(sinks, l * 8, [[0, 128], [1, 8]]), "esink")
        S_.op("dve", lambda e: e.tensor_tensor(out=lvt[:, 0, :], in0=lvt[:, 0, :], in1=lvt[:, 1, :], op=ALU.mult),
              reads=["lvt"], writes=["lvt"])
        S_.op("dve", lambda e: e.tensor_tensor(out=lvt[:, 2, :], in0=lvt[:, 2, :], in1=lvt[:, 3, :], op=ALU.mult),
              reads=["lvt"], writes=["lvt"])
        S_.op("dve", lambda e: e.reduce_sum(out=lamt[:, 0:1], in_=lvt[:, 0, :], axis=AX.X), reads=["lvt"], writes=["lamt"])
        S_.op("dve", lambda e: e.reduce_sum(out=lamt[:, 1:2], in_=lvt[:, 2, :], axis=AX.X), reads=["lvt"], writes=["lamt"])
        S_.op("act", lambda e: e.activation(out=lamt[:, 2:4], in_=lamt[:, 0:2], func=AF.Exp), reads=["lamt"], writes=["lamt"])
        S_.op("dve", lambda e: e.scalar_tensor_tensor(out=lamt[:, 4:5], in0=lamt[:, 3:4], scalar=-lam_init, in1=lamt[:, 2:3],
                                                      op0=ALU.add, op1=ALU.subtract), reads=["lamt"], writes=["lamt"])
        S_.op("dve", lambda e: e.tensor_scalar(out=gsub, in0=gsub, scalar1=(1.0 - lam_init), scalar2=None, op0=ALU.mult),
              reads=["gsub"], writes=["gsub"])
        S_.op("act", lambda e: e.activation(out=esink, in_=esink, func=AF.Exp), reads=["esink"], writes=["esink"])

        def flush_y(jc, ncols_used, row0):
            nchk = ncols_used // 128
            for c in range(nchk):
                tp = PB[6][:].bitcast(BF16)[:, 0:512]
                for jj in range(4):
                    S_.op("pe", lambda e, c=c, jj=jj: e.transpose(tp[:, jj * 128:(jj + 1) * 128],
                                                                 ytok[:, jj, c * 128:(c + 1) * 128], ident[:]),
                          reads=["ytok", "ident"], writes=["pb6"])
                S_.op("act", lambda e, c=c: e.activation(out=ysT[:, c * 512:(c + 1) * 512], in_=tp, func=AF.Identity),
                      reads=["pb6"], writes=["ysT"])
            dst = yT.ap()[row0:row0 + ncols_used, jc * 512:(jc + 1) * 512].rearrange("(c p) t -> p c t", p=128)
            S_.dma("sp", lambda e: e.dma_start(out=dst, in_=ysT[:, 0:nchk * 512].rearrange("p (c t) -> p c t", c=nchk)),
                   "yTw", reads=["ysT"], writes=["yT"])

        ld(qT, qkT.ap()[0:512, :].rearrange("(c p) t -> p c t", p=128), "attn_in", reads=allk("qkT"))
        ld(kT, qkT.ap()[512:1024, :].rearrange("(c p) t -> p c t", p=128), "attn_in", reads=allk("qkT"))
        S_.op("pool", lambda e: e.memset(vA[:, :, :, 128:129], 1.0), writes=["attn_in"])
        for h_ in range(4):
            ld(vA[:, :, h_, 0:128], vS.ap()[:, h_ * 128:(h_ + 1) * 128].rearrange("(b p) e -> p b e", p=128), "attn_in",
               reads=allk("vS"))
        omt = ctmp[2][:].rearrange("p (j e) -> p j e", j=4)
        om1 = ctmp[1][:].rearrange("p (j e) -> p j e", j=4)
        ai = 0
        for jc in range(4):
            jlist = list(range(4 * jc, 4 * jc + 4))
            for h in range(4):
                for m in range(2):
                    a0, a1, k0, k1 = accs[ai % 2]
                    ai += 1
                    nkt = 4 * jc + 4

                    touched = {}
                    tot = {k0: 8 * jc + 3, k1: 8 * jc + 7}

                    def pv(j, ptap, ptk, h=h, a0=a0, a1=a1, k0=k0, k1=k1, touched=touched, tot=tot, ktc=[None]):
                        jj = j - 4 * jc
                        bank, bk = (a0, k0) if jj < 2 else (a1, k1)
                        o0 = (jj % 2) * 256
                        kt_ = ktc[0]
                        st_ = bk not in touched
                        touched[bk] = touched.get(bk, 0) + 1
                        sp_ = touched[bk] == tot[bk]
                        S_.op("pe", lambda e: e.matmul(bank[:, o0:o0 + 129], lhsT=ptap, rhs=vA[:, kt_, h, :],
                                                      start=st_, stop=sp_),
                              reads=[ptk, "attn_in"], writes=[bk])
                    for kt in range(nkt):
                        pv.__defaults__[-1][0] = kt
                        attn_tile(h, kT[m * 64:(m + 1) * 64, h, kt * 128:(kt + 1) * 128],
                                  lambda q0, n, m=m, h=h: qT[m * 64:(m + 1) * 64, h, q0:q0 + n], kt, jlist, pv)
                    flush_pv()
                    dstm = omt if m == 0 else om1
                    for jj in range(4):
                        bank, bk = (a0, k0) if jj < 2 else (a1, k1)
                        o0 = (jj % 2) * 256
                        S_.op("dve", lambda e, bank=bank, o0=o0, jj=jj: e.reciprocal(out=small[:, 16 + jj:17 + jj],
                                                                                    in_=bank[:, o0 + 128:o0 + 129]),
                              reads=[bk], writes=["rden"])
                        S_.op("dve", lambda e, bank=bank, o0=o0, jj=jj, dstm=dstm: e.tensor_scalar(
                            out=dstm[:, jj, :], in0=bank[:, o0:o0 + 128], scalar1=small[:, 16 + jj:17 + jj], scalar2=None,
                            op0=ALU.mult), reads=[bk, "rden"], writes=["om%d" % m])
                S_.op("dve", lambda e: e.scalar_tensor_tensor(out=omt, in0=om1, scalar=lamt[:, 4:5], in1=omt,
                                                              op0=ALU.mult, op1=ALU.add),
                      reads=["om0", "om1", "lamt"], writes=["om0"])
                S_.op("dve", lambda e: e.tensor_tensor(out=om1, in0=omt, in1=omt, op=ALU.mult), reads=["om0"], writes=["om1"])
                S_.op("dve", lambda e: e.reduce_sum(out=small[:, 20:24], in_=om1, axis=AX.X), reads=["om1"], writes=["ssq"])
                S_.op("dve", lambda e: e.tensor_scalar(out=small[:, 20:24], in0=small[:, 20:24], scalar1=1.0 / 128, scalar2=EPS,
                                                       op0=ALU.mult, op1=ALU.add), reads=["ssq"], writes=["ssq"])
                S_.op("act", lambda e: e.activation(out=small[:, 20:24], in_=small[:, 20:24], func=AF.Sqrt), reads=["ssq"], writes=["ssq"])
                S_.op("dve", lambda e: e.reciprocal(out=small[:, 20:24], in_=small[:, 20:24]), reads=["ssq"], writes=["ssq"])
                for jj in range(4):
                    S_.op("dve", lambda e, jj=jj, h=h: e.scalar_tensor_tensor(
                        out=ytok[:, jj, h * 128:(h + 1) * 128], in0=omt[:, jj, :], scalar=small[:, 20 + jj:21 + jj],
                        in1=gsub, op0=ALU.mult, op1=ALU.mult), reads=["om0", "ssq", "gsub"], writes=["ytok"])
            flush_y(jc, 512, 0)

        stop("A")
        kB = kT[:, 0:2, :]
        ld(qT, qkT.ap()[1024:1536, :].rearrange("(c p) t -> p c t", p=128), "attn_in", reads=allk("qkT") + ["yT"])
        for g in range(2):
            for half in range(2):
                ld(kB[half * 64:(half + 1) * 64, g, :], qkT.ap()[1536 + g * 64:1536 + (g + 1) * 64, :], "attn_in")
        vB = vC[:, :, 0:2, :]
        S_.op("pool", lambda e: e.memset(vB[:, :, :, 64:65], 1.0), writes=["attn_in"])
        for h_ in range(2):
            ld(vB[:, :, h_, 0:64], vS.ap()[:, 512 + h_ * 64:512 + (h_ + 1) * 64].rearrange("(b p) e -> p b e", p=128), "attn_in")
        for jc in range(4):
            jlist = list(range(4 * jc, 4 * jc + 4))
            for h in range(8):
                a0, a1, k0, k1 = accs[ai % 2]
                ai += 1
                g = h // 4
                pb_ = (h % 2) * 64

                touchedb = {}
                totb = 7 if jc == 0 else 8

                def pvb(j, ptap, ptk, g=g, a0=a0, k0=k0, touchedb=touchedb, totb=totb, ktc=[None]):
                    jj = j - 4 * jc
                    kt_ = ktc[0]
                    first = k0 not in touchedb
                    touchedb[k0] = touchedb.get(k0, 0) + 1
                    lastb = touchedb[k0] == totb
                    S_.op("pe", lambda e: e.matmul(a0[:, jj * 128:jj * 128 + 65], lhsT=ptap, rhs=vB[:, kt_, g, :],
                                                  start=first, stop=lastb),
                          reads=[ptk, "attn_in"], writes=[k0])
                for kt in range(max(0, 4 * jc - 1), 4 * jc + 4):
                    pvb.__defaults__[-1][0] = kt
                    js = [j for j in jlist if j in (kt, kt + 1)]
                    attn_tile(4 + h, kB[pb_:pb_ + 64, g, kt * 128:(kt + 1) * 128],
                              lambda q0, n, h=h, pb_=pb_: qT[pb_:pb_ + 64, h // 2, q0:q0 + n], kt, js, pvb)
                flush_pv()
                for jj in range(4):
                    S_.op("dve", lambda e, jj=jj, h=h, a0=a0: e.tensor_scalar(
                        out=small[:, 16 + jj:17 + jj], in0=a0[:, jj * 128 + 64:jj * 128 + 65], scalar1=esink[:, h:h + 1],
                        scalar2=None, op0=ALU.add), reads=[k0, "esink"], writes=["rden"])
                    S_.op("dve", lambda e, jj=jj: e.reciprocal(out=small[:, 16 + jj:17 + jj], in_=small[:, 16 + jj:17 + jj]),
                          reads=["rden"], writes=["rden"])
                    S_.op("dve", lambda e, jj=jj, h=h, a0=a0: e.tensor_scalar(
                        out=ytok[:, jj, h * 64:(h + 1) * 64], in0=a0[:, jj * 128:jj * 128 + 64],
                        scalar1=small[:, 16 + jj:17 + jj], scalar2=None, op0=ALU.mult),
                        reads=[k0, "rden"], writes=["ytok"])
            flush_y(jc, 512, 512)

        stop("B")
        ld(qT, qkT.ap()[1664:2176, :].rearrange("(c p) t -> p c t", p=128), "attn_in", reads=allk("qkT") + ["yT"])
        ld(kT, qkT.ap()[2176:2688, :].rearrange("(c p) t -> p c t", p=128), "attn_in")
        ld(q32, qc32.ap().rearrange("(c p) t -> p c t", p=128), "attn_in", reads=allk("qc32"))
        S_.op("pool", lambda e: e.memset(vC[:, :, :, 64:65], 1.0), writes=["attn_in"])
        for h_ in range(8):
            ld(vC[:, :, h_, 0:64], vS.ap()[:, 640 + h_ * 64:640 + (h_ + 1) * 64].rearrange("(b p) e -> p b e", p=128), "attn_in")
        kmean = kmean_t[:].rearrange("p (c n) -> p c n", c=4)
        for c_ in range(4):
            S_.op("dve", lambda e, c_=c_: e.reduce_sum(out=kmean_t[:, c_ * 8:(c_ + 1) * 8],
                                                      in_=kT[:, c_, :].rearrange("p (n s) -> p n s", s=256), axis=AX.X),
                  reads=["attn_in"], writes=["kmean"])
        khi = khl_t[:, 0:32].rearrange("p (c n) -> p c n", c=4)
        klo = khl_t[:, 32:64].rearrange("p (c n) -> p c n", c=4)
        qlo = big2[:, 4096:4096 + 8192].rearrange("p (c t) -> p c t", c=4)
        S_.op("act", lambda e: e.activation(out=khl_t[:, 0:32], in_=kmean_t[:], func=AF.Identity), reads=["kmean"], writes=["khl"])
        S_.op("dve", lambda e: e.tensor_tensor(out=khl_t[:, 32:64], in0=kmean_t[:], in1=khl_t[:, 0:32], op=ALU.subtract),
              reads=["kmean", "khl"], writes=["khl"])
        for c_ in range(4):
            S_.op("dve", lambda e, c_=c_: e.tensor_tensor(out=qlo[:, c_, :], in0=q32[:, c_, :], in1=qT[:, c_, :], op=ALU.subtract),
                  reads=["attn_in"], writes=["qlo"])
        def selbc(i, jc):
            a = selc[:, i, 4 * jc:4 * jc + 4, :]
            return bass.AP(a.tensor, a.offset, [list(a.ap[0]), [8, 4], [0, 8], [1, 8]])
        stop("C0")
        selt = sqt[0][:, 0:256].rearrange("p (j h n) -> p j h n", j=4, h=8)
        gte = sqt[1][:, 0:256].rearrange("p (j h n) -> p j h n", j=4, h=8)
        cmp_ = big2[:].bitcast(F32)[:, 0:2048]
        accC = ctmp[2][:].rearrange("p (j e) -> p j e", j=4)
        for jc in range(4):
            jlist = list(range(4 * jc, 4 * jc + 4))
            gcnt = {0: 0, 1: 0}
            for jj in range(4):
                j = 4 * jc + jj
                for h in range(8):
                    pb_ = (h % 2) * 64
                    def gmm(e, jj=jj, j=j, h=h, pb_=pb_):
                        qs = slice(j * 128, (j + 1) * 128)
                        par = h % 2
                        bank = PB[7] if par == 0 else PB[6]
                        c0_ = ((h // 2) * 4 + jj) * 8
                        gc_ = bank[:, c0_:c0_ + 8]
                        first = gcnt[par] == 0
                        gcnt[par] += 3
                        e.matmul(gc_, lhsT=qT[pb_:pb_ + 64, h // 2, qs], rhs=khi[pb_:pb_ + 64, h // 2, :],
                                 start=first, stop=False)
                        e.matmul(gc_, lhsT=qT[pb_:pb_ + 64, h // 2, qs], rhs=klo[pb_:pb_ + 64, h // 2, :],
                                 start=False, stop=False)
                        return e.matmul(gc_, lhsT=qlo[pb_:pb_ + 64, h // 2, qs], rhs=khi[pb_:pb_ + 64, h // 2, :],
                                        start=False, stop=(gcnt[par] == 48))
                    S_.op("pe", gmm, reads=["attn_in", "khl", "qlo"], writes=["pb7", "pb6"])
            stop("C1")
            stop("C1_%d" % jc)
            g2 = sqt[1][:, 0:256]
            s2 = sqt[0][:, 0:256]
            gp = PB[7][:, 0:256]
            P0 = list(g2.ap[0])
            Ps = list(s2.ap[0])

            def mk(i, jc=jc):
                a = selc[:, i, 4 * jc:4 * jc + 4, :]
                return bass.AP(a.tensor, a.offset, [list(a.ap[0]), [1, 8], [0, 8], [8, 4]])
            for par in range(2):
                gpb = (PB[7] if par == 0 else PB[6])[:, 0:128]
                in_ps = bass.AP(gpb.tensor, gpb.offset, [list(gpb.ap[0]), [1, 8], [32, 4], [8, 4]])
                out_g = bass.AP(g2.tensor, g2.offset + par * 4, [P0, [32, 8], [8, 4], [1, 4]])
                a_ = selc[:, 2, 4 * jc:4 * jc + 4, :]
                m2 = bass.AP(a_.tensor, a_.offset, [list(a_.ap[0]), [1, 8], [0, 4], [8, 4]])
                S_.op("dve", lambda e, in_ps=in_ps, out_g=out_g, m2=m2: e.tensor_tensor(out=out_g, in0=in_ps, in1=m2, op=ALU.add),
                      reads=["pb7", "pb6", "selc"], writes=["gte"])
            in0 = bass.AP(g2.tensor, g2.offset, [P0, [0, 8], [32, 8], [1, 32]])
            in1 = bass.AP(g2.tensor, g2.offset, [P0, [32, 8], [0, 8], [1, 32]])
            cmpv = cmp_.rearrange("p (n m a) -> p n m a", n=8, m=8)
            S_.op("dve", lambda e, in0=in0, in1=in1: e.tensor_tensor(out=cmpv, in0=in0, in1=in1, op=ALU.subtract),
                  reads=["gte"], writes=["cmp"])
            S_.op("dve", lambda e: e.tensor_scalar(out=cmp_, in0=cmp_, scalar1=1e20, scalar2=1.0, op0=ALU.mult, op1=ALU.min),
                  reads=["cmp"], writes=["cmp"])
            S_.op("dve", lambda e: e.tensor_scalar(out=cmp_, in0=cmp_, scalar1=0.0, scalar2=None, op0=ALU.max),
                  reads=["cmp"], writes=["cmp"])
            cin = bass.AP(cmp_.tensor, cmp_.offset, [list(cmp_.ap[0]), [256, 8], [1, 32], [32, 8]])
            s2na = bass.AP(s2.tensor, s2.offset, [Ps, [32, 8], [1, 32]])
            S_.op("dve", lambda e, cin=cin, s2na=s2na: e.reduce_sum(out=s2na, in_=cin, axis=AX.X), reads=["cmp"], writes=["selt"])
            S_.op("dve", lambda e: e.tensor_scalar(out=s2, in0=s2, scalar1=-1.0, scalar2=2.5, op0=ALU.mult, op1=ALU.add),
                  reads=["selt"], writes=["selt"])
            S_.op("dve", lambda e: e.tensor_scalar(out=s2, in0=s2, scalar1=1e20, scalar2=1.0, op0=ALU.mult, op1=ALU.min),
                  reads=["selt"], writes=["selt"])
            S_.op("dve", lambda e: e.tensor_scalar(out=s2, in0=s2, scalar1=0.0, scalar2=None, op0=ALU.max),
                  reads=["selt"], writes=["selt"])
            s2v = bass.AP(s2.tensor, s2.offset, [Ps, [32, 8], [4, 8], [1, 4]])
            S_.op("dve", lambda e, s2v=s2v, m0=mk(0): e.tensor_tensor(out=s2v, in0=s2v, in1=m0, op=ALU.mult),
                  reads=["selt", "selc"], writes=["selt"])
            S_.op("dve", lambda e, s2v=s2v, m1=mk(1): e.tensor_tensor(out=s2v, in0=s2v, in1=m1, op=ALU.add),
                  reads=["selt", "selc"], writes=["selt"])
            stop("C2")
            stop("C2_%d" % jc)
            for h in range(8):
                pb_ = (h % 2) * 64
                for nb in range(2 * jc + 2):
                    a0, a1, k0, k1 = accs[ai % 2]
                    ai += 1
                    js_n = [j for j in jlist if j // 2 >= nb]

                    touchedc = {}
                    totc = sum((1 if j == 2 * nb else 2) for j in js_n)

                    def pvc(j, ptap, ptk, h=h, a0=a0, k0=k0, nb=nb, touchedc=touchedc, totc=totc, ktc=[None]):
                        jj = j - 4 * jc
                        kt_ = ktc[0]
                        first = k0 not in touchedc
                        touchedc[k0] = touchedc.get(k0, 0) + 1
                        last = touchedc[k0] == totc
                        S_.op("pe", lambda e: e.matmul(a0[:, jj * 128:jj * 128 + 65], lhsT=ptap, rhs=vC[:, kt_, h, :],
                                                      start=first, stop=last),
                              reads=[ptk, "attn_in"], writes=[k0])
                    for kt in (2 * nb, 2 * nb + 1):
                        pvc.__defaults__[-1][0] = kt
                        js = [j for j in js_n if j >= kt]
                        if not js:
                            continue
                        attn_tile(12 + h, kT[pb_:pb_ + 64, h // 2, kt * 128:(kt + 1) * 128],
                                  lambda q0, n, h=h, pb_=pb_: qT[pb_:pb_ + 64, h // 2, q0:q0 + n], kt, js, pvc)
                    flush_pv()
                    for j in js_n:
                        jj = j - 4 * jc
                        if nb == 0:
                            S_.op("dve", lambda e, jj=jj, h=h, a0=a0, nb=nb: e.tensor_scalar(
                                out=accC[:, jj, 0:65], in0=a0[:, jj * 128:jj * 128 + 65], scalar1=sqt[0][:, nb * 32 + h * 4 + jj:nb * 32 + h * 4 + jj + 1],
                                scalar2=None, op0=ALU.mult), reads=[k0, "selt"], writes=["accC"])
                        else:
                            S_.op("dve", lambda e, jj=jj, h=h, a0=a0, nb=nb: e.scalar_tensor_tensor(
                                out=accC[:, jj, 0:65], in0=a0[:, jj * 128:jj * 128 + 65], scalar=sqt[0][:, nb * 32 + h * 4 + jj:nb * 32 + h * 4 + jj + 1],
                                in1=accC[:, jj, 0:65], op0=ALU.mult, op1=ALU.add), reads=[k0, "selt", "accC"], writes=["accC"])
                    stop("C3")
                for jj in range(4):
                    S_.op("dve", lambda e, jj=jj: e.reciprocal(out=small[:, 16 + jj:17 + jj], in_=accC[:, jj, 64:65]),
                          reads=["accC"], writes=["rden"])
                    S_.op("dve", lambda e, jj=jj, h=h: e.tensor_scalar(
                        out=ytok[:, jj, h * 64:(h + 1) * 64], in0=accC[:, jj, 0:64], scalar1=small[:, 16 + jj:17 + jj],
                        scalar2=None, op0=ALU.mult), reads=["accC", "rden"], writes=["ytok"])
                stop("C4")
                stop("C4_%d_%d" % (jc, h))
            flush_y(jc, 512, 1024)
            stop("C5")

        stop("C")
        S_.barrier()
        yTt = bigv[:, 0:6144].rearrange("p (c t) -> p c t", c=12)
        gts = bigv[:, 6144:12288].rearrange("p (b c t) -> p b c t", b=3, c=4)
        mixT = bigv[:, 12288:20480].rearrange("p (c t) -> p c t", c=16)
        actT = bigv[:, 0:44 * TW].rearrange("p (c t) -> p c t", c=44)
        for tt in range(NT):
            t0 = tt * TW
            ld(xt[:], x_src.ap()[:, t0:t0 + TW].rearrange("(c p) t -> p c t", p=128), "xt", reads=[tokkey("x", tt)])
            ld(yTt, yT.ap()[:, t0:t0 + TW].rearrange("(c p) t -> p c t", p=128), "yTt", reads=["yT", "attn_in", "ytok"])
            for cg in range(4):
                for br in range(3):
                    r0 = br * 2048 + cg * 512
                    ld(gts[:, br, :, :], gT.ap()[r0:r0 + 512, t0:t0 + TW].rearrange("(c p) t -> p c t", p=128), "gts",
                       reads=allk("gT") + ["attn_in"])
                wt, wk = wslot()
                wv = wt[:, 0:6144].rearrange("p (b k n) -> p b k n", b=3, k=4)
                for br in range(3):
                    S_.dma("pool", lambda e, br=br, wv=wv: e.dma_start(
                        out=wv[:, br, :, :], in_=w_o[br].ap()[l, :, cg * 512:(cg + 1) * 512].rearrange("(k p) n -> p k n", p=128)),
                        "ws%d" % (int(wk[0][2:]) // 2), writes=wk)
                for ch in range(4):
                    pss = []
                    for br in range(3):
                        ps, pk = next_pb()

                        def mm(e, br=br, ch=ch, ps=ps, wv=wv):
                            last = None
                            for k in range(4):
                                last = e.matmul(ps[:], lhsT=wv[:, br, k, ch * 128:(ch + 1) * 128], rhs=yTt[:, br * 4 + k, :],
                                                start=(k == 0), stop=(k == 3))
                            return last
                        S_.op("pe", mm, reads=wk + ["yTt"], writes=[pk])
                        pss.append((ps, pk))
                    t_a, t_b = ctmp[0], ctmp[1]
                    S_.op("dve", lambda e, ch=ch, pss=pss: e.tensor_tensor(out=t_a[:], in0=pss[0][0][:], in1=gts[:, 0, ch, :], op=ALU.mult),
                          reads=[pss[0][1], "gts"], writes=["t_a"])
                    S_.op("dve", lambda e, ch=ch, pss=pss: e.tensor_tensor(out=t_b[:], in0=pss[1][0][:], in1=gts[:, 1, ch, :], op=ALU.mult),
                          reads=[pss[1][1], "gts"], writes=["t_b"])
                    S_.op("dve", lambda e: e.tensor_tensor(out=t_a[:], in0=t_a[:], in1=t_b[:], op=ALU.add),
                          reads=["t_a", "t_b"], writes=["t_a"])
                    S_.op("dve", lambda e, ch=ch, pss=pss: e.tensor_tensor(out=t_b[:], in0=pss[2][0][:], in1=gts[:, 2, ch, :], op=ALU.mult),
                          reads=[pss[2][1], "gts", "t_b"], writes=["t_b"])
                    S_.op("dve", lambda e, ch=ch: e.tensor_tensor(out=mixT[:, cg * 4 + ch, :], in0=t_a[:], in1=t_b[:], op=ALU.add),
                          reads=["t_a", "t_b"], writes=["mixT"])
            for cg in range(4):
                wv, wk = load_w(w_out.ap()[l, :, cg * 512:(cg + 1) * 512], 16, 512)

                def cons(j, ps, pk, cg=cg):
                    S_.op("act", lambda e: e.activation(out=oT[:, cg * 4 + j, :], in_=ps[:], func=AF.Identity),
                          reads=[pk], writes=["oT"])
                gemm_fm(wv, wk, 16, 4, lambda k: mixT[:, k, :], ["mixT"], cons)
            post_norm_residual(l, PC_POSTMIX)
            make_h(l, PC_PREFFN)
            gbuf = [stg[i_][:].rearrange("p j t -> p (j t)").bitcast(F32).rearrange("p (j t) -> p j t", j=2) for i_ in range(2)]

            def up_chunk(wv, wk, j, fch, which):
                ps, pk = next_pb()

                def mm(e, j=j, ps=ps, wv=wv):
                    last = None
                    for k in range(16):
                        last = e.matmul(ps[:], lhsT=wv[:, k, j * 128:(j + 1) * 128], rhs=hT[:, k, :],
                                        start=(k == 0), stop=(k == 15))
                    return last
                S_.op("pe", mm, reads=wk + ["hT"], writes=[pk])
                ub, uk = ubuf[which], "ubuf%d" % which
                S_.op("act", lambda e: e.activation(out=ub[:, 0:2], in_=halo[:, fch, :], func=AF.Identity),
                      reads=["halo"], writes=[uk])
                S_.op("act", lambda e: e.activation(out=ub[:, 2:TW + 2], in_=ps[:], func=AF.Identity),
                      reads=[pk], writes=[uk])
                S_.op("act", lambda e: e.activation(out=halo[:, fch, :], in_=ub[:, TW:TW + 2], func=AF.Identity),
                      reads=[uk], writes=["halo"])
                ct, ck = ctmp[which], "ctmp%d" % which
                cw = PC_CW
                S_.op("dve", lambda e: e.tensor_scalar(
                    out=ct[:], in0=ub[:, 2:TW + 2], scalar1=pcol[:, 0, cw + 176 + fch:cw + 177 + fch],
                    scalar2=pcol[:, 0, PC_CB + fch:PC_CB + fch + 1], op0=ALU.mult, op1=ALU.add),
                    reads=[uk, "pcol"], writes=[ck])
                S_.op("dve", lambda e: e.scalar_tensor_tensor(
                    out=ct[:], in0=ub[:, 1:TW + 1], scalar=pcol[:, 0, cw + 88 + fch:cw + 89 + fch], in1=ct[:],
                    op0=ALU.mult, op1=ALU.add), reads=[uk, ck], writes=[ck])
                S_.op("dve", lambda e: e.scalar_tensor_tensor(
                    out=ct[:], in0=ub[:, 0:TW], scalar=pcol[:, 0, cw + fch:cw + fch + 1], in1=ct[:],
                    op0=ALU.mult, op1=ALU.add), reads=[uk, ck], writes=[ck])
                return ct, ck

            for i4 in range(11):
                wvg, wkg = load_w(w_up.ap()[l, :, i4 * 512:(i4 + 1) * 512], 16, 512)
                for j in range(4):
                    ct, ck = up_chunk(wvg, wkg, j, i4 * 4 + j, 0)
                    gb, gk = gbuf[j // 2][:, j % 2, :], "gb%d" % j
                    S_.op("act", lambda e, ct=ct, gb=gb: e.activation(out=gb, in_=ct[:], func=AF.Gelu_apprx_tanh),
                          reads=[ck], writes=[gk])
                wvv, wkv = load_w(w_up.ap()[l, :, DFF + i4 * 512:DFF + (i4 + 1) * 512], 16, 512)
                for j in range(4):
                    fi = i4 * 4 + j
                    ct, ck = up_chunk(wvv, wkv, j, fi + 44, 1)
                    gb, gk = gbuf[j // 2][:, j % 2, :], "gb%d" % j
                    S_.op("dve", lambda e, ct=ct, gb=gb, fi=fi: e.tensor_tensor(out=actT[:, fi, :], in0=gb, in1=ct[:], op=ALU.mult),
                          reads=[gk, ck], writes=["actT"])
            for cg in range(16):
                wva, wka = load_ws(w_down.ap()[l, 0:2816, cg * 128:(cg + 1) * 128], 22, 128)
                wvb, wkb = load_ws(w_down.ap()[l, 2816:5632, cg * 128:(cg + 1) * 128], 22, 128)
                ps, pk = next_pb()

                def mm(e, ps=ps, wva=wva, wvb=wvb):
                    last = None
                    for k in range(44):
                        wv_ = wva if k < 22 else wvb
                        last = e.matmul(ps[:], lhsT=wv_[:, k % 22, :], rhs=actT[:, k, :], start=(k == 0), stop=(k == 43))
                    return last
                S_.op("pe", mm, reads=wka + wkb + ["actT"], writes=[pk])
                S_.op("act", lambda e, ps=ps, cg=cg: e.activation(out=oT[:, cg, :], in_=ps[:], func=AF.Identity),
                      reads=[pk], writes=["oT"])
            post_norm_residual(l, PC_POSTFFN)
            S_.dma("sp", lambda e, t0=t0: e.dma_start(out=out.ap()[:, t0:t0 + TW].rearrange("(c p) t -> p c t", p=128), in_=xt[:]),
                   "outw", reads=["xt"], writes=[tokkey("x", tt)])
        S_.barrier()
        S_.op("pool", lambda e: e.memset(halo[:], 0.0), reads=["halo"], writes=["halo"])
    except _Stop:
        pass
    S_.barrier()
    es.close()
    S_.close()
    return nc


_CACHE = {}


def host_inputs(inp, n_layers=DEPTH):
    f = lambda a: np.ascontiguousarray(np.asarray(a, dtype=np.float32))
    pc = np.zeros((DEPTH, 128, PC), np.float32)
    col = lambda v, n: f(v).reshape(DEPTH, n, 128).transpose(0, 2, 1)
    pc[:, :, PC_PREMIX:PC_PREMIX + 16] = col(inp["pre_mix_g"], 16)
    pc[:, :, PC_POSTMIX:PC_POSTMIX + 16] = col(inp["post_mix_g"], 16)
    pc[:, :, PC_PREFFN:PC_PREFFN + 16] = col(inp["pre_ffn_g"], 16)
    pc[:, :, PC_POSTFFN:PC_POSTFFN + 16] = col(inp["post_ffn_g"], 16)
    pc[:, :, PC_BGATE:PC_BGATE + 48] = col(inp["b_gate"], 48)
    cw = f(inp["conv_w"]).reshape(DEPTH, 3, 88, 128).transpose(0, 3, 1, 2).reshape(DEPTH, 128, 264)
    pc[:, :, PC_CW:PC_CW + 264] = cw
    pc[:, :, PC_CB:PC_CB + 88] = col(inp["conv_b"], 88)
    shared = dict(
        table=f(inp["rel_bias_table"]), w_in=f(inp["w_in"][:n_layers]), w_oa=f(inp["w_oa"][:n_layers]), w_ob=f(inp["w_ob"][:n_layers]),
        w_oc=f(inp["w_oc"][:n_layers]), w_out=f(inp["w_out"][:n_layers]), w_up=f(inp["w_up"][:n_layers]), w_down=f(inp["w_down"][:n_layers]),
        pcols=np.ascontiguousarray(pc.transpose(1, 0, 2)),
        lamv=np.ascontiguousarray(np.stack([f(inp["lam_q1"]), f(inp["lam_k1"]), f(inp["lam_q2"]), f(inp["lam_k2"])], 1)),
        subg=f(inp["diff_subln_g"]), sinks=f(inp["sinks"]))
    shared.update(host_consts())
    x = f(inp["x"])
    return [dict(shared, xT=np.ascontiguousarray(x[b].T)) for b in range(4)]


def kernel(**inputs):
    if "nc" not in _CACHE:
        _CACHE["nc"] = build()
    in_maps = host_inputs(inputs)
    res = run_bass_kernel_spmd(_CACHE["nc"], in_maps, core_ids=list(range(4)))
    return np.stack([np.ascontiguousarray(res.results[b]["out"].T) for b in range(4)], 0).astype(np.float32)
```

```python
import math
import numpy as np
import ml_dtypes
import concourse.bass as bass
import concourse.mybir as mybir
from concourse.bass_utils import run_bass_kernel_spmd

F32 = mybir.dt.float32
BF16 = mybir.dt.bfloat16
AF = mybir.ActivationFunctionType
ALU = mybir.AluOpType
AX = mybir.AxisListType

D = 2048
S = 2048
DEPTH = 4
INW = 9984
DFF = 5632
NT = 4
TW = 512
NQB = 16
EPS = 1e-6
NEG = -30000.0
PC_PREMIX, PC_POSTMIX, PC_PREFFN, PC_POSTFFN, PC_BGATE, PC_CW, PC_CB = 0, 16, 32, 48, 64, 112, 376
PC = 464


class Sched:
    ENGS = ("pe", "act", "dve", "pool", "sp")

    def __init__(self, nc):
        self.nc = nc
        self.eng = {"pe": nc.tensor, "act": nc.scalar, "dve": nc.vector,
                    "pool": nc.gpsimd, "sp": nc.sync}
        self.sem, self.cnt = {}, {}
        self.waited = {e: {} for e in self.ENGS}
        self.last_w, self.reads = {}, {}
        self._stack = []
        self.cur = {}
        self.gen = {}
        for e in self.ENGS:
            self._rot("E_" + e)

    def _mk(self, name):
        cm = self.nc.semaphore(name)
        h = cm.__enter__()
        self._stack.append(cm)
        self.sem[name] = h
        self.cnt[name] = 0

    LIMIT = 1500

    def _rot(self, key):
        g = self.gen.get(key, -1) + 1
        self.gen[key] = g
        name = "%s#%d" % (key, g)
        self._mk(name)
        self.cur[key] = name
        return name

    def close(self):
        for cm in reversed(self._stack):
            cm.__exit__(None, None, None)

    def _deps(self, reads, writes):
        ev = []
        for k in reads:
            if k in self.last_w:
                ev.append(self.last_w[k])
        for k in writes:
            if k in self.last_w:
                ev.append(self.last_w[k])
            ev.extend(self.reads.get(k, ()))
        return ev

    def _emit_waits(self, e, evs):
        need = {}
        for (s, v, src) in evs:
            if src == "pe" and e == "pe":
                continue
            if v > need.get(s, 0):
                need[s] = v
        for s, v in need.items():
            if self.waited[e].get(s, 0) >= v:
                continue
            self.eng[e].wait_ge(self.sem[s], v)
            self.waited[e][s] = v

    def _record(self, event, reads, writes):
        for k in reads:
            self.reads.setdefault(k, []).append(event)
        for k in writes:
            self.last_w[k] = event
            self.reads[k] = []

    def op(self, e, fn, reads=(), writes=()):
        self._emit_waits(e, self._deps(reads, writes))
        ins = fn(self.eng[e])
        s = self.cur["E_" + e]
        if self.cnt[s] + 1 > self.LIMIT:
            s = self._rot("E_" + e)
        self.cnt[s] += 1
        ins.then_inc(self.sem[s], 1)
        self._record((s, self.cnt[s], e), reads, writes)

    def dma(self, e, fn, semkey, reads=(), writes=()):
        key = "D_" + semkey
        if key not in self.cur:
            self._rot(key)
        self._emit_waits(e, self._deps(reads, writes))
        insl = fn(self.eng[e])
        if not isinstance(insl, (list, tuple)):
            insl = [insl]
        s = self.cur[key]
        if self.cnt[s] + 16 * len(insl) > self.LIMIT:
            s = self._rot(key)
        for ins in insl:
            ins.then_inc(self.sem[s], 16)
            self.cnt[s] += 16
        self._record((s, self.cnt[s], "dma"), reads, writes)

    def barrier(self):
        evs = [(s, c, "x") for s, c in self.cnt.items() if c > 0]
        for e in self.ENGS:
            self._emit_waits(e, evs)


def rel_bucket_np(n):
    n = np.maximum(n, 0)
    nf = np.maximum(n, 1).astype(np.float32)
    large = 16 + (np.log(nf / np.float32(16)) / np.float32(math.log(8.0)) * np.float32(16)).astype(np.int32)
    large = np.minimum(large, 31)
    return np.where(n < 16, n, large)


def host_consts():
    k = np.arange(128)[:, None]
    q = np.arange(128)[None, :]
    E = np.zeros((128, 32, 2, 128), np.float32)
    for dl in range(2):
        b = rel_bucket_np(q - k + 128 * dl)
        for bb in range(32):
            E[:, bb, dl, :] = (b == bb)
    mdiag = np.where(q >= k, 0.0, NEG).astype(np.float32)
    mnear = np.where(q < k, 0.0, NEG).astype(np.float32)
    masks = np.stack([mdiag, mnear], 1)
    ident = np.eye(128, dtype=np.float32)
    ones = np.ones((128, 128), np.float32)
    j = np.arange(16)[:, None]
    n = np.arange(8)[None, :]
    valid = (n < (j // 2)).astype(np.float32)
    own = (n == (j // 2)).astype(np.float32)
    def bc(a):
        return np.ascontiguousarray(np.broadcast_to(a[None, :, :], (128, 16, 8))).astype(np.float32)
    selc = np.stack([bc(valid), bc(own), bc((1.0 - valid) * -1e9)], 1)
    return dict(cE=E, cmask=masks, cident=ident.astype(ml_dtypes.bfloat16), cones=ones, cident32=ident, csel=selc)


class _Stop(Exception):
    pass


def build(n_layers=DEPTH, stop_after=None, dbg=False):
    nc = bass.Bass("TRN2", target_bir_lowering=False)
    dt_in = lambda name, shape, dt=F32: nc.dram_tensor(name, list(shape), dt, kind="ExternalInput")
    xT_in = dt_in("xT", [D, S])
    table = dt_in("table", [32, 20])
    w_in = dt_in("w_in", [n_layers, D, INW])
    w_o = [dt_in(n_, [n_layers, 512, D]) for n_ in ("w_oa", "w_ob", "w_oc")]
    w_out = dt_in("w_out", [n_layers, D, D])
    w_up = dt_in("w_up", [n_layers, D, 2 * DFF])
    w_down = dt_in("w_down", [n_layers, DFF, D])
    pcols = dt_in("pcols", [128, DEPTH, PC])
    lamv = dt_in("lamv", [DEPTH, 4, 64])
    subg = dt_in("subg", [DEPTH, 128])
    sinks = dt_in("sinks", [DEPTH, 8])
    cE = dt_in("cE", [128, 32, 2, 128])
    cmask = dt_in("cmask", [128, 2, 128])
    cident = dt_in("cident", [128, 128], BF16)
    cident32 = dt_in("cident32", [128, 128])
    cones = dt_in("cones", [128, 128])
    csel = dt_in("csel", [128, 3, 16, 8])
    out = nc.dram_tensor("out", [D, S], F32, kind="ExternalOutput")
    kd = "ExternalOutput" if dbg else "Internal"
    qkT = nc.dram_tensor("qkT", [2688, S], BF16, kind=kd)
    qc32 = nc.dram_tensor("qc32", [512, S], F32, kind=kd)
    vS = nc.dram_tensor("vS", [S, 1152], BF16, kind=kd)
    gT = nc.dram_tensor("gT", [6144, S], BF16, kind=kd)
    yT = nc.dram_tensor("yT", [1536, S], BF16, kind=kd)
    QK_ROW = {"qa": 0, "ka": 512, "qb": 1024, "kb": 1536, "qc": 1664, "kc": 2176}

    S_ = Sched(nc)
    import contextlib
    es = contextlib.ExitStack()
    sb = lambda name, shape, dt=F32: es.enter_context(nc.sbuf_tensor(name, list(shape), dt))
    PB = [es.enter_context(nc.psum_tensor("pb%d" % i, [128, 512], F32)) for i in range(8)]
    pcol = sb("pcol", [128, 1, PC])
    ident = sb("ident", [128, 128], BF16)
    ident32 = sb("ident32", [128, 128])
    ones32 = sb("ones32", [128, 128])
    onesb = sb("onesb", [128, 128], BF16)
    biasT = sb("biasT", [128, 20, 2, 128])
    cfar = sb("cfar", [128, 20])
    selc = sb("selc", [128, 3, 16, 8])
    WS = [sb("ws%d" % i, [128, 8192], BF16) for i in range(2)]
    wsi = [0]
    xt = sb("xt", [128, 16, TW])
    rstd = sb("rstd", [128, TW])
    sqt = [sb("sqt%d" % i, [128, TW]) for i in range(2)]
    stg = [sb("stg%d" % i, [128, 4, TW], BF16) for i in range(2)]
    big = sb("big", [128, 27648], BF16)
    big2 = sb("big2", [128, 16384], BF16)
    hT = big2[:, 0:8192].rearrange("p (c t) -> p c t", c=16)
    halo = sb("halo", [128, 88, 2])
    ubuf = [sb("ubuf%d" % i, [128, TW + 2]) for i in range(2)]
    ctmp = [sb("ctmp%d" % i, [128, TW]) for i in range(3)]
    small = sb("small", [128, 64])
    lvt_t = sb("lvt_t", [128, 256])
    gsub_t = sb("gsub_t", [128, 128])
    kmean_t = sb("kmean_t", [128, 32])
    khl_t = sb("khl_t", [128, 64], BF16)

    def ld(dst, src, key, eng="sp", reads=()):
        S_.dma(eng, lambda e: e.dma_start(out=dst, in_=src), key, reads=reads, writes=[key])

    ld(ident[:], cident.ap(), "ident")
    ld(ident32[:], cident32.ap(), "ident32")
    ld(ones32[:], cones.ap(), "ones32")
    ld(selc[:], csel.ap(), "selc")
    ld(cfar[:], bass.AP(table, 31 * 20, [[0, 128], [1, 20]]), "cfar")
    S_.op("pool", lambda e: e.memset(halo[:], 0.0), writes=["halo"])
    S_.op("pool", lambda e: e.memset(onesb[:], 1.0), writes=["onesb"])

    Et = big[:].bitcast(F32)[:, 0:8192].rearrange("p (b d q) -> p b d q", b=32, d=2)
    tabbc = big[:].bitcast(F32)[:, 8192:8832]
    ld(Et, cE.ap(), "Et")
    ld(tabbc, bass.AP(table, 0, [[0, 128], [1, 640]]), "tabbc")
    ld(biasT[:, :, 0, :], bass.AP(cmask, 0, [[256, 128], [0, 20], [1, 128]]), "biasT")
    S_.op("pool", lambda e: e.memset(biasT[:, :, 1, :], 0.0), reads=[], writes=["biasT1"])
    ld(biasT[:, 4:12, 1, :], bass.AP(cmask, 128, [[256, 128], [0, 8], [1, 128]]), "biasT1", reads=["biasT1"])
    for h in range(20):
        for b in range(32):
            def f(e, h=h, b=b):
                return e.scalar_tensor_tensor(out=biasT[:, h, :, :], in0=Et[:, b, :, :],
                                              scalar=tabbc[:, b * 20 + h:b * 20 + h + 1],
                                              in1=biasT[:, h, :, :], op0=ALU.mult, op1=ALU.add)
            S_.op("dve", f, reads=["Et", "tabbc", "biasT", "biasT1", "bT%d" % h], writes=["bT%d" % h])
    for h in range(20):
        S_.op("dve", lambda e, h=h: e.tensor_scalar(out=biasT[:, h, :, :], in0=biasT[:, h, :, :], scalar1=8.0, scalar2=None,
                                                   op0=ALU.mult), reads=["bT%d" % h], writes=["bT%d" % h])
    S_.barrier()
    bias_keys = ["bT%d" % h for h in range(20)] + ["cfar"]

    def wslot():
        i = wsi[0]
        wsi[0] = (i + 1) % 2
        return WS[i], ["wq%d" % (2 * i), "wq%d" % (2 * i + 1)]

    wqi = [0]

    def load_ws(src_ap, kc, ncols):
        q = wqi[0]
        wqi[0] = (q + 1) % 4
        view = WS[q // 2][:, (q % 2) * 4096:(q % 2) * 4096 + kc * ncols].rearrange("p (k n) -> p k n", k=kc)
        S_.dma("pool", lambda e: e.dma_start(out=view, in_=src_ap.rearrange("(k p) n -> p k n", p=128)),
               "wq%d" % q, writes=["wq%d" % q])
        return view, ["wq%d" % q]

    pbi = [0]

    def next_pb(lo=0, hi=4):
        i = lo + pbi[0] % (hi - lo)
        pbi[0] += 1
        return PB[i], "pb%d" % i

    def load_w(src_ap, kc, ncols):
        wt, key = wslot()
        view = wt[:, 0:kc * ncols].rearrange("p (k n) -> p k n", k=kc)
        S_.dma("pool", lambda e: e.dma_start(out=view, in_=src_ap.rearrange("(k p) n -> p k n", p=128)),
               "ws%d" % (int(key[0][2:]) // 2), writes=key)
        return view, key

    def rms_stats(src_chunk, src_keys, nchunks=16):
        for c in range(nchunks):
            sq, sk = sqt[c % 2][:].bitcast(BF16)[:, 0:TW], "sqt%d" % (c % 2)
            S_.op("act", lambda e, c=c, sq=sq: e.activation(out=sq, in_=src_chunk(c), func=AF.Square),
                  reads=src_keys, writes=[sk])
            S_.op("pe", lambda e, c=c, sq=sq: e.matmul(PB[4][:], lhsT=onesb[:], rhs=sq, start=(c == 0),
                                                     stop=(c == nchunks - 1)),
                  reads=[sk, "onesb"], writes=["pb4"])
        S_.op("dve", lambda e: e.tensor_scalar(out=rstd[:], in0=PB[4][:], scalar1=1.0 / D, scalar2=EPS,
                                               op0=ALU.mult, op1=ALU.add), reads=["pb4"], writes=["rstd"])
        S_.op("act", lambda e: e.activation(out=rstd[:], in_=rstd[:], func=AF.Sqrt), reads=["rstd"], writes=["rstd"])
        S_.op("dve", lambda e: e.reciprocal(out=rstd[:], in_=rstd[:]), reads=["rstd"], writes=["rstd"])

    def make_h(l, gcol0, dst=None):
        if dst is None:
            dst = lambda c: hT[:, c, :]
        rms_stats(lambda c: xt[:, c, :], ["xt"])
        for c in range(16):
            S_.op("dve", lambda e, c=c: e.scalar_tensor_tensor(
                out=dst(c), in0=xt[:, c, :], scalar=pcol[:, 0, gcol0 + c:gcol0 + c + 1], in1=rstd[:],
                op0=ALU.mult, op1=ALU.mult), reads=["xt", "rstd", "pcol"], writes=["hT"])

    def gemm_fm(wview, wkey, kc, nch, act_chunk, act_keys, consumer):
        for j in range(nch):
            ps, pk = next_pb()

            def mm(e, j=j, ps=ps):
                last = None
                for k in range(kc):
                    last = e.matmul(ps[:], lhsT=wview[:, k, j * 128:(j + 1) * 128], rhs=act_chunk(k),
                                    start=(k == 0), stop=(k == kc - 1))
                return last
            S_.op("pe", mm, reads=wkey + act_keys, writes=[pk])
            consumer(j, ps, pk)

    oT = big2[:].bitcast(F32).rearrange("p (c t) -> p c t", c=16)

    def post_norm_residual(l, gcol0):
        rms_stats(lambda c: oT[:, c, :], ["oT"])
        for c in range(16):
            S_.op("dve", lambda e, c=c: e.scalar_tensor_tensor(
                out=oT[:, c, :], in0=oT[:, c, :], scalar=pcol[:, 0, gcol0 + c:gcol0 + c + 1], in1=rstd[:],
                op0=ALU.mult, op1=ALU.mult), reads=["oT", "rstd"], writes=["oT"])
            S_.op("dve", lambda e, c=c: e.tensor_tensor(out=xt[:, c, :], in0=xt[:, c, :], in1=oT[:, c, :],
                                                        op=ALU.add), reads=["oT", "xt"], writes=["xt"])

    def tokkey(name, tt):
        return "%s:%d" % (name, tt)

    def stop(tag):
        if stop_after == tag:
            raise _Stop()
    try:
      for l in range(n_layers):
        x_src = xT_in if l == 0 else out
        stop("setup")
        ld(pcol[:], pcols.ap()[:, l:l + 1, :], "pcol")
        hT2 = big2[:].rearrange("p (c t) -> p c t", c=16)
        groups = [("qa", 0, 512, "fm"), ("ka", 512, 512, "fm"), ("va", 1024, 512, "tm"),
                  ("qb", 1536, 512, "fm"), ("kb", 2048, 128, "fm"), ("vb", 2176, 128, "tm"),
                  ("qc", 2304, 512, "fm"), ("kc", 2816, 512, "fm"), ("vc", 3328, 512, "tm")]
        groups += [("g%d" % i, 3840 + i * 512, 512, "gate") for i in range(12)]
        vcol = {"va": 0, "vb": 512, "vc": 640}
        for tp in range(NT // 2):
            for half in range(2):
                tt = 2 * tp + half
                ld(xt[:], x_src.ap()[:, tt * TW:(tt + 1) * TW].rearrange("(c p) t -> p c t", p=128), "xt",
                   reads=[tokkey("x", tt)])
                make_h(l, PC_PREMIX, dst=lambda c, half=half: hT2[:, c, half * TW:(half + 1) * TW])
            for gi, (nm, c0, ncol, kind) in enumerate(groups):
                wv, wk = load_w(w_in.ap()[l, :, c0:c0 + ncol], 16, ncol)
                nch = ncol // 128
                for half in range(2):
                    tt = 2 * tp + half
                    t0 = tt * TW
                    hTh = hT2[:, :, half * TW:(half + 1) * TW]
                    sidx = (2 * gi + half) % 2
                    st, sk = stg[sidx], "stg%d" % sidx
                    if kind in ("fm", "gate"):
                        def cons(j, ps, pk, nm=nm, kind=kind, st=st, sk=sk, gi=gi, t0=t0, tt=tt):
                            if kind == "gate":
                                bc_ = PC_BGATE + (gi - 9) * 4 + j
                                S_.op("act", lambda e: e.activation(out=st[:, j, :], in_=ps[:], func=AF.Sigmoid,
                                                                    bias=pcol[:, 0, bc_:bc_ + 1], scale=1.0),
                                      reads=[pk, "pcol"], writes=[sk])
                            else:
                                S_.op("act", lambda e: e.activation(out=st[:, j, :], in_=ps[:], func=AF.Identity),
                                      reads=[pk], writes=[sk])
                                if nm == "qc":
                                    S_.op("act", lambda e: e.activation(out=ctmp[2][:], in_=ps[:], func=AF.Identity),
                                          reads=[pk], writes=["ctmp2"])
                                    d2 = qc32.ap()[j * 128:(j + 1) * 128, t0:t0 + TW]
                                    S_.dma("sp", lambda e, d2=d2: e.dma_start(out=d2, in_=ctmp[2][:]), "qc32w",
                                           reads=["ctmp2"], writes=[tokkey("qc32", tt)])
                        gemm_fm(wv, wk, 16, nch, lambda k, hTh=hTh: hTh[:, k, :], ["hT"], cons)
                        if kind == "gate":
                            r0 = (gi - 9) * 512
                            dst = gT.ap()[r0:r0 + 512, t0:t0 + TW].rearrange("(j p) t -> p j t", p=128)
                            S_.dma("sp", lambda e, dst=dst, st=st: e.dma_start(out=dst, in_=st[:]), "stw%d" % sidx,
                                   reads=[sk], writes=[tokkey("gT", tt)])
                        else:
                            r0 = QK_ROW[nm]
                            dst = qkT.ap()[r0:r0 + ncol, t0:t0 + TW].rearrange("(j p) t -> p j t", p=128)
                            S_.dma("sp", lambda e, dst=dst, st=st, nch=nch: e.dma_start(out=dst, in_=st[:, 0:nch, :]),
                                   "stw%d" % sidx, reads=[sk], writes=[tokkey("qkT", tt)])
                    else:
                        stv = st[:].rearrange("p j t -> p (j t)")[:, 0:4 * ncol].rearrange("p (i n) -> p i n", i=4)
                        for i in range(4):
                            ps, pk = next_pb()

                            def mm(e, i=i, ps=ps, wv=wv, ncol=ncol, hTh=hTh):
                                last = None
                                for k in range(16):
                                    last = e.matmul(ps[:, 0:ncol], lhsT=hTh[:, k, i * 128:(i + 1) * 128], rhs=wv[:, k, :],
                                                    start=(k == 0), stop=(k == 15))
                                return last
                            S_.op("pe", mm, reads=wk + ["hT"], writes=[pk])
                            S_.op("act", lambda e, i=i, ps=ps, ncol=ncol, stv=stv: e.activation(
                                out=stv[:, i, :], in_=ps[:, 0:ncol], func=AF.Identity), reads=[pk], writes=[sk])
                        v0 = vcol[nm]
                        dst = vS.ap()[t0:t0 + TW, v0:v0 + ncol].rearrange("(i p) n -> p i n", p=128)
                        S_.dma("sp", lambda e, dst=dst, stv=stv: e.dma_start(out=dst, in_=stv), "stw%d" % sidx,
                               reads=[sk], writes=[tokkey("vS", tt)])

        stop("S1")
        S_.barrier()
        allk = lambda nm: [tokkey(nm, tt) for tt in range(NT)]
        bigv = big[:]
        qT = bigv[:, 0:8192].rearrange("p (c t) -> p c t", c=4)
        kT = bigv[:, 8192:16384].rearrange("p (c t) -> p c t", c=4)
        vA = bigv[:, 16384:16384 + 16 * 4 * 129].rearrange("p (b h e) -> p b h e", b=16, h=4)
        vC = bigv[:, 16384:16384 + 16 * 8 * 65].rearrange("p (b h e) -> p b h e", b=16, h=8)
        q32 = xt[:].rearrange("p c t -> p (c t)").rearrange("p (c t) -> p c t", c=4)
        ytok = bigv[:, 24704:24704 + 2048].rearrange("p (j n) -> p j n", j=4)
        PT = [stg[0][:].rearrange("p j t -> p (j t)")[:, i * 512:(i + 1) * 512] for i in range(4)]
        PT += [ubuf[i][:].bitcast(BF16)[:, 0:512] for i in range(2)]
        PTK = ["PT%d" % i for i in range(6)]
        SPS = [(PB[0], "pb0"), (PB[1], "pb1"), (PB[6], "pb6"), (PB[7], "pb7")]
        pti = [0]
        tmp32 = ctmp[0]
        accs = [(PB[2], PB[3], "pb2", "pb3"), (PB[4], PB[5], "pb4", "pb5")]
        ysT = stg[1][:].rearrange("p j t -> p (j t)")

        pend = [None]
        pend_epi = [None]

        def run_epi():
            if pend_epi[0] is not None:
                f_ = pend_epi[0]
                pend_epi[0] = None
                f_()

        def run_pv(p):
            fn, kt_, items = p
            fn.__defaults__[-1][0] = kt_
            for (j, ap_, k_) in items:
                fn(j, ap_, k_)

        def flush_pv():
            if pend[0] is not None:
                run_pv(pend[0])
                pend[0] = None
            run_epi()

        def attn_tile(hh, kt_ap, q_ap_fn, kt, jlist, pv_fn, scale_bias=True, swa=False):
            groups_ = []
            far = [j for j in jlist if j - kt >= 2]
            if far:
                groups_.append(("far", far))
            if kt + 1 in jlist:
                groups_.append(("near", [kt + 1]))
            if kt in jlist:
                groups_.append(("diag", [kt]))
            for kind, js in groups_:
                n = len(js) * 128
                sps, spk = SPS[pti[0] % 4]
                pt, ptk = PT[pti[0] % 6], PTK[pti[0] % 6]
                pti[0] += 1
                if kind == "far":
                    S_.op("pe", lambda e, js=js, n=n, sps=sps: e.matmul(sps[:, 0:n], lhsT=kt_ap, rhs=q_ap_fn(js[0] * 128, n),
                                                                      start=True, stop=True),
                          reads=["attn_in"], writes=[spk])
                    S_.op("act", lambda e, n=n, sps=sps, pt=pt: e.activation(out=pt[:, 0:n], in_=sps[:, 0:n], func=AF.Exp,
                                                                            bias=cfar[:, hh:hh + 1], scale=0.125),
                          reads=[spk, "cfar"], writes=[ptk])
                else:
                    dl = 1 if kind == "near" else 0

                    def stb(e, js=js, sps=sps, dl=dl):
                        e.matmul(sps[:, 0:128], lhsT=kt_ap, rhs=q_ap_fn(js[0] * 128, 128), start=True, stop=False)
                        return e.matmul(sps[:, 0:128], lhsT=ident32[:], rhs=biasT[:, hh, dl, :], start=False, stop=True)
                    S_.op("pe", stb, reads=["attn_in", "ident32"] + bias_keys, writes=[spk])
                    S_.op("act", lambda e, sps=sps, pt=pt: e.activation(out=pt[:, 0:128], in_=sps[:, 0:128], func=AF.Exp,
                                                                       scale=0.125),
                          reads=[spk], writes=[ptk])
                prev = pend[0]
                pend[0] = (pv_fn, kt, [(j, pt[:, ji * 128:(ji + 1) * 128], ptk) for ji, j in enumerate(js)])
                if prev is not None:
                    run_pv(prev)
                run_epi()

        lam_init = 0.8 - 0.6 * math.exp(-0.3 * l)
        lamt = small[:, 0:8]
        lvt = lvt_t[:].rearrange("p (a b) -> p a b", a=4)
        ld(lvt, bass.AP(lamv, l * 256, [[0, 128], [64, 4], [1, 64]]), "lvt")
        gsub = gsub_t[:]
        ld(gsub, bass.AP(subg, l * 128, [[0, 128], [1, 128]]), "gsub")
        esink = small[:, 8:16]
        ld(esink, bass.AP(sinks, l * 8, [[0, 128], [1, 8]]), "esink")
        S_.op("dve", lambda e: e.tensor_tensor(out=lvt[:, 0, :], in0=lvt[:, 0, :], in1=lvt[:, 1, :], op=ALU.mult),
              reads=["lvt"], writes=["lvt"])
        S_.op("dve", lambda e: e.tensor_tensor(out=lvt[:, 2, :], in0=lvt[:, 2, :], in1=lvt[:, 3, :], op=ALU.mult),
              reads=["lvt"], writes=["lvt"])
        S_.op("dve", lambda e: e.reduce_sum(out=lamt[:, 0:1], in_=lvt[:, 0, :], axis=AX.X), reads=["lvt"], writes=["lamt"])
        S_.op("dve", lambda e: e.reduce_sum(out=lamt[:, 1:2], in_=lvt[:, 2, :], axis=AX.X), reads=["lvt"], writes=["lamt"])
        S_.op("act", lambda e: e.activation(out=lamt[:, 2:4], in_=lamt[:, 0:2], func=AF.Exp), reads=["lamt"], writes=["lamt"])
        S_.op("dve", lambda e: e.scalar_tensor_tensor(out=lamt[:, 4:5], in0=lamt[:, 3:4], scalar=-lam_init, in1=lamt[:, 2:3],
                                                      op0=ALU.add, op1=ALU.subtract), reads=["lamt"], writes=["lamt"])
        S_.op("dve", lambda e: e.tensor_scalar(out=gsub, in0=gsub, scalar1=(1.0 - lam_init), scalar2=None, op0=ALU.mult),
              reads=["gsub"], writes=["gsub"])
        S_.op("act", lambda e: e.activation(out=esink, in_=esink, func=AF.Exp), reads=["esink"], writes=["esink"])

        def flush_y(jc, ncols_used, row0):
            nchk = ncols_used // 128
            for c in range(nchk):
                tp = PB[6][:].bitcast(BF16)[:, 0:512]
                for jj in range(4):
                    S_.op("pe", lambda e, c=c, jj=jj: e.transpose(tp[:, jj * 128:(jj + 1) * 128],
                                                                 ytok[:, jj, c * 128:(c + 1) * 128], ident[:]),
                          reads=["ytok", "ident"], writes=["pb6"])
                S_.op("act", lambda e, c=c: e.activation(out=ysT[:, c * 512:(c + 1) * 512], in_=tp, func=AF.Identity),
                      reads=["pb6"], writes=["ysT"])
            dst = yT.ap()[row0:row0 + ncols_used, jc * 512:(jc + 1) * 512].rearrange("(c p) t -> p c t", p=128)
            S_.dma("sp", lambda e: e.dma_start(out=dst, in_=ysT[:, 0:nchk * 512].rearrange("p (c t) -> p c t", c=nchk)),
                   "yTw", reads=["ysT"], writes=["yT"])

        ld(qT, qkT.ap()[0:512, :].rearrange("(c p) t -> p c t", p=128), "attn_in", reads=allk("qkT"))
        ld(kT, qkT.ap()[512:1024, :].rearrange("(c p) t -> p c t", p=128), "attn_in", reads=allk("qkT"))
        S_.op("pool", lambda e: e.memset(vA[:, :, :, 128:129], 1.0), writes=["attn_in"])
        for h_ in range(4):
            ld(vA[:, :, h_, 0:128], vS.ap()[:, h_ * 128:(h_ + 1) * 128].rearrange("(b p) e -> p b e", p=128), "attn_in",
               reads=allk("vS"))
        omt = ctmp[2][:].rearrange("p (j e) -> p j e", j=4)
        om1 = ctmp[1][:].rearrange("p (j e) -> p j e", j=4)
        ai = 0
        for jc in range(4):
            jlist = list(range(4 * jc, 4 * jc + 4))
            for h in range(4):
                for m in range(2):
                    a0, a1, k0, k1 = accs[ai % 2]
                    ai += 1
                    nkt = 4 * jc + 4

                    touched = {}
                    tot = {k0: 8 * jc + 3, k1: 8 * jc + 7}

                    def pv(j, ptap, ptk, h=h, a0=a0, a1=a1, k0=k0, k1=k1, touched=touched, tot=tot, ktc=[None]):
                        jj = j - 4 * jc
                        bank, bk = (a0, k0) if jj < 2 else (a1, k1)
                        o0 = (jj % 2) * 256
                        kt_ = ktc[0]
                        st_ = bk not in touched
                        touched[bk] = touched.get(bk, 0) + 1
                        sp_ = touched[bk] == tot[bk]
                        S_.op("pe", lambda e: e.matmul(bank[:, o0:o0 + 129], lhsT=ptap, rhs=vA[:, kt_, h, :],
                                                      start=st_, stop=sp_),
                              reads=[ptk, "attn_in"], writes=[bk])
                    for kt in range(nkt):
                        pv.__defaults__[-1][0] = kt
                        attn_tile(h, kT[m * 64:(m + 1) * 64, h, kt * 128:(kt + 1) * 128],
                                  lambda q0, n, m=m, h=h: qT[m * 64:(m + 1) * 64, h, q0:q0 + n], kt, jlist, pv)
                    flush_pv()
                    dstm = omt if m == 0 else om1
                    for jj in range(4):
                        bank, bk = (a0, k0) if jj < 2 else (a1, k1)
                        o0 = (jj % 2) * 256
                        S_.op("dve", lambda e, bank=bank, o0=o0, jj=jj: e.reciprocal(out=small[:, 16 + jj:17 + jj],
                                                                                    in_=bank[:, o0 + 128:o0 + 129]),
                              reads=[bk], writes=["rden"])
                        S_.op("dve", lambda e, bank=bank, o0=o0, jj=jj, dstm=dstm: e.tensor_scalar(
                            out=dstm[:, jj, :], in0=bank[:, o0:o0 + 128], scalar1=small[:, 16 + jj:17 + jj], scalar2=None,
                            op0=ALU.mult), reads=[bk, "rden"], writes=["om%d" % m])
                S_.op("dve", lambda e: e.scalar_tensor_tensor(out=omt, in0=om1, scalar=lamt[:, 4:5], in1=omt,
                                                              op0=ALU.mult, op1=ALU.add),
                      reads=["om0", "om1", "lamt"], writes=["om0"])
                S_.op("dve", lambda e: e.tensor_tensor(out=om1, in0=omt, in1=omt, op=ALU.mult), reads=["om0"], writes=["om1"])
                S_.op("dve", lambda e: e.reduce_sum(out=small[:, 20:24], in_=om1, axis=AX.X), reads=["om1"], writes=["ssq"])
                S_.op("dve", lambda e: e.tensor_scalar(out=small[:, 20:24], in0=small[:, 20:24], scalar1=1.0 / 128, scalar2=EPS,
                                                       op0=ALU.mult, op1=ALU.add), reads=["ssq"], writes=["ssq"])
                S_.op("act", lambda e: e.activation(out=small[:, 20:24], in_=small[:, 20:24], func=AF.Sqrt), reads=["ssq"], writes=["ssq"])
                S_.op("dve", lambda e: e.reciprocal(out=small[:, 20:24], in_=small[:, 20:24]), reads=["ssq"], writes=["ssq"])
                for jj in range(4):
                    S_.op("dve", lambda e, jj=jj, h=h: e.scalar_tensor_tensor(
                        out=ytok[:, jj, h * 128:(h + 1) * 128], in0=omt[:, jj, :], scalar=small[:, 20 + jj:21 + jj],
                        in1=gsub, op0=ALU.mult, op1=ALU.mult), reads=["om0", "ssq", "gsub"], writes=["ytok"])
            flush_y(jc, 512, 0)

        stop("A")
        kB = kT[:, 0:2, :]
        ld(qT, qkT.ap()[1024:1536, :].rearrange("(c p) t -> p c t", p=128), "attn_in", reads=allk("qkT") + ["yT"])
        for g in range(2):
            for half in range(2):
                ld(kB[half * 64:(half + 1) * 64, g, :], qkT.ap()[1536 + g * 64:1536 + (g + 1) * 64, :], "attn_in")
        vB = vC[:, :, 0:2, :]
        S_.op("pool", lambda e: e.memset(vB[:, :, :, 64:65], 1.0), writes=["attn_in"])
        for h_ in range(2):
            ld(vB[:, :, h_, 0:64], vS.ap()[:, 512 + h_ * 64:512 + (h_ + 1) * 64].rearrange("(b p) e -> p b e", p=128), "attn_in")
        for jc in range(4):
            jlist = list(range(4 * jc, 4 * jc + 4))
            for h in range(8):
                a0, a1, k0, k1 = accs[ai % 2]
                ai += 1
                g = h // 4
                pb_ = (h % 2) * 64

                touchedb = {}
                totb = 7 if jc == 0 else 8

                def pvb(j, ptap, ptk, g=g, a0=a0, k0=k0, touchedb=touchedb, totb=totb, ktc=[None]):
                    jj = j - 4 * jc
                    kt_ = ktc[0]
                    first = k0 not in touchedb
                    touchedb[k0] = touchedb.get(k0, 0) + 1
                    lastb = touchedb[k0] == totb
                    S_.op("pe", lambda e: e.matmul(a0[:, jj * 128:jj * 128 + 65], lhsT=ptap, rhs=vB[:, kt_, g, :],
                                                  start=first, stop=lastb),
                          reads=[ptk, "attn_in"], writes=[k0])
                for kt in range(max(0, 4 * jc - 1), 4 * jc + 4):
                    pvb.__defaults__[-1][0] = kt
                    js = [j for j in jlist if j in (kt, kt + 1)]
                    attn_tile(4 + h, kB[pb_:pb_ + 64, g, kt * 128:(kt + 1) * 128],
                              lambda q0, n, h=h, pb_=pb_: qT[pb_:pb_ + 64, h // 2, q0:q0 + n], kt, js, pvb)
                flush_pv()
                for jj in range(4):
                    S_.op("dve", lambda e, jj=jj, h=h, a0=a0: e.tensor_scalar(
                        out=small[:, 16 + jj:17 + jj], in0=a0[:, jj * 128 + 64:jj * 128 + 65], scalar1=esink[:, h:h + 1],
                        scalar2=None, op0=ALU.add), reads=[k0, "esink"], writes=["rden"])
                    S_.op("dve", lambda e, jj=jj: e.reciprocal(out=small[:, 16 + jj:17 + jj], in_=small[:, 16 + jj:17 + jj]),
                          reads=["rden"], writes=["rden"])
                    S_.op("dve", lambda e, jj=jj, h=h, a0=a0: e.tensor_scalar(
                        out=ytok[:, jj, h * 64:(h + 1) * 64], in0=a0[:, jj * 128:jj * 128 + 64],
                        scalar1=small[:, 16 + jj:17 + jj], scalar2=None, op0=ALU.mult),
                        reads=[k0, "rden"], writes=["ytok"])
            flush_y(jc, 512, 512)

        stop("B")
        ld(qT, qkT.ap()[1664:2176, :].rearrange("(c p) t -> p c t", p=128), "attn_in", reads=allk("qkT") + ["yT"])
        ld(kT, qkT.ap()[2176:2688, :].rearrange("(c p) t -> p c t", p=128), "attn_in")
        ld(q32, qc32.ap().rearrange("(c p) t -> p c t", p=128), "attn_in", reads=allk("qc32"))
        S_.op("pool", lambda e: e.memset(vC[:, :, :, 64:65], 1.0), writes=["attn_in"])
        for h_ in range(8):
            ld(vC[:, :, h_, 0:64], vS.ap()[:, 640 + h_ * 64:640 + (h_ + 1) * 64].rearrange("(b p) e -> p b e", p=128), "attn_in")
        kmean = kmean_t[:].rearrange("p (c n) -> p c n", c=4)
        for c_ in range(4):
            S_.op("dve", lambda e, c_=c_: e.reduce_sum(out=kmean_t[:, c_ * 8:(c_ + 1) * 8],
                                                      in_=kT[:, c_, :].rearrange("p (n s) -> p n s", s=256), axis=AX.X),
                  reads=["attn_in"], writes=["kmean"])
        khi = khl_t[:, 0:32].rearrange("p (c n) -> p c n", c=4)
        klo = khl_t[:, 32:64].rearrange("p (c n) -> p c n", c=4)
        qlo = big2[:, 4096:4096 + 8192].rearrange("p (c t) -> p c t", c=4)
        S_.op("act", lambda e: e.activation(out=khl_t[:, 0:32], in_=kmean_t[:], func=AF.Identity), reads=["kmean"], writes=["khl"])
        S_.op("dve", lambda e: e.tensor_tensor(out=khl_t[:, 32:64], in0=kmean_t[:], in1=khl_t[:, 0:32], op=ALU.subtract),
              reads=["kmean", "khl"], writes=["khl"])
        for c_ in range(4):
            S_.op("dve", lambda e, c_=c_: e.tensor_tensor(out=qlo[:, c_, :], in0=q32[:, c_, :], in1=qT[:, c_, :], op=ALU.subtract),
                  reads=["attn_in"], writes=["qlo"])
        def selbc(i, jc):
            a = selc[:, i, 4 * jc:4 * jc + 4, :]
            return bass.AP(a.tensor, a.offset, [list(a.ap[0]), [8, 4], [0, 8], [1, 8]])
        stop("C0")
        selt = sqt[0][:, 0:256].rearrange("p (j h n) -> p j h n", j=4, h=8)
        gte = sqt[1][:, 0:256].rearrange("p (j h n) -> p j h n", j=4, h=8)
        cmp_ = big2[:].bitcast(F32)[:, 0:2048]
        accC = ctmp[2][:].rearrange("p (j e) -> p j e", j=4)
        for jc in range(4):
            jlist = list(range(4 * jc, 4 * jc + 4))
            gcnt = {0: 0, 1: 0}
            for jj in range(4):
                j = 4 * jc + jj
                for h in range(8):
                    pb_ = (h % 2) * 64
                    def gmm(e, jj=jj, j=j, h=h, pb_=pb_):
                        qs = slice(j * 128, (j + 1) * 128)
                        par = h % 2
                        bank = PB[7] if par == 0 else PB[6]
                        c0_ = ((h // 2) * 4 + jj) * 8
                        gc_ = bank[:, c0_:c0_ + 8]
                        first = gcnt[par] == 0
                        gcnt[par] += 3
                        e.matmul(gc_, lhsT=qT[pb_:pb_ + 64, h // 2, qs], rhs=khi[pb_:pb_ + 64, h // 2, :],
                                 start=first, stop=False)
                        e.matmul(gc_, lhsT=qT[pb_:pb_ + 64, h // 2, qs], rhs=klo[pb_:pb_ + 64, h // 2, :],
                                 start=False, stop=False)
                        return e.matmul(gc_, lhsT=qlo[pb_:pb_ + 64, h // 2, qs], rhs=khi[pb_:pb_ + 64, h // 2, :],
                                        start=False, stop=(gcnt[par] == 48))
                    S_.op("pe", gmm, reads=["attn_in", "khl", "qlo"], writes=["pb7", "pb6"])
            stop("C1")
            stop("C1_%d" % jc)
            g2 = sqt[1][:, 0:256]
            s2 = sqt[0][:, 0:256]
            gp = PB[7][:, 0:256]
            P0 = list(g2.ap[0])
            Ps = list(s2.ap[0])

            def mk(i, jc=jc):
                a = selc[:, i, 4 * jc:4 * jc + 4, :]
                return bass.AP(a.tensor, a.offset, [list(a.ap[0]), [1, 8], [0, 8], [8, 4]])
            for par in range(2):
                gpb = (PB[7] if par == 0 else PB[6])[:, 0:128]
                in_ps = bass.AP(gpb.tensor, gpb.offset, [list(gpb.ap[0]), [1, 8], [32, 4], [8, 4]])
                out_g = bass.AP(g2.tensor, g2.offset + par * 4, [P0, [32, 8], [8, 4], [1, 4]])
                a_ = selc[:, 2, 4 * jc:4 * jc + 4, :]
                m2 = bass.AP(a_.tensor, a_.offset, [list(a_.ap[0]), [1, 8], [0, 4], [8, 4]])
                S_.op("dve", lambda e, in_ps=in_ps, out_g=out_g, m2=m2: e.tensor_tensor(out=out_g, in0=in_ps, in1=m2, op=ALU.add),
                      reads=["pb7", "pb6", "selc"], writes=["gte"])
            in0 = bass.AP(g2.tensor, g2.offset, [P0, [0, 8], [32, 8], [1, 32]])
            in1 = bass.AP(g2.tensor, g2.offset, [P0, [32, 8], [0, 8], [1, 32]])
            cmpv = cmp_.rearrange("p (n m a) -> p n m a", n=8, m=8)
            S_.op("dve", lambda e, in0=in0, in1=in1: e.tensor_tensor(out=cmpv, in0=in0, in1=in1, op=ALU.subtract),
                  reads=["gte"], writes=["cmp"])
            S_.op("dve", lambda e: e.tensor_scalar(out=cmp_, in0=cmp_, scalar1=1e20, scalar2=1.0, op0=ALU.mult, op1=ALU.min),
                  reads=["cmp"], writes=["cmp"])
            S_.op("dve", lambda e: e.tensor_scalar(out=cmp_, in0=cmp_, scalar1=0.0, scalar2=None, op0=ALU.max),
                  reads=["cmp"], writes=["cmp"])
            cin = bass.AP(cmp_.tensor, cmp_.offset, [list(cmp_.ap[0]), [256, 8], [1, 32], [32, 8]])
            s2na = bass.AP(s2.tensor, s2.offset, [Ps, [32, 8], [1, 32]])
            S_.op("dve", lambda e, cin=cin, s2na=s2na: e.reduce_sum(out=s2na, in_=cin, axis=AX.X), reads=["cmp"], writes=["selt"])
            S_.op("dve", lambda e: e.tensor_scalar(out=s2, in0=s2, scalar1=-1.0, scalar2=2.5, op0=ALU.mult, op1=ALU.add),
                  reads=["selt"], writes=["selt"])
            S_.op("dve", lambda e: e.tensor_scalar(out=s2, in0=s2, scalar1=1e20, scalar2=1.0, op0=ALU.mult, op1=ALU.min),
                  reads=["selt"], writes=["selt"])
            S_.op("dve", lambda e: e.tensor_scalar(out=s2, in0=s2, scalar1=0.0, scalar2=None, op0=ALU.max),
                  reads=["selt"], writes=["selt"])
            s2v = bass.AP(s2.tensor, s2.offset, [Ps, [32, 8], [4, 8], [1, 4]])
            S_.op("dve", lambda e, s2v=s2v, m0=mk(0): e.tensor_tensor(out=s2v, in0=s2v, in1=m0, op=ALU.mult),
                  reads=["selt", "selc"], writes=["selt"])
            S_.op("dve", lambda e, s2v=s2v, m1=mk(1): e.tensor_tensor(out=s2v, in0=s2v, in1=m1, op=ALU.add),
                  reads=["selt", "selc"], writes=["selt"])
            stop("C2")
            stop("C2_%d" % jc)
            for h in range(8):
                pb_ = (h % 2) * 64
                for nb in range(2 * jc + 2):
                    a0, a1, k0, k1 = accs[ai % 2]
                    ai += 1
                    js_n = [j for j in jlist if j // 2 >= nb]

                    touchedc = {}
                    totc = sum((1 if j == 2 * nb else 2) for j in js_n)

                    def pvc(j, ptap, ptk, h=h, a0=a0, k0=k0, nb=nb, touchedc=touchedc, totc=totc, ktc=[None]):
                        jj = j - 4 * jc
                        kt_ = ktc[0]
                        first = k0 not in touchedc
                        touchedc[k0] = touchedc.get(k0, 0) + 1
                        last = touchedc[k0] == totc
                        S_.op("pe", lambda e: e.matmul(a0[:, jj * 128:jj * 128 + 65], lhsT=ptap, rhs=vC[:, kt_, h, :],
                                                      start=first, stop=last),
                              reads=[ptk, "attn_in"], writes=[k0])
                    for kt in (2 * nb, 2 * nb + 1):
                        pvc.__defaults__[-1][0] = kt
                        js = [j for j in js_n if j >= kt]
                        if not js:
                            continue
                        attn_tile(12 + h, kT[pb_:pb_ + 64, h // 2, kt * 128:(kt + 1) * 128],
                                  lambda q0, n, h=h, pb_=pb_: qT[pb_:pb_ + 64, h // 2, q0:q0 + n], kt, js, pvc)
                    def epi(js_n=js_n, nb=nb, h=h, a0=a0, k0=k0):
                        for j in js_n:
                            jj = j - 4 * jc
                            if nb == 0:
                                S_.op("dve", lambda e, jj=jj, h=h, a0=a0, nb=nb: e.tensor_scalar(
                                    out=accC[:, jj, 0:65], in0=a0[:, jj * 128:jj * 128 + 65], scalar1=sqt[0][:, nb * 32 + h * 4 + jj:nb * 32 + h * 4 + jj + 1],
                                    scalar2=None, op0=ALU.mult), reads=[k0, "selt"], writes=["accC"])
                            else:
                                S_.op("dve", lambda e, jj=jj, h=h, a0=a0, nb=nb: e.scalar_tensor_tensor(
                                    out=accC[:, jj, 0:65], in0=a0[:, jj * 128:jj * 128 + 65], scalar=sqt[0][:, nb * 32 + h * 4 + jj:nb * 32 + h * 4 + jj + 1],
                                    in1=accC[:, jj, 0:65], op0=ALU.mult, op1=ALU.add), reads=[k0, "selt", "accC"], writes=["accC"])
                    run_epi()
                    pend_epi[0] = epi
                flush_pv()
                for jj in range(4):
                    S_.op("dve", lambda e, jj=jj: e.reciprocal(out=small[:, 16 + jj:17 + jj], in_=accC[:, jj, 64:65]),
                          reads=["accC"], writes=["rden"])
                    S_.op("dve", lambda e, jj=jj, h=h: e.tensor_scalar(
                        out=ytok[:, jj, h * 64:(h + 1) * 64], in0=accC[:, jj, 0:64], scalar1=small[:, 16 + jj:17 + jj],
                        scalar2=None, op0=ALU.mult), reads=["accC", "rden"], writes=["ytok"])
                stop("C4")
                stop("C4_%d_%d" % (jc, h))
            flush_y(jc, 512, 1024)
            stop("C5")

        stop("C")
        S_.barrier()
        yTt = bigv[:, 0:6144].rearrange("p (c t) -> p c t", c=12)
        gts = bigv[:, 6144:12288].rearrange("p (b c t) -> p b c t", b=3, c=4)
        mixT = bigv[:, 12288:20480].rearrange("p (c t) -> p c t", c=16)
        actT = bigv[:, 0:44 * TW].rearrange("p (c t) -> p c t", c=44)
        for tt in range(NT):
            t0 = tt * TW
            ld(xt[:], x_src.ap()[:, t0:t0 + TW].rearrange("(c p) t -> p c t", p=128), "xt", reads=[tokkey("x", tt)])
            ld(yTt, yT.ap()[:, t0:t0 + TW].rearrange("(c p) t -> p c t", p=128), "yTt", reads=["yT", "attn_in", "ytok"])
            for cg in range(4):
                for br in range(3):
                    r0 = br * 2048 + cg * 512
                    ld(gts[:, br, :, :], gT.ap()[r0:r0 + 512, t0:t0 + TW].rearrange("(c p) t -> p c t", p=128), "gts",
                       reads=allk("gT") + ["attn_in"])
                wt, wk = wslot()
                wv = wt[:, 0:6144].rearrange("p (b k n) -> p b k n", b=3, k=4)
                for br in range(3):
                    S_.dma("pool", lambda e, br=br, wv=wv: e.dma_start(
                        out=wv[:, br, :, :], in_=w_o[br].ap()[l, :, cg * 512:(cg + 1) * 512].rearrange("(k p) n -> p k n", p=128)),
                        "ws%d" % (int(wk[0][2:]) // 2), writes=wk)
                for ch in range(4):
                    pss = []
                    for br in range(3):
                        ps, pk = next_pb()

                        def mm(e, br=br, ch=ch, ps=ps, wv=wv):
                            last = None
                            for k in range(4):
                                last = e.matmul(ps[:], lhsT=wv[:, br, k, ch * 128:(ch + 1) * 128], rhs=yTt[:, br * 4 + k, :],
                                                start=(k == 0), stop=(k == 3))
                            return last
                        S_.op("pe", mm, reads=wk + ["yTt"], writes=[pk])
                        pss.append((ps, pk))
                    t_a, t_b = ctmp[0], ctmp[1]
                    S_.op("dve", lambda e, ch=ch, pss=pss: e.tensor_tensor(out=t_a[:], in0=pss[0][0][:], in1=gts[:, 0, ch, :], op=ALU.mult),
                          reads=[pss[0][1], "gts"], writes=["t_a"])
                    S_.op("dve", lambda e, ch=ch, pss=pss: e.tensor_tensor(out=t_b[:], in0=pss[1][0][:], in1=gts[:, 1, ch, :], op=ALU.mult),
                          reads=[pss[1][1], "gts"], writes=["t_b"])
                    S_.op("dve", lambda e: e.tensor_tensor(out=t_a[:], in0=t_a[:], in1=t_b[:], op=ALU.add),
                          reads=["t_a", "t_b"], writes=["t_a"])
                    S_.op("dve", lambda e, ch=ch, pss=pss: e.tensor_tensor(out=t_b[:], in0=pss[2][0][:], in1=gts[:, 2, ch, :], op=ALU.mult),
                          reads=[pss[2][1], "gts", "t_b"], writes=["t_b"])
                    S_.op("dve", lambda e, ch=ch: e.tensor_tensor(out=mixT[:, cg * 4 + ch, :], in0=t_a[:], in1=t_b[:], op=ALU.add),
                          reads=["t_a", "t_b"], writes=["mixT"])
            for cg in range(4):
                wv, wk = load_w(w_out.ap()[l, :, cg * 512:(cg + 1) * 512], 16, 512)

                def cons(j, ps, pk, cg=cg):
                    S_.op("act", lambda e: e.activation(out=oT[:, cg * 4 + j, :], in_=ps[:], func=AF.Identity),
                          reads=[pk], writes=["oT"])
                gemm_fm(wv, wk, 16, 4, lambda k: mixT[:, k, :], ["mixT"], cons)
            post_norm_residual(l, PC_POSTMIX)
            make_h(l, PC_PREFFN)
            gbuf = [stg[i_][:].rearrange("p j t -> p (j t)").bitcast(F32).rearrange("p (j t) -> p j t", j=2) for i_ in range(2)]

            def up_chunk(wv, wk, j, fch, which):
                ps, pk = next_pb()

                def mm(e, j=j, ps=ps, wv=wv):
                    last = None
                    for k in range(16):
                        last = e.matmul(ps[:], lhsT=wv[:, k, j * 128:(j + 1) * 128], rhs=hT[:, k, :],
                                        start=(k == 0), stop=(k == 15))
                    return last
                S_.op("pe", mm, reads=wk + ["hT"], writes=[pk])
                ub, uk = ubuf[which], "ubuf%d" % which
                S_.op("act", lambda e: e.activation(out=ub[:, 0:2], in_=halo[:, fch, :], func=AF.Identity),
                      reads=["halo"], writes=[uk])
                S_.op("act", lambda e: e.activation(out=ub[:, 2:TW + 2], in_=ps[:], func=AF.Identity),
                      reads=[pk], writes=[uk])
                S_.op("act", lambda e: e.activation(out=halo[:, fch, :], in_=ub[:, TW:TW + 2], func=AF.Identity),
                      reads=[uk], writes=["halo"])
                ct, ck = ctmp[which], "ctmp%d" % which
                cw = PC_CW
                S_.op("dve", lambda e: e.tensor_scalar(
                    out=ct[:], in0=ub[:, 2:TW + 2], scalar1=pcol[:, 0, cw + 176 + fch:cw + 177 + fch],
                    scalar2=pcol[:, 0, PC_CB + fch:PC_CB + fch + 1], op0=ALU.mult, op1=ALU.add),
                    reads=[uk, "pcol"], writes=[ck])
                S_.op("dve", lambda e: e.scalar_tensor_tensor(
                    out=ct[:], in0=ub[:, 1:TW + 1], scalar=pcol[:, 0, cw + 88 + fch:cw + 89 + fch], in1=ct[:],
                    op0=ALU.mult, op1=ALU.add), reads=[uk, ck], writes=[ck])
                S_.op("dve", lambda e: e.scalar_tensor_tensor(
                    out=ct[:], in0=ub[:, 0:TW], scalar=pcol[:, 0, cw + fch:cw + fch + 1], in1=ct[:],
                    op0=ALU.mult, op1=ALU.add), reads=[uk, ck], writes=[ck])
                return ct, ck

            for i4 in range(11):
                wvg, wkg = load_w(w_up.ap()[l, :, i4 * 512:(i4 + 1) * 512], 16, 512)
                for j in range(4):
                    ct, ck = up_chunk(wvg, wkg, j, i4 * 4 + j, 0)
                    gb, gk = gbuf[j // 2][:, j % 2, :], "gb%d" % j
                    S_.op("act", lambda e, ct=ct, gb=gb: e.activation(out=gb, in_=ct[:], func=AF.Gelu_apprx_tanh),
                          reads=[ck], writes=[gk])
                wvv, wkv = load_w(w_up.ap()[l, :, DFF + i4 * 512:DFF + (i4 + 1) * 512], 16, 512)
                for j in range(4):
                    fi = i4 * 4 + j
                    ct, ck = up_chunk(wvv, wkv, j, fi + 44, 1)
                    gb, gk = gbuf[j // 2][:, j % 2, :], "gb%d" % j
                    S_.op("dve", lambda e, ct=ct, gb=gb, fi=fi: e.tensor_tensor(out=actT[:, fi, :], in0=gb, in1=ct[:], op=ALU.mult),
                          reads=[gk, ck], writes=["actT"])
            for cg in range(16):
                wva, wka = load_ws(w_down.ap()[l, 0:2816, cg * 128:(cg + 1) * 128], 22, 128)
                wvb, wkb = load_ws(w_down.ap()[l, 2816:5632, cg * 128:(cg + 1) * 128], 22, 128)
                ps, pk = next_pb()

                def mm(e, ps=ps, wva=wva, wvb=wvb):
                    last = None
                    for k in range(44):
                        wv_ = wva if k < 22 else wvb
                        last = e.matmul(ps[:], lhsT=wv_[:, k % 22, :], rhs=actT[:, k, :], start=(k == 0), stop=(k == 43))
                    return last
                S_.op("pe", mm, reads=wka + wkb + ["actT"], writes=[pk])
                S_.op("act", lambda e, ps=ps, cg=cg: e.activation(out=oT[:, cg, :], in_=ps[:], func=AF.Identity),
                      reads=[pk], writes=["oT"])
            post_norm_residual(l, PC_POSTFFN)
            S_.dma("sp", lambda e, t0=t0: e.dma_start(out=out.ap()[:, t0:t0 + TW].rearrange("(c p) t -> p c t", p=128), in_=xt[:]),
                   "outw", reads=["xt"], writes=[tokkey("x", tt)])
        S_.barrier()
        S_.op("pool", lambda e: e.memset(halo[:], 0.0), reads=["halo"], writes=["halo"])
    except _Stop:
        pass
    S_.barrier()
    es.close()
    S_.close()
    return nc


_CACHE = {}


def host_inputs(inp, n_layers=DEPTH):
    f = lambda a: np.ascontiguousarray(np.asarray(a, dtype=np.float32))
    pc = np.zeros((DEPTH, 128, PC), np.float32)
    col = lambda v, n: f(v).reshape(DEPTH, n, 128).transpose(0, 2, 1)
    pc[:, :, PC_PREMIX:PC_PREMIX + 16] = col(inp["pre_mix_g"], 16)
    pc[:, :, PC_POSTMIX:PC_POSTMIX + 16] = col(inp["post_mix_g"], 16)
    pc[:, :, PC_PREFFN:PC_PREFFN + 16] = col(inp["pre_ffn_g"], 16)
    pc[:, :, PC_POSTFFN:PC_POSTFFN + 16] = col(inp["post_ffn_g"], 16)
    pc[:, :, PC_BGATE:PC_BGATE + 48] = col(inp["b_gate"], 48)
    cw = f(inp["conv_w"]).reshape(DEPTH, 3, 88, 128).transpose(0, 3, 1, 2).reshape(DEPTH, 128, 264)
    pc[:, :, PC_CW:PC_CW + 264] = cw
    pc[:, :, PC_CB:PC_CB + 88] = col(inp["conv_b"], 88)
    shared = dict(
        table=f(inp["rel_bias_table"]), w_in=f(inp["w_in"][:n_layers]), w_oa=f(inp["w_oa"][:n_layers]), w_ob=f(inp["w_ob"][:n_layers]),
        w_oc=f(inp["w_oc"][:n_layers]), w_out=f(inp["w_out"][:n_layers]), w_up=f(inp["w_up"][:n_layers]), w_down=f(inp["w_down"][:n_layers]),
        pcols=np.ascontiguousarray(pc.transpose(1, 0, 2)),
        lamv=np.ascontiguousarray(np.stack([f(inp["lam_q1"]), f(inp["lam_k1"]), f(inp["lam_q2"]), f(inp["lam_k2"])], 1)),
        subg=f(inp["diff_subln_g"]), sinks=f(inp["sinks"]))
    shared.update(host_consts())
    x = f(inp["x"])
    return [dict(shared, xT=np.ascontiguousarray(x[b].T)) for b in range(4)]


def kernel(**inputs):
    if "nc" not in _CACHE:
        _CACHE["nc"] = build()
    in_maps = host_inputs(inputs)
    res = run_bass_kernel_spmd(_CACHE["nc"], in_maps, core_ids=list(range(4)))
    return np.stack([np.ascontiguousarray(res.results[b]["out"].T) for b in range(4)], 0).astype(np.float32)
```
